# Optimizing a Trainium2 kernel written in Bass

```python
import math
import jax, jax.numpy as jnp
from jax import lax
import numpy as np

D_MODEL = 1024
BATCH = 8
SEQ = 2048
DEPTH = 2
DEC_BATCH = 32
DEC_SEQ = 32
PAST_LEN = 4096

CHUNK = 64
N_A = DEPTH // 2
N_B = DEPTH - N_A
H_A = 8
DK_A = D_MODEL // H_A
DV_A = 2 * DK_A
A_IN = 2 * H_A * DK_A + 2 * H_A * DV_A
H_B = 8
DH_B = D_MODEL // (2 * H_B)
QK_B = 2 * H_B * DH_B
V_B = H_B * 2 * DH_B
D_FF = 11 * D_MODEL // 4
N_MOD = 9
ROPE_THETA = 10000.0
EPS = 1e-6
Q_BLOCK = 128
HALF = 0.5

kernel_name = "yoco_retention_diffattn_streaming_step"


def rms_norm(x, g):
    xf = x.astype(jnp.float32)
    y = xf * lax.rsqrt(jnp.mean(xf * xf, axis=-1, keepdims=True) + EPS)
    return (y * g.astype(jnp.float32)).astype(x.dtype)


def modulate(x, g, shift, scale):
    return rms_norm(x, g) * (1 + scale[:, None, :]) + shift[:, None, :]


def swiglu(h, w_in, w_out):
    a, b = jnp.split(h @ w_in, 2, axis=-1)
    return (jax.nn.silu(a) * b) @ w_out


def rotary(x, pos):
    d = x.shape[-1]
    inv = jnp.power(ROPE_THETA, -jnp.arange(0, d, 2, dtype=jnp.float32) / d)
    ang = pos[:, None] * inv[None, :]
    ang = ang.reshape((ang.shape[0],) + (1,) * (x.ndim - 3) + (ang.shape[1],))
    cos, sin = jnp.cos(ang), jnp.sin(ang)
    x1, x2 = jnp.split(x.astype(jnp.float32), 2, axis=-1)
    return jnp.concatenate([x1 * cos - x2 * sin, x2 * cos + x1 * sin], axis=-1).astype(x.dtype)


def retention_log_gamma():
    return jnp.log1p(-jnp.exp2(-5.0 - jnp.arange(H_A, dtype=jnp.float32)))


def retention_block(state, q, k, v, log_gamma):
    n = q.shape[2]
    idx = jnp.arange(n, dtype=jnp.float32)
    lg = log_gamma[:, None]
    d_intra = jnp.exp(lg[:, :, None] * jnp.abs(idx[:, None] - idx[None, :])).astype(q.dtype)
    q_dec = jnp.exp(lg * idx).astype(q.dtype)
    k_dec = jnp.exp(lg * (n - idx)).astype(q.dtype)
    c_dec = jnp.exp(log_gamma * n).astype(q.dtype)
    scores = jnp.einsum('bhid,bhjd->bhij', q, k) * d_intra
    out = (jnp.einsum('bhij,bhjv->bhiv', scores, v)
           + jnp.einsum('bhid,bhdv->bhiv', q * q_dec[..., None], state))
    new_state = state * c_dec[:, None, None] + jnp.einsum('bhjd,bhjv->bhdv', k * k_dec[..., None], v)
    return out, new_state


def retention_mixer(h, pos, state, w_in, g_gn, w_out, prompt):
    b, s, _ = h.shape
    q, k, v, g = jnp.split(h @ w_in, [H_A * DK_A, 2 * H_A * DK_A, 2 * H_A * DK_A + H_A * DV_A], axis=-1)
    q = rotary(q.reshape(b, s, H_A, DK_A), pos)
    k = rotary(k.reshape(b, s, H_A, DK_A), pos) * (DK_A ** -0.5)
    v = v.reshape(b, s, H_A, DV_A)
    lg = retention_log_gamma()
    if prompt:
        nc = s // CHUNK

        def to_chunks(t):
            return t.reshape(b, nc, CHUNK, H_A, t.shape[-1]).transpose(1, 0, 3, 2, 4)

        def step(st, qkv):
            out, st = retention_block(st, *qkv, lg)
            return st, out

        state0 = jnp.zeros((b, H_A, DK_A, DV_A), h.dtype)
        new_state, o = lax.scan(step, state0, (to_chunks(q), to_chunks(k), to_chunks(v)))
        o = o.transpose(1, 0, 3, 2, 4).reshape(b, s, H_A, DV_A)
    else:
        o, new_state = retention_block(state, q.transpose(0, 2, 1, 3), k.transpose(0, 2, 1, 3),
                                       v.transpose(0, 2, 1, 3), lg)
        o = o.transpose(0, 2, 1, 3)
    o = rms_norm(o, g_gn.reshape(H_A, DV_A))
    y = (jax.nn.silu(g) * o.reshape(b, s, H_A * DV_A)) @ w_out
    return y, new_state


def shared_kv(x, c, pos, w_ada_kv, b_ada_kv, g_kv, w_kv):
    b, s, _ = x.shape
    shift, scale = jnp.split(jax.nn.silu(c) @ w_ada_kv + b_ada_kv, 2, axis=-1)
    k, v = jnp.split(modulate(x, g_kv, shift, scale) @ w_kv, 2, axis=-1)
    k = rotary(k.reshape(b, s, H_B, 2, DH_B), pos)
    v = v.reshape(b, s, H_B, 2 * DH_B)
    return k, v


def diff_attention_prompt(q, k, v, lam):
    b, s = q.shape[:2]
    nb = s // Q_BLOCK
    key_pos = jnp.arange(s)
    qb = q.reshape(b, nb, Q_BLOCK, H_B, 2, DH_B).transpose(1, 0, 2, 3, 4, 5)

    def block(args):
        qblk, i = args
        qpos = i * Q_BLOCK + jnp.arange(Q_BLOCK)
        visible = key_pos[None, :] < (qpos[:, None] // CHUNK + 1) * CHUNK
        logits = jnp.einsum('bqhtd,bkhtd->bhtqk', qblk, k).astype(jnp.float32) * (DH_B ** -0.5)
        p = jax.nn.softmax(jnp.where(visible, logits, -jnp.inf), axis=-1)
        pd = (p[:, :, 0] - lam * p[:, :, 1]).astype(v.dtype)
        return jnp.einsum('bhqk,bkhv->bqhv', pd, v)

    o = lax.map(block, (qb, jnp.arange(nb)))
    return o.transpose(1, 0, 2, 3, 4).reshape(b, s, H_B, 2 * DH_B)


def diff_attention_sample(q, k_new, v_new, k_past, v_past, lam):
    sc = DH_B ** -0.5
    lp = jnp.einsum('bqhtd,bkhtd->bhtqk', q, k_past).astype(jnp.float32) * sc
    ln = jnp.einsum('bqhtd,bkhtd->bhtqk', q, k_new).astype(jnp.float32) * sc
    p = jax.nn.softmax(jnp.concatenate([lp, ln], axis=-1), axis=-1)
    pd = (p[:, :, 0] - lam * p[:, :, 1]).astype(v_new.dtype)
    past = k_past.shape[1]
    return (jnp.einsum('bhqk,bkhv->bqhv', pd[..., :past], v_past)
            + jnp.einsum('bhqk,bkhv->bqhv', pd[..., past:], v_new))


def diff_mixer(h, pos, k, v, k_past, v_past, w_q, lam_p, g_sub, w_out, lam_init, prompt):
    b, s, _ = h.shape
    q = rotary((h @ w_q).reshape(b, s, H_B, 2, DH_B), pos)
    lp = lam_p.astype(jnp.float32)
    lam = jnp.exp(jnp.sum(lp[0] * lp[1])) - jnp.exp(jnp.sum(lp[2] * lp[3])) + lam_init
    if prompt:
        o = diff_attention_prompt(q, k, v, lam)
    else:
        o = diff_attention_sample(q, k, v, k_past, v_past, lam)
    o = rms_norm(o, g_sub) * (1 - lam_init)
    return o.reshape(b, s, V_B) @ w_out


def run_trunk(x, c, pos, ret_state, k_past, v_past, prompt,
              w_ada, b_ada, g_norm, w_ffn_in, w_ffn_out, w_in_a, g_gn_a, w_out_a,
              w_ada_kv, b_ada_kv, g_kv, w_kv, w_q_b, lam_b, g_subln_b, w_out_b):
    b = x.shape[0]
    new_ret = []
    k = v = None
    for l in range(DEPTH):
        mod = (jax.nn.silu(c) @ w_ada[l] + b_ada[l]).reshape(b, N_MOD, D_MODEL)
        h = modulate(x, g_norm[l, 0], mod[:, 0], mod[:, 1])
        x = x + HALF * mod[:, 2, None] * rms_norm(swiglu(h, w_ffn_in[l, 0], w_ffn_out[l, 0]), g_norm[l, 1])
        h = modulate(x, g_norm[l, 2], mod[:, 3], mod[:, 4])
        if l < N_A:
            y, st = retention_mixer(h, pos, None if prompt else ret_state[l],
                                    w_in_a[l], g_gn_a[l], w_out_a[l], prompt)
            new_ret.append(st)
        else:
            j = l - N_A
            lam_init = 0.8 - 0.6 * math.exp(-0.3 * l)
            y = diff_mixer(h, pos, k, v, k_past, v_past, w_q_b[j], lam_b[j], g_subln_b[j], w_out_b[j],
                           lam_init, prompt)
        x = x + mod[:, 5, None] * rms_norm(y, g_norm[l, 3])
        h = modulate(x, g_norm[l, 4], mod[:, 6], mod[:, 7])
        x = x + HALF * mod[:, 8, None] * rms_norm(swiglu(h, w_ffn_in[l, 1], w_ffn_out[l, 1]), g_norm[l, 5])
        if l == N_A - 1:
            k, v = shared_kv(x, c, pos, w_ada_kv, b_ada_kv, g_kv, w_kv)
    return x, jnp.stack(new_ret), k, v


def _nrm(key, shape, s):
    return jax.random.normal(key, shape, jnp.float32) * s


def setup_inputs(seed: int = 0) -> dict:
    key = jax.random.key(seed)
    ks = jax.random.split(key, 23)
    D = D_MODEL
    return {
        "x_prompt": _nrm(ks[0], (BATCH, SEQ, D), 1.0),
        "x_sample": _nrm(ks[1], (DEC_BATCH, DEC_SEQ, D), 1.0),
        "state_ret": _nrm(ks[2], (N_A, DEC_BATCH, H_A, DK_A, DV_A), 0.5),
        "cache_k": _nrm(ks[3], (DEC_BATCH, PAST_LEN, 2 * H_B, DH_B), 1.0),
        "cache_v": _nrm(ks[4], (DEC_BATCH, PAST_LEN, H_B, 2 * DH_B), 1.0),
        "c_prompt": _nrm(ks[5], (BATCH, D), 1.0),
        "c_sample": _nrm(ks[6], (DEC_BATCH, D), 1.0),
        "w_ada": _nrm(ks[7], (DEPTH, D, N_MOD * D), 0.5 * D ** -0.5),
        "b_ada": _nrm(ks[8], (DEPTH, N_MOD * D), 0.01),
        "g_norm": 1.0 + _nrm(ks[9], (DEPTH, 6, D), 0.02),
        "w_ffn_in": _nrm(ks[10], (DEPTH, 2, D, 2 * D_FF), D ** -0.5),
        "w_ffn_out": _nrm(ks[11], (DEPTH, 2, D_FF, D), D_FF ** -0.5),
        "w_in_a": _nrm(ks[12], (N_A, D, A_IN), D ** -0.5),
        "g_gn_a": 1.0 + _nrm(ks[13], (N_A, H_A * DV_A), 0.02),
        "w_out_a": _nrm(ks[14], (N_A, H_A * DV_A, D), (H_A * DV_A) ** -0.5),
        "w_ada_kv": _nrm(ks[15], (D, 2 * D), 0.5 * D ** -0.5),
        "b_ada_kv": _nrm(ks[16], (2 * D,), 0.01),
        "g_kv": 1.0 + _nrm(ks[17], (D,), 0.02),
        "w_kv": _nrm(ks[18], (D, QK_B + V_B), D ** -0.5),
        "w_q_b": _nrm(ks[19], (N_B, D, QK_B), D ** -0.5),
        "lam_b": _nrm(ks[20], (N_B, 4, DH_B), 0.1),
        "g_subln_b": 1.0 + _nrm(ks[21], (N_B, 2 * DH_B), 0.02),
        "w_out_b": _nrm(ks[22], (N_B, V_B, D), V_B ** -0.5),
    }


def reference(x_prompt, x_sample, state_ret, cache_k, cache_v, c_prompt, c_sample,
              w_ada, b_ada, g_norm, w_ffn_in, w_ffn_out, w_in_a, g_gn_a, w_out_a,
              w_ada_kv, b_ada_kv, g_kv, w_kv, w_q_b, lam_b, g_subln_b, w_out_b):
    weights = (w_ada, b_ada, g_norm, w_ffn_in, w_ffn_out, w_in_a, g_gn_a, w_out_a,
               w_ada_kv, b_ada_kv, g_kv, w_kv, w_q_b, lam_b, g_subln_b, w_out_b)
    bp, sp = x_prompt.shape[:2]
    bs, ss = x_sample.shape[:2]
    pos_p = jnp.arange(sp, dtype=jnp.float32)
    y_prompt, ret_p, k_p, v_p = run_trunk(x_prompt, c_prompt, pos_p, None, None, None, True, *weights)
    pos_s = PAST_LEN + jnp.arange(ss, dtype=jnp.float32)
    k_past = cache_k.reshape(cache_k.shape[0], cache_k.shape[1], H_B, 2, DH_B)
    y_sample, ret_s, k_s, v_s = run_trunk(x_sample, c_sample, pos_s, state_ret, k_past, cache_v, False, *weights)
    return (y_prompt, y_sample,
            ret_p, k_p.reshape(bp, sp, 2 * H_B, DH_B), v_p,
            ret_s, k_s.reshape(bs, ss, 2 * H_B, DH_B), v_s)
```

```python
import math
import os
import contextlib
import numpy as np
import concourse.bass as bass
import concourse.mybir as mybir
from concourse.bass_utils import run_bass_kernel_spmd

F32 = mybir.dt.float32
BF16 = mybir.dt.bfloat16
ALU = mybir.AluOpType
AF = mybir.ActivationFunctionType
AX = mybir.AxisListType

D = 1024
DC = 8
DFF = 2816
FC = 22
NSEQ = 5
NTOK = 2176
TMAX = 768
EPS = 1e-6
H = 8
LAM_INIT = 0.8 - 0.6 * math.exp(-0.3 * 1)
PASTN = 4096

GROUPS = [
    (0, 768, [(0, 512, [(0, 512, 0)], False), (512, 768, [(512, 768, 0)], False)]),
    (768, 768, [(0, 512, [(0, 512, 0)], False), (512, 768, [(512, 768, 0)], False)]),
    (1536, 640, [(0, 512, [(0, 512, 0)], False),
                 (512, 640, [(512 + 32 * s, 544 + 32 * s, 1 + s) for s in range(4)], True)]),
]


def _merge(d, src):
    for k, v in src.items():
        o = d.get(k)
        if o is None or o[1] < v[1]:
            d[k] = v


class Res:
    __slots__ = ("W", "R")

    def __init__(self, init=None):
        self.W = dict(init) if init else {}
        self.R = {}


class Eng:
    def __init__(self, name, sem, inorder=False, dsems=()):
        self.name = name
        self.sem = sem
        self.n = 0
        self.seen = {}
        self.prog = []
        self.inorder = inorder
        self.dsems = list(dsems)
        self.dcnt = [0] * len(self.dsems)
        self.di = 0

    def _wait(self, deps):
        for key, (sem, val) in deps.items():
            if self.inorder and sem is self.sem:
                continue
            if self.seen.get(key, 0) >= val:
                continue
            self.seen[key] = val
            self.prog.append(("wait", sem, val))

    def op(self, fn, reads=(), writes=(), mark=True):
        deps = {}
        for r in reads:
            _merge(deps, r.W)
        for w in writes:
            _merge(deps, w.W)
            _merge(deps, w.R)
        self._wait(deps)
        tk = (self.sem, self.n + 1)
        if mark:
            self.n += 1
        self.prog.append(("op", fn, mark))
        k = id(self.sem)
        for r in reads:
            o = r.R.get(k)
            if o is None or o[1] < tk[1]:
                r.R[k] = tk
        for w in writes:
            w.W = {k: tk}
            w.R = {}

    def dma(self, out_ap, in_ap, reads=(), writes=()):
        i = self.di % len(self.dsems)
        self.di += 1
        sem = self.dsems[i]
        prev = self.dcnt[i]
        deps = {}
        for r in reads:
            _merge(deps, r.W)
        for w in writes:
            _merge(deps, w.W)
            _merge(deps, w.R)
        if prev > 0:
            deps[id(sem)] = (sem, prev)
        self._wait(deps)
        self.dcnt[i] = prev + 16
        self.prog.append(("dma", out_ap, in_ap, sem))
        tk = (sem, prev + 16)
        k = id(sem)
        for r in reads:
            o = r.R.get(k)
            if o is None or o[1] < tk[1]:
                r.R[k] = tk
        for w in writes:
            w.W[k] = tk
            w.R = {}

    def run(self, e):
        for it in self.prog:
            if it[0] == "wait":
                e.wait_ge(it[1], it[2])
            elif it[0] == "op":
                ins = it[1](e)
                if it[2]:
                    ins.then_inc(self.sem, 1)
            else:
                e.dma_start(out=it[1], in_=it[2]).then_inc(it[3], 16)


class K:
    pass


def build_program():
    nc = bass.Bass("TRN2", target_bir_lowering=False)
    es = contextlib.ExitStack()

    def din(name, shape, dt=F32):
        return nc.dram_tensor(name, list(shape), dt, kind="ExternalInput").ap()

    def dout(name, shape):
        return nc.dram_tensor(name, list(shape), F32, kind="ExternalOutput").ap()

    xT_d = din("xT", [D, NTOK])
    cT_d = din("cT", [128, DC * NSEQ])
    st_in_d = din("st_in", [4, H, 128, 256])
    kcT_d = din("kcT", [4, H, 128, PASTN])
    vc_d = din("vc", [4, PASTN, D])
    w_ada_d = din("w_ada", [2, D, 9 * D])
    b_ada_d = din("b_adaT", [128, 2 * 72])
    g_norm_d = din("g_normT", [128, 2 * 6 * 8])
    w_fi_d = din("w_ffn_in", [2, 2, D, 2 * DFF])
    w_fo_d = din("w_ffn_out", [2, 2, DFF, D])
    w_ina_d = din("w_in_a", [D, 6144])
    g_gn_d = din("g_gnT", [128, 16])
    w_outa_d = din("w_out_a", [2048, D])
    w_adakv_d = din("w_ada_kv", [D, 2048])
    b_kv_d = din("b_kvT", [128, 16])
    g_kv_d = din("g_kvT", [128, 8])
    w_kv_d = din("w_kv", [D, 2048])
    w_q_d = din("w_q_b", [D, D])
    lam_d = din("lam_bb", [128, 256])
    g_sub_d = din("g_subT", [128, 1])
    w_outb_d = din("w_out_b", [D, D])
    rotR_C_d = din("rotR_C", [128, NTOK])
    rotR_S_d = din("rotR_S", [128, NTOK])
    rotD_C_d = din("rotD_C", [128, NTOK])
    rotD_S_d = din("rotD_S", [128, NTOK])
    strip_d = din("ret_strip", [H, 128, 768])
    qd_d = din("ret_qd", [H, 128, 768])
    kd_d = din("ret_kd", [128, H * 2 * 6])
    DS_d = din("ret_DS", [H, 128, 128])
    qdS_d = din("ret_qdS", [H, 128, 128])
    kdS_d = din("ret_kdS", [128, H * 4])
    rc_d = din("ret_c", [128, H * 3])

    yT_d = dout("yT", [D, NTOK])
    kT_o = dout("kT_out", [D, NTOK])
    v_o = dout("v_out", [NTOK, D])
    stp_o = dout("st_p", [H, 128, 256])
    sts_o = dout("st_s", [4, H, 128, 256])

    def sem(name):
        return es.enter_context(nc.semaphore(name))

    pe = Eng("pe", sem("s_pe"), inorder=True)
    act = Eng("act", sem("s_act"))
    dve = Eng("dve", sem("s_dve"))
    pool = Eng("pool", sem("s_pool"), dsems=[sem(f"dq_p{i}") for i in range(8)])
    sp = Eng("sp", sem("s_sp"), dsems=[sem(f"dq_s{i}") for i in range(8)])
    engines = [pe, act, dve, pool, sp]
    pc = pool if os.environ.get('KPOOL') else dve

    def fence():
        d = {}
        for e in engines:
            if e.n > 0:
                d[id(e.sem)] = (e.sem, e.n)
            for s_, c_ in zip(e.dsems, e.dcnt):
                if c_ > 0:
                    d[id(s_)] = (s_, c_)
        return d

    def sb(name, cols, dt):
        return es.enter_context(nc.sbuf_tensor("sb_" + name, [128, cols], dt))

    xT = sb("xT", DC * TMAX, F32)
    hb = sb("hb", DC * TMAX, BF16)
    ub = sb("ub", FC * TMAX, BF16)
    osb = sb("osb", DC * TMAX, F32)
    slabs = [sb(f"slab{i}", 6144, BF16) for i in range(3)]
    tmpS = [sb(f"tmpS{i}", 512, F32) for i in range(2)]
    tmpT = [sb(f"tmpT{i}", 512, F32) for i in range(2)]
    rs_t = sb("rs_t", 512, F32)
    rstd_t = [sb(f"rstd{i}", 512, F32) for i in range(1)]
    cm05 = sb("cm05", 512, F32)
    ones = sb("ones", 128, BF16)
    modT = sb("modT", 2 * 360, F32)
    kvmod = sb("kvmod", 80, F32)
    abc = sb("abc", 2 * 3 * 3 * 40, F32)
    kvab = sb("kvab", 80, F32)
    b_ada = sb("b_ada", 144, F32)
    g_norm = sb("g_norm", 96, F32)
    g_gn = sb("g_gn", 16, F32)
    b_kv = sb("b_kv", 16, F32)
    g_kv = sb("g_kv", 8, F32)
    g_sub = sb("g_sub", 1, F32)
    g_sub2 = sb("g_sub2", 1, F32)
    lam_sb = sb("lam_sb", 256, F32)
    lam_t = sb("lam_t", 128, F32)
    lam_r = sb("lam_r", 4, F32)
    neg_lam = sb("neg_lam", 1, F32)
    c_sb = sb("c_sb", 40, F32)
    scT = sb("scT", 40, BF16)
    kd_sb = sb("kd_sb", H * 12, F32)
    kdS_sb = sb("kdS_sb", H * 4, F32)
    rc_sb = sb("rc_sb", H * 3, F32)
    st32 = sb("st32", H * 256, F32)
    stbf = sb("stbf", H * 256, BF16)
    ARENA_COLS = 10650
    arena = sb("arena", ARENA_COLS, F32)

    PS = [es.enter_context(nc.psum_tensor(f"ps{i}", [128, 512], F32)) for i in range(8)]
    psR = [Res() for _ in range(8)]
    rot_state = [0]

    def ps_rot():
        i = rot_state[0] % 4
        rot_state[0] += 1
        return PS[i], psR[i]

    xR = [[Res() for _ in range(2)] for _ in range(DC)]
    hR = [[Res() for _ in range(2)] for _ in range(DC)]
    uR = [[Res() for _ in range(2)] for _ in range(FC)]
    oR = [[Res() for _ in range(2)] for _ in range(DC)]
    slR = [Res() for _ in range(3)]
    tmpSR = [Res() for _ in range(2)]
    tmpTR = [Res() for _ in range(2)]
    rsR = Res()
    rstdR = [Res() for _ in range(1)]
    constR = Res()
    modR = Res()
    stR = [Res() for _ in range(H)]
    kT_oR = [[Res() for _ in range(3)] for _ in range(H)]
    v_oR = [Res() for _ in range(3)]
    cnt = {"slab": 0, "tS": 0, "tT": 0, "rstd": 0}

    def next_slab():
        i = cnt["slab"] % 3
        cnt["slab"] += 1
        return slabs[i], slR[i]

    def col(buf, c, a, b):
        return buf[:, c * TMAX + a: c * TMAX + b]

    def mm(ps_ap, lhsT, rhs, start, stop, reads, writes, force=False):
        pe.op(lambda e: e.matmul(ps_ap, lhsT, rhs, start=start, stop=stop), reads, writes, mark=(stop or force))

    def actf(out, in_, func, reads, writes, scale=None, bias=None):
        kw = {}
        if scale is not None:
            kw["scale"] = scale
        if bias is not None:
            kw["bias"] = bias
        act.op(lambda e: e.activation(out=out, in_=in_, func=func, **kw), reads, writes)

    def tt(eng, out, in0, in1, op, reads, writes):
        eng.op(lambda e: e.tensor_tensor(out=out, in0=in0, in1=in1, op=op), reads, writes)

    def ts(eng, out, in0, s1, s2, op0, op1, reads, writes):
        eng.op(lambda e: e.tensor_scalar(out=out, in0=in0, scalar1=s1, scalar2=s2, op0=op0, op1=op1), reads, writes)

    def stt(out, in0, scalar, in1, op0, op1, reads, writes):
        dve.op(lambda e: e.scalar_tensor_tensor(out=out, in0=in0, scalar=scalar, in1=in1, op0=op0, op1=op1),
               reads, writes)

    def cp(eng, out, in_, reads, writes):
        eng.op(lambda e: e.tensor_copy(out=out, in_=in_), reads, writes)

    def wdma(out_ap, in_ap, reads, writes):
        pool.dma(out_ap, in_ap, reads, writes)

    def ldma(out_ap, in_ap, reads, writes):
        sp.dma(out_ap, in_ap, reads, writes)

    def load_wslab(src2d, kch, ncols, off=0, slab=None, slr=None):
        for k0 in range(0, kch, 8):
            k1 = min(kch, k0 + 8)
            dst = slab[:, off + k0 * ncols: off + k1 * ncols].rearrange("p (k n) -> p k n", k=k1 - k0)
            wdma(dst, src2d[k0 * 128:k1 * 128, :].rearrange("(k p) n -> p k n", p=128), [], [slr])

    dve.op(lambda e: e.memset(ones[:], 1.0), [], [constR])
    dve.op(lambda e: e.memset(cm05[:], -0.5), [], [constR])
    smallR = Res()
    for dst, src in ((b_ada, b_ada_d), (g_norm, g_norm_d), (g_gn, g_gn_d), (b_kv, b_kv_d), (g_kv, g_kv_d),
                     (g_sub, g_sub_d), (lam_sb, lam_d), (c_sb, cT_d), (kd_sb, kd_d), (kdS_sb, kdS_d),
                     (rc_sb, rc_d)):
        ldma(dst[:], src, [], [smallR])
    lamR = Res()
    tt(dve, lam_t[:, 0:64], lam_sb[:, 0:64], lam_sb[:, 64:128], ALU.mult, [smallR], [lamR])
    tt(dve, lam_t[:, 64:128], lam_sb[:, 128:192], lam_sb[:, 192:256], ALU.mult, [smallR], [lamR])
    dve.op(lambda e: e.reduce_sum(out=lam_r[:, 0:1], in_=lam_t[:, 0:64], axis=AX.X), [lamR], [lamR])
    dve.op(lambda e: e.reduce_sum(out=lam_r[:, 1:2], in_=lam_t[:, 64:128], axis=AX.X), [lamR], [lamR])
    actf(lam_r[:, 2:4], lam_r[:, 0:2], AF.Exp, [lamR], [lamR])
    tt(dve, neg_lam[:], lam_r[:, 3:4], lam_r[:, 2:3], ALU.subtract, [lamR], [lamR])
    ts(dve, neg_lam[:], neg_lam[:], -LAM_INIT, None, ALU.add, ALU.bypass, [lamR], [lamR])
    ts(dve, g_sub2[:], g_sub[:], 1.0 - LAM_INIT, None, ALU.mult, ALU.bypass, [smallR], [lamR])
    actf(scT[:], c_sb[:], AF.Silu, [smallR], [modR])

    def ada_mm(wsrc, ncols_total, psb, psr):
        nsl = ncols_total // 512
        for j in range(nsl):
            slab, slr = next_slab()
            load_wslab(wsrc[:, j * 512:(j + 1) * 512], 8, 512, 0, slab, slr)
            for nn in range(4):
                n = j * 4 + nn
                for k in range(8):
                    mm(psb[:, n * 5:(n + 1) * 5], slab[:, k * 512 + nn * 128: k * 512 + nn * 128 + 128],
                       scT[:, k * 5:(k + 1) * 5], k == 0, k == 7, [slr, modR], [psr])

    for l in range(2):
        ada_mm(w_ada_d[l], 9 * D, PS[4 + l], psR[4 + l])
        for s in range(NSEQ):
            tt(dve, modT[:, l * 360 + s: (l + 1) * 360: 5], PS[4 + l][:, s:360:5], b_ada[:, l * 72:(l + 1) * 72],
               ALU.add, [psR[4 + l], smallR], [modR])
    ada_mm(w_adakv_d, 2048, PS[6], psR[6])
    for s in range(NSEQ):
        tt(dve, kvmod[:, s:80:5], PS[6][:, s:80:5], b_kv[:, 0:16], ALU.add, [psR[6], smallR], [modR])

    def modv(l, m, c):
        o = l * 360 + (m * 8 + c) * 5
        return modT[:, o:o + 5]

    def abcv(l, s, kind, c, seq=None):
        o = (((l * 3 + s) * 3 + kind) * 8 + c) * 5
        if seq is None:
            return abc[:, o:o + 5]
        return abc[:, o + seq:o + seq + 1]

    abcR = Res()
    for l in range(2):
        for s in range(3):
            half = 1.0 if s == 1 else 0.5
            for c in range(DC):
                gpre = g_norm[:, (l * 6 + 2 * s) * 8 + c:(l * 6 + 2 * s) * 8 + c + 1]
                gpost = g_norm[:, (l * 6 + 2 * s + 1) * 8 + c:(l * 6 + 2 * s + 1) * 8 + c + 1]
                ts(dve, abcv(l, s, 0, c), modv(l, 3 * s + 1, c), 1.0, gpre, ALU.add, ALU.mult, [modR, smallR], [abcR])
                cp(dve, abcv(l, s, 1, c), modv(l, 3 * s, c), [modR], [abcR])
                ts(dve, abcv(l, s, 2, c), modv(l, 3 * s + 2, c), half, gpost, ALU.mult, ALU.mult, [modR, smallR], [abcR])
    for c in range(DC):
        ts(dve, kvab[:, c * 5:(c + 1) * 5], kvmod[:, (8 + c) * 5:(9 + c) * 5], 1.0, g_kv[:, c:c + 1], ALU.add, ALU.mult,
           [modR, smallR], [abcR])
        cp(dve, kvab[:, 40 + c * 5:45 + c * 5], kvmod[:, c * 5:(c + 1) * 5], [modR], [abcR])

    def rstd_from(ps_ap, psr, n, dim):
        i = 0
        ts(dve, rs_t[:, 0:n], ps_ap, 1.0 / dim, EPS, ALU.mult, ALU.add, [psr], [rsR])
        actf(rs_t[:, 0:n], rs_t[:, 0:n], AF.Sqrt, [rsR], [rsR])
        dve.op(lambda e: e.reciprocal(out=rstd_t[i][:, 0:n], in_=rs_t[:, 0:n]), [rsR], [rstdR[i]])
        return rstd_t[i], rstdR[i]

    def prenorm(tiles, Afn, Bfn):
        for ti, (a, b, segs, _) in enumerate(tiles):
            n = b - a
            for c in range(DC):
                actf(col(ub, c, a, b), col(xT, c, a, b), AF.Square, [xR[c][ti]], [uR[c][ti]])
            for c in range(DC):
                mm(PS[7][:, 0:n], ones[:], col(ub, c, a, b), c == 0, c == DC - 1, [constR, uR[c][ti]], [psR[7]])
            rt, rr = rstd_from(PS[7][:, 0:n], psR[7], n, D)
            for c in range(DC):
                j = cnt["tT"] % 2
                cnt["tT"] += 1
                tt(dve, tmpT[j][:, 0:n], col(xT, c, a, b), rt[:, 0:n], ALU.mult, [xR[c][ti], rr], [tmpTR[j]])
                for (sa, sb_, seq) in segs:
                    ts(pc, col(hb, c, sa, sb_), tmpT[j][:, sa - a:sb_ - a], Afn(c, seq), Bfn(c, seq),
                       ALU.mult, ALU.add, [tmpTR[j], abcR], [hR[c][ti]])

    def post_chunk(po, por, c, ti, tile, Cfn):
        a, b, segs, _ = tile
        n = b - a
        KSUB = int(os.environ.get("KSUB", "9"))
        if KSUB < 4:
            return
        actf(col(hb, c, a, b), po[:, 0:n], AF.Square, [por], [hR[c][ti]])
        if KSUB < 5:
            return
        for (sa, sb_, seq) in segs:
            actf(col(osb, c, sa, sb_), po[:, sa - a:sb_ - a], AF.Identity, [por, abcR], [oR[c][ti]], scale=Cfn(c, seq))

    def post_final(tiles):
        for ti, (a, b, segs, _) in enumerate(tiles):
            n = b - a
            for c in range(DC):
                mm(PS[7][:, 0:n], ones[:], col(hb, c, a, b), c == 0, c == DC - 1, [constR, hR[c][ti]], [psR[7]])
            rt, rr = rstd_from(PS[7][:, 0:n], psR[7], n, D)
            for c in range(DC):
                j = cnt["tT"] % 2
                cnt["tT"] += 1
                tt(dve, tmpT[j][:, 0:n], col(osb, c, a, b), rt[:, 0:n], ALU.mult, [oR[c][ti], rr], [tmpTR[j]])
                tt(pc, col(xT, c, a, b), col(xT, c, a, b), tmpT[j][:, 0:n], ALU.add, [tmpTR[j]], [xR[c][ti]])

    def out_proj(wsrc, kch, src_buf, srcR, tiles, Cfn):
        for ns in range(4):
            slab, slr = next_slab()
            load_wslab(wsrc[:, ns * 256:(ns + 1) * 256], kch, 256, 0, slab, slr)
            for ti, tile in enumerate(tiles):
                a, b = tile[0], tile[1]
                n = b - a
                for nn in range(2):
                    c = ns * 2 + nn
                    po, por = ps_rot()
                    for k in range(kch):
                        mm(po[:, 0:n], slab[:, k * 256 + nn * 128:k * 256 + nn * 128 + 128], col(src_buf, k, a, b),
                           k == 0, k == kch - 1, [slr, srcR[k][ti]], [por])
                    post_chunk(po, por, c, ti, tile, Cfn)
        if int(os.environ.get("KSUB", "9")) < 6:
            return
        post_final(tiles)

    def ffn(l, f, tiles):
        s = 0 if f == 0 else 2
        KSUB = int(os.environ.get("KSUB", "9"))
        prenorm(tiles, lambda c, q: abcv(l, s, 0, c, q), lambda c, q: abcv(l, s, 1, c, q))
        if KSUB < 2:
            return
        wi = w_fi_d[l, f]
        widths = [256] * 11
        c0 = 0
        for w in widths:
            slab, slr = next_slab()
            load_wslab(wi[:, c0:c0 + w], 8, w, 0, slab, slr)
            load_wslab(wi[:, DFF + c0:DFF + c0 + w], 8, w, 8 * w, slab, slr)
            for ti, (a, b, segs, _) in enumerate(tiles):
                n = b - a
                for cc in range(w // 128):
                    ch = c0 // 128 + cc
                    pa, par = ps_rot()
                    pb, pbr = ps_rot()
                    for k in range(8):
                        mm(pa[:, 0:n], slab[:, k * w + cc * 128:k * w + cc * 128 + 128], col(hb, k, a, b),
                           k == 0, k == 7, [slr, hR[k][ti]], [par])
                    for k in range(8):
                        mm(pb[:, 0:n], slab[:, 8 * w + k * w + cc * 128:8 * w + k * w + cc * 128 + 128],
                           col(hb, k, a, b), k == 0, k == 7, [slr, hR[k][ti]], [pbr])
                    j = cnt["tS"] % 2
                    cnt["tS"] += 1
                    actf(tmpS[j][:, 0:n], pa[:, 0:n], AF.Silu, [par], [tmpSR[j]])
                    tt(dve, col(ub, ch, a, b), tmpS[j][:, 0:n], pb[:, 0:n], ALU.mult, [tmpSR[j], pbr], [uR[ch][ti]])
            c0 += w
        if KSUB < 3:
            return
        out_proj(w_fo_d[l, f], FC, ub, uR, tiles, lambda c, q: abcv(l, s, 2, c, q))

    ar = {"off": 0, "fence": {}}

    def arena_reset():
        ar["off"] = 0
        ar["fence"] = fence()

    def aalloc(cols, dt):
        words = cols if dt == F32 else (cols + 1) // 2
        o = ar["off"]
        ar["off"] += words
        assert ar["off"] <= ARENA_COLS, ar["off"]
        v = arena[:, o:o + words]
        if dt != F32:
            v = v.bitcast(dt)
        return v, Res(ar["fence"])

    def rotary(ps_ap, psr, n, Ct, St, tabR, cols, half, scale, out_ap, outR, tmps, split=None):
        (xs, xsR), (sw, swR), (t1, t1R) = tmps
        actf(xs[:, 0:n], ps_ap, AF.Copy, [psr], [xsR], scale=scale)
        nb = 128 // (2 * half)
        for bb in range(nb):
            p0 = bb * 2 * half
            cp(pc, sw[p0:p0 + half, 0:n], xs[p0 + half:p0 + 2 * half, 0:n], [xsR], [swR])
            cp(pc, sw[p0 + half:p0 + 2 * half, 0:n], xs[p0:p0 + half, 0:n], [xsR], [swR])
        tt(dve, t1[:, 0:n], xs[:, 0:n], Ct[:, cols[0]:cols[1]], ALU.mult, [xsR, tabR], [t1R])
        tt(pc, sw[:, 0:n], sw[:, 0:n], St[:, cols[0]:cols[1]], ALU.mult, [swR, tabR], [swR])
        if split is None:
            tt(dve, out_ap, t1[:, 0:n], sw[:, 0:n], ALU.add, [t1R, swR], [outR])
        else:
            tt(dve, split[0], t1[0:64, 0:n], sw[0:64, 0:n], ALU.add, [t1R, swR], [outR])
            tt(dve, split[1], t1[64:128, 0:n], sw[64:128, 0:n], ALU.add, [t1R, swR], [outR])

    def retention(gi, g0, T, tiles):
        l, s = 0, 1
        prenorm(tiles, lambda c, q: abcv(l, s, 0, c, q), lambda c, q: abcv(l, s, 1, c, q))
        arena_reset()
        Tp = sum(t[1] - t[0] for t in tiles if not t[3])
        npb = Tp // 128
        var = 0 if Tp == 768 else 1
        has_sample = any(t[3] for t in tiles)
        Ct, tabR = aalloc(TMAX, F32)
        St, _ = aalloc(TMAX, F32)
        ldma(Ct[:, 0:T], rotR_C_d[:, g0:g0 + T], [], [tabR])
        ldma(St[:, 0:T], rotR_S_d[:, g0:g0 + T], [], [tabR])
        rtmps = [(tmpS[0], tmpSR[0]), aalloc(512, F32), (tmpT[0], tmpTR[0])]
        qT, qTR = aalloc(TMAX, BF16)
        kT, kTR = aalloc(TMAX, BF16)
        qdT, qdTR = aalloc(TMAX, BF16)
        sg, sgR = aalloc(2 * TMAX, BF16)
        vtok, vtokR = aalloc(6 * 256, BF16)
        kdtok, kdtokR = aalloc(6 * 128, BF16)
        kdS = [aalloc(128, BF16) for _ in range(4)]
        strip, stripR = aalloc(768, F32)
        qd, qdR = aalloc(768, F32)
        DS, DSR = aalloc(128, F32)
        qdS, _ = aalloc(128, F32)
        og, ogR = aalloc(2 * 512, F32)
        PT = [aalloc(512, BF16) for _ in range(2)]
        sqg, sqgR = aalloc(2 * 512, BF16)
        stS32 = [aalloc(256, F32) for _ in range(1)]
        stSbf = [aalloc(256, BF16) for _ in range(4)]
        stO = [aalloc(256, F32) for _ in range(1)]
        ptc = 0
        for h in range(H):
            slab, slr = next_slab()
            W = 768
            load_wslab(w_ina_d[:, h * 128:(h + 1) * 128], 8, 128, 0, slab, slr)
            load_wslab(w_ina_d[:, 1024 + h * 128:1024 + (h + 1) * 128], 8, 128, 8 * 128, slab, slr)
            load_wslab(w_ina_d[:, 2048 + h * 256:2048 + (h + 1) * 256], 8, 256, 16 * 128, slab, slr)
            load_wslab(w_ina_d[:, 4096 + h * 256:4096 + (h + 1) * 256], 8, 256, 16 * 128 + 8 * 256, slab, slr)
            OQ, OK_, OV, OG = 0, 1024, 2048, 2048 + 2048
            ldma(strip[:], strip_d[h], [], [stripR])
            ldma(qd[:], qd_d[h], [], [qdR])
            if has_sample:
                ldma(DS[:], DS_d[h], [], [DSR])
                ldma(qdS[:], qdS_d[h], [], [DSR])
                for s4 in range(4):
                    wdma(stSbf[s4][0][:], st_in_d[s4, h], [], [stSbf[s4][1]])
            for ti, (a, b, segs, smp) in enumerate(tiles):
                n = b - a
                pq, pqr = ps_rot()
                for k in range(8):
                    mm(pq[:, 0:n], slab[:, OQ + k * 128:OQ + k * 128 + 128], col(hb, k, a, b), k == 0, k == 7,
                       [slr, hR[k][ti]], [pqr])
                rotary(pq[:, 0:n], pqr, n, Ct, St, tabR, (a, b), 64, 1.0, qT[:, a:b], qTR, rtmps)
                pk, pkr = ps_rot()
                for k in range(8):
                    mm(pk[:, 0:n], slab[:, OK_ + k * 128:OK_ + k * 128 + 128], col(hb, k, a, b), k == 0, k == 7,
                       [slr, hR[k][ti]], [pkr])
                rotary(pk[:, 0:n], pkr, n, Ct, St, tabR, (a, b), 64, 128 ** -0.5, kT[:, a:b], kTR, rtmps)
                for vc in range(2):
                    pg, pgr = ps_rot()
                    for k in range(8):
                        mm(pg[:, 0:n], slab[:, OG + k * 256 + vc * 128:OG + k * 256 + vc * 128 + 128],
                           col(hb, k, a, b), k == 0, k == 7, [slr, hR[k][ti]], [pgr])
                    actf(sg[:, vc * TMAX + a:vc * TMAX + b], pg[:, 0:n], AF.Silu, [pgr], [sgR])
                for tb in range(n // 128):
                    blk = (a // 128) + tb
                    ca = a + tb * 128
                    pv, pvr = ps_rot()
                    for k in range(8):
                        mm(pv[:, 0:256], col(hb, k, ca, ca + 128), slab[:, OV + k * 256:OV + k * 256 + 256],
                           k == 0, k == 7, [slr, hR[k][ti]], [pvr])
                    actf(vtok[:, blk * 256:(blk + 1) * 256], pv[:, 0:256], AF.Copy, [pvr], [vtokR])
                    pt, ptr = ps_rot()
                    mm(pt[:, 0:128], kT[:, ca:ca + 128], ident[:], True, True, [kTR, constR], [ptr])
                    if not smp:
                        actf(kdtok[:, blk * 128:(blk + 1) * 128], pt[:, 0:128], AF.Identity, [ptr, smallR], [kdtokR],
                             scale=kd_sb[:, (h * 2 + var) * 6 + blk:(h * 2 + var) * 6 + blk + 1])
                    else:
                        for s4 in range(4):
                            actf(kdS[s4][0][:], pt[:, 0:128], AF.Identity, [ptr, smallR], [kdS[s4][1]],
                                 scale=kdS_sb[:, h * 4 + s4:h * 4 + s4 + 1])
            for ti, (a, b, segs, smp) in enumerate(tiles):
                if smp:
                    tt(dve, qdT[:, a:b], qT[:, a:b], qdS[:], ALU.mult, [qTR, DSR], [qdTR])
                elif gi > 0:
                    tt(dve, qdT[:, a:b], qT[:, a:b], qd[:, a:b], ALU.mult, [qTR, qdR], [qdTR])
            for ti, (a, b, segs, smp) in enumerate(tiles):
                n = b - a
                po = [(PS[4], psR[4]), (PS[5], psR[5])]
                if not smp:
                    nkb = b // 128
                    for kb in range(nkb):
                        c0 = max(a, kb * 128)
                        w_ = b - c0
                        ps_, psr_ = ps_rot()
                        mm(ps_[:, 0:w_], kT[:, kb * 128:(kb + 1) * 128], qT[:, c0:b], True, True, [kTR, qTR], [psr_])
                        pt_, ptr_ = PT[ptc % 2]
                        ptc += 1
                        tt(dve, pt_[:, 0:w_], ps_[:, 0:w_], strip[:, c0 - kb * 128:b - kb * 128], ALU.mult,
                           [psr_, stripR], [ptr_])
                        last = (kb == nkb - 1) and gi == 0
                        for vc in range(2):
                            mm(po[vc][0][:, c0 - a:n], vtok[:, kb * 256 + vc * 128:kb * 256 + vc * 128 + 128],
                               pt_[:, 0:w_], kb == 0, last, [vtokR, ptr_], [po[vc][1]], force=(vc == 1))
                    if gi > 0:
                        for vc in range(2):
                            mm(po[vc][0][:, 0:n], stbf[:, h * 256 + vc * 128:h * 256 + vc * 128 + 128], qdT[:, a:b],
                               False, True, [stR[h], qdTR], [po[vc][1]])
                else:
                    blk = a // 128
                    ps_, psr_ = ps_rot()
                    mm(ps_[:, 0:128], kT[:, a:b], qT[:, a:b], True, True, [kTR, qTR], [psr_])
                    pt_, ptr_ = PT[ptc % 2]
                    ptc += 1
                    tt(dve, pt_[:, 0:128], ps_[:, 0:128], DS[:], ALU.mult, [psr_, DSR], [ptr_])
                    for vc in range(2):
                        mm(po[vc][0][:, 0:128], vtok[:, blk * 256 + vc * 128:blk * 256 + vc * 128 + 128],
                           pt_[:, 0:128], True, False, [vtokR, ptr_], [po[vc][1]])
                    for vc in range(2):
                        for s4 in range(4):
                            mm(po[vc][0][:, 32 * s4:32 * s4 + 32], stSbf[s4][0][:, vc * 128:vc * 128 + 128],
                               qdT[:, a + 32 * s4:a + 32 * s4 + 32], False, s4 == 3, [stSbf[s4][1], qdTR],
                               [po[vc][1]])
                for vc in range(2):
                    actf(sqg[:, vc * 512:vc * 512 + n], po[vc][0][:, 0:n], AF.Square, [po[vc][1]], [sqgR])
                    actf(og[:, vc * 512:vc * 512 + n], po[vc][0][:, 0:n], AF.Identity, [po[vc][1], smallR], [ogR],
                         scale=g_gn[:, h * 2 + vc:h * 2 + vc + 1])
                    tt(dve, og[:, vc * 512:vc * 512 + n], og[:, vc * 512:vc * 512 + n],
                       sg[:, vc * TMAX + a:vc * TMAX + b], ALU.mult, [sgR], [ogR])
                for vc in range(2):
                    mm(PS[7][:, 0:n], ones[:], sqg[:, vc * 512:vc * 512 + n], vc == 0, vc == 1, [constR, sqgR], [psR[7]])
                rt, rr = rstd_from(PS[7][:, 0:n], psR[7], n, 256)
                for vc in range(2):
                    tt(pc, col(ub, h * 2 + vc, a, b), og[:, vc * 512:vc * 512 + n], rt[:, 0:n], ALU.mult,
                       [ogR, rr], [uR[h * 2 + vc][ti]])
            pst, pstr = PS[6], psR[6]
            for blk in range(npb):
                mm(pst[:, 0:256], kdtok[:, blk * 128:(blk + 1) * 128], vtok[:, blk * 256:(blk + 1) * 256],
                   blk == 0, blk == npb - 1, [kdtokR, vtokR], [pstr])
            if gi == 0:
                cp(dve, st32[:, h * 256:(h + 1) * 256], pst[:, 0:256], [pstr], [stR[h]])
            else:
                ts(dve, st32[:, h * 256:(h + 1) * 256], st32[:, h * 256:(h + 1) * 256], 1.0,
                   rc_sb[:, h * 3 + var:h * 3 + var + 1], ALU.mult, ALU.mult, [smallR], [stR[h]])
                tt(dve, st32[:, h * 256:(h + 1) * 256], st32[:, h * 256:(h + 1) * 256], pst[:, 0:256], ALU.add,
                   [pstr], [stR[h]])
            actf(stbf[:, h * 256:(h + 1) * 256], st32[:, h * 256:(h + 1) * 256], AF.Copy, [stR[h]], [stR[h]])
            if gi == len(GROUPS) - 1:
                ldma(stp_o[h], st32[:, h * 256:(h + 1) * 256], [stR[h]], [])
            if has_sample:
                blk = npb
                for s4 in range(4):
                    mm(pst[:, 0:256], kdS[s4][0][:], vtok[:, blk * 256:(blk + 1) * 256], True, True,
                       [kdS[s4][1], vtokR], [pstr])
                    so, sor = stO[0]
                    ldma(stS32[0][0][:], st_in_d[s4, h], [], [stS32[0][1]])
                    ts(dve, so[:], stS32[0][0][:], 1.0, rc_sb[:, h * 3 + 2:h * 3 + 3], ALU.mult, ALU.mult,
                       [smallR, stS32[0][1]], [sor])
                    tt(dve, so[:], so[:], pst[:, 0:256], ALU.add, [pstr], [sor])
                    ldma(sts_o[s4, h], so[:], [sor], [])
        out_proj(w_outa_d, 16, ub, uR, tiles, lambda c, q: abcv(l, s, 2, c, q))

    def shared_kv(gi, g0, T, tiles):
        prenorm(tiles, lambda c, q: kvab[:, c * 5 + q:c * 5 + q + 1], lambda c, q: kvab[:, 40 + c * 5 + q:41 + c * 5 + q])
        arena_reset()
        Ct, tabR = aalloc(TMAX, F32)
        St, _ = aalloc(TMAX, F32)
        ldma(Ct[:, 0:T], rotD_C_d[:, g0:g0 + T], [], [tabR])
        ldma(St[:, 0:T], rotD_S_d[:, g0:g0 + T], [], [tabR])
        rtmps = [(tmpS[0], tmpSR[0]), aalloc(512, F32), (tmpT[0], tmpTR[0])]
        kf = [aalloc(512, F32) for _ in range(2)]
        vf = [aalloc(512, F32) for _ in range(2)]
        kc = 0
        for j in range(2):
            slab, slr = next_slab()
            load_wslab(w_kv_d[:, j * 512:(j + 1) * 512], 8, 512, 0, slab, slr)
            for ti, (a, b, segs, smp) in enumerate(tiles):
                n = b - a
                for hh in range(4):
                    hd = j * 4 + hh
                    pk, pkr = ps_rot()
                    for k in range(8):
                        mm(pk[:, 0:n], slab[:, k * 512 + hh * 128:k * 512 + hh * 128 + 128], col(hb, k, a, b),
                           k == 0, k == 7, [slr, hR[k][ti]], [pkr])
                    ko, kor = kf[kc % 2]
                    kc += 1
                    rotary(pk[:, 0:n], pkr, n, Ct, St, tabR, (a, b), 32, 1.0, ko[:, 0:n], kor, rtmps)
                    ldma(kT_o[hd * 128:(hd + 1) * 128, g0 + a:g0 + b], ko[:, 0:n], [kor], [kT_oR[hd][gi]])
        vcn = 0
        for j in range(2):
            slab, slr = next_slab()
            load_wslab(w_kv_d[:, 1024 + j * 512:1024 + (j + 1) * 512], 8, 512, 0, slab, slr)
            for ti, (a, b, segs, smp) in enumerate(tiles):
                n = b - a
                for tb in range(n // 128):
                    ca = a + tb * 128
                    pv, pvr = ps_rot()
                    for k in range(8):
                        mm(pv[:, 0:512], col(hb, k, ca, ca + 128), slab[:, k * 512:(k + 1) * 512], k == 0, k == 7,
                           [slr, hR[k][ti]], [pvr])
                    vo, vor = vf[vcn % 2]
                    vcn += 1
                    actf(vo[:], pv[:, 0:512], AF.Copy, [pvr], [vor])
                    ldma(v_o[g0 + ca:g0 + ca + 128, j * 512:(j + 1) * 512], vo[:], [vor], [v_oR[gi]])

    def diff_attn(gi, g0, T, tiles):
        l, s = 1, 1
        SC = 64 ** -0.5
        prenorm(tiles, lambda c, q: abcv(l, s, 0, c, q), lambda c, q: abcv(l, s, 1, c, q))
        arena_reset()
        Tp = sum(t[1] - t[0] for t in tiles if not t[3])
        nkeys = g0 + Tp
        nkb_all = nkeys // 128
        Ct, tabR = aalloc(TMAX, F32)
        St, _ = aalloc(TMAX, F32)
        ldma(Ct[:, 0:T], rotD_C_d[:, g0:g0 + T], [], [tabR])
        ldma(St[:, 0:T], rotD_S_d[:, g0:g0 + T], [], [tabR])
        rtmps = [(tmpS[0], tmpSR[0]), aalloc(512, F32), (tmpT[0], tmpTR[0])]
        KT, KTR = aalloc(2048, BF16)
        Vs, VsR = aalloc(16 * 128, BF16)
        qT, qTR = aalloc(1024, BF16)
        dve.op(lambda e: e.memset(qT[:], 0.0), [], [qTR])
        E = [aalloc(512, BF16) for _ in range(2)]
        r_sb, r_R = aalloc(512, F32)
        a0, a0R = aalloc(256, F32)
        a1, a1R = aalloc(256, F32)
        o_sb, o_R = a0, a0R
        sqo, sqoR = aalloc(256, BF16)
        has_sample = any(t[3] for t in tiles)
        if has_sample:
            KC, KCR = aalloc(PASTN, BF16)
            VC, VCR = aalloc(32 * 128, BF16)
            KN, KNR = aalloc(32, BF16)
            VN, VNR = aalloc(128, BF16)
        ec = 0
        KR = int(os.environ.get("KR", "9"))
        for h in range(H):
            slab, slr = next_slab()
            load_wslab(w_q_d[:, h * 128:(h + 1) * 128], 8, 128, 0, slab, slr)
            kdeps = [kT_oR[h][g] for g in range(gi + 1)]
            vdeps = [v_oR[g] for g in range(gi + 1)]
            wdma(KT[:, 0:nkeys], kT_o[h * 128:(h + 1) * 128, 0:nkeys], kdeps, [KTR])
            for b0 in range(0, nkb_all, 8):
                b1 = min(nkb_all, b0 + 8)
                wdma(Vs[:, b0 * 128:b1 * 128].rearrange("p (b v) -> p b v", b=b1 - b0),
                     v_o[b0 * 128:b1 * 128, h * 128:(h + 1) * 128].rearrange("(b p) v -> p b v", p=128), vdeps, [VsR])

            def finalize(acc_o, aoR, acc_s, asR, nq, dst_cols, ti):
                a_, b_ = dst_cols
                if KR < 5:
                    return
                dve.op(lambda e: e.reciprocal(out=r_sb[:, 0:2 * nq], in_=acc_s[:, 0:2 * nq]), [asR], [r_R])
                tt(dve, a0[:, 0:nq], acc_o[:, 0:nq], r_sb[:, 0:nq], ALU.mult, [aoR, r_R], [a0R])
                actf(a1[:, 0:nq], acc_o[:, nq:2 * nq], AF.Identity, [aoR, lamR], [a1R], scale=neg_lam[:, 0:1])
                tt(dve, a1[:, 0:nq], a1[:, 0:nq], r_sb[:, nq:2 * nq], ALU.mult, [r_R], [a1R])
                tt(pc, o_sb[:, 0:nq], a0[:, 0:nq], a1[:, 0:nq], ALU.add, [a1R], [o_R])
                if KR < 6:
                    return
                actf(sqo[:, 0:nq], o_sb[:, 0:nq], AF.Square, [o_R], [sqoR])
                mm(PS[7][:, 0:nq], ones[:], sqo[:, 0:nq], True, True, [constR, sqoR], [psR[7]])
                rt, rr = rstd_from(PS[7][:, 0:nq], psR[7], nq, 128)
                ts(dve, o_sb[:, 0:nq], o_sb[:, 0:nq], 1.0, g_sub2[:, 0:1], ALU.mult, ALU.mult, [lamR], [o_R])
                tt(dve, col(ub, h, a_, b_), o_sb[:, 0:nq], rt[:, 0:nq], ALU.mult, [o_R, rr], [uR[h][ti]])

            for ti, (a, b, segs, smp) in enumerate(tiles):
                n = b - a
                if KR < 2:
                    continue
                pq, pqr = ps_rot()
                for k in range(8):
                    mm(pq[:, 0:n], slab[:, k * 128:(k + 1) * 128], col(hb, k, a, b), k == 0, k == 7,
                       [slr, hR[k][ti]], [pqr])
                rotary(pq[:, 0:n], pqr, n, Ct, St, tabR, (a, b), 32, 1.0, None, qTR, rtmps,
                       split=(qT[0:64, 0:n], qT[64:128, 512:512 + n]))
                if KR < 3:
                    continue
                acc_o, aoR = PS[4], psR[4]
                acc_s, asR = PS[5], psR[5]
                if not smp:
                    for qa in range(0, n, 256):
                        nq = 256
                        gq0 = g0 + a + qa
                        kb_last = (gq0 + nq - 1) // 128
                        for kb in range(kb_last + 1):
                            c0 = max(0, kb * 128 - gq0)
                            w_ = nq - c0
                            lg, lgr = ps_rot()
                            for t in range(2):
                                mm(lg[:, t * nq + c0:(t + 1) * nq], KT[:, kb * 128:(kb + 1) * 128],
                                   qT[:, t * 512 + qa + c0:t * 512 + qa + nq], True, True, [KTR, qTR], [lgr])
                            Et, EtR = E[ec % 2]
                            ec += 1
                            if c0 == 0:
                                actf(Et[:, 0:2 * nq], lg[:, 0:2 * nq], AF.Exp, [lgr], [EtR], scale=SC)
                            else:
                                for t in range(2):
                                    actf(Et[:, t * nq + c0:(t + 1) * nq], lg[:, t * nq + c0:(t + 1) * nq], AF.Exp,
                                         [lgr], [EtR], scale=SC)
                            if KR < 4:
                                continue
                            if kb * 128 >= gq0:
                                for t in range(2):
                                    pc.op(lambda e, t=t, c0=c0, Et=Et, nq=nq: e.memset(Et[64:128, t * nq + c0:t * nq + c0 + 64], 0.0),
                                            [], [EtR])
                            if c0 == 0:
                                mm(acc_o[:, 0:2 * nq], Vs[:, kb * 128:(kb + 1) * 128], Et[:, 0:2 * nq],
                                   kb == 0, kb == kb_last, [VsR, EtR], [aoR])
                                mm(acc_s[:, 0:2 * nq], ones[:], Et[:, 0:2 * nq],
                                   kb == 0, kb == kb_last, [constR, EtR], [asR], force=True)
                            else:
                                for t in range(2):
                                    mm(acc_o[:, t * nq + c0:(t + 1) * nq], Vs[:, kb * 128:(kb + 1) * 128],
                                       Et[:, t * nq + c0:(t + 1) * nq], False, kb == kb_last and t == 1,
                                       [VsR, EtR], [aoR])
                                for t in range(2):
                                    mm(acc_s[:, t * nq + c0:(t + 1) * nq], ones[:], Et[:, t * nq + c0:(t + 1) * nq],
                                       False, kb == kb_last and t == 1, [constR, EtR], [asR], force=(t == 1))
                        finalize(acc_o, aoR, acc_s, asR, nq, (a + qa, a + qa + nq), ti)
                else:
                    for s4 in range(4):
                        nq = 32
                        wdma(KC[:], kcT_d[s4, h], [], [KCR])
                        for b0 in range(0, 32, 8):
                            wdma(VC[:, b0 * 128:(b0 + 8) * 128].rearrange("p (b v) -> p b v", b=8),
                                 vc_d[s4, b0 * 128:(b0 + 8) * 128, h * 128:(h + 1) * 128].rearrange("(b p) v -> p b v", p=128),
                                 [], [VCR])
                        gc = NTOK - 128 + 32 * s4
                        wdma(KN[:], kT_o[h * 128:(h + 1) * 128, gc:gc + 32], [kT_oR[h][gi]], [KNR])
                        wdma(VN[0:32, :], v_o[gc:gc + 32, h * 128:(h + 1) * 128], [v_oR[gi]], [VNR])
                        qc = 32 * s4
                        for kb8 in range(4):
                            lg, lgr = ps_rot()
                            for kbi in range(8):
                                kb = kb8 * 8 + kbi
                                for t in range(2):
                                    mm(lg[:, (kbi * 2 + t) * 32:(kbi * 2 + t) * 32 + 32],
                                       KC[:, kb * 128:(kb + 1) * 128],
                                       qT[:, t * 512 + qc:t * 512 + qc + 32], True, True, [KCR, qTR], [lgr])
                            Et, EtR = E[ec % 2]
                            ec += 1
                            actf(Et[:, 0:512], lg[:, 0:512], AF.Exp, [lgr], [EtR], scale=SC)
                            for kbi in range(8):
                                kb = kb8 * 8 + kbi
                                mm(acc_o[:, 0:64], VC[:, kb * 128:(kb + 1) * 128], Et[:, kbi * 64:kbi * 64 + 64],
                                   kb == 0, False, [VCR, EtR], [aoR])
                            for kbi in range(8):
                                kb = kb8 * 8 + kbi
                                mm(acc_s[:, 0:64], ones[:], Et[:, kbi * 64:kbi * 64 + 64], kb == 0, False,
                                   [constR, EtR], [asR], force=(kbi == 7))
                        lg, lgr = ps_rot()
                        for t in range(2):
                            mm(lg[0:32, t * 32:t * 32 + 32], KN[:, 0:32],
                               qT[:, t * 512 + qc:t * 512 + qc + 32], True, True, [KNR, qTR], [lgr])
                        Et, EtR = E[ec % 2]
                        ec += 1
                        actf(Et[0:32, 0:64], lg[0:32, 0:64], AF.Exp, [lgr], [EtR], scale=SC)
                        mm(acc_o[:, 0:64], VN[0:32, :], Et[0:32, 0:64], False, True, [VNR, EtR], [aoR])
                        mm(acc_s[:, 0:64], ones[0:32, :], Et[0:32, 0:64], False, True, [constR, EtR], [asR])
                        finalize(acc_o, aoR, acc_s, asR, nq, (a + qc, a + qc + 32), ti)
        if KR < 9:
            return
        out_proj(w_outb_d, 8, ub, uR, tiles, lambda c, q: abcv(l, s, 2, c, q))

    ident = sb("ident", 128, BF16)
    ident_d = din("ident", [128, 128])
    wdma(ident[:], ident_d, [], [constR])

    if os.environ.get("KNOSAMPLE"):
        GROUPS[2] = (1536, 512, [(0, 512, [(0, 512, 0)], False)])
    STAGE = int(os.environ.get("KSTAGE", "99"))
    NG = int(os.environ.get("KGROUPS", "3"))
    for gi, (g0, T, tiles) in enumerate(GROUPS[:NG]):
        for c in range(DC):
            for ti, (a, b, segs, _) in enumerate(tiles):
                ldma(col(xT, c, a, b), xT_d[c * 128:(c + 1) * 128, g0 + a:g0 + b], [], [xR[c][ti]])
        if STAGE >= 2:
            ffn(0, 0, tiles)
        if STAGE >= 3:
            retention(gi, g0, T, tiles)
        if STAGE >= 4:
            ffn(0, 1, tiles)
        if STAGE >= 5:
            shared_kv(gi, g0, T, tiles)
        if STAGE >= 6:
            ffn(1, 0, tiles)
        if STAGE >= 7:
            diff_attn(gi, g0, T, tiles)
        if STAGE >= 8:
            ffn(1, 1, tiles)
        for c in range(DC):
            for ti, (a, b, segs, _) in enumerate(tiles):
                ldma(yT_d[c * 128:(c + 1) * 128, g0 + a:g0 + b], col(xT, c, a, b), [xR[c][ti]], [])

    fin = {}
    for e in (pool, sp):
        for s_, c_ in zip(e.dsems, e.dcnt):
            if c_ > 0:
                fin[id(s_)] = (s_, c_)
    sp._wait(fin)

    with nc.Block() as block:
        @block.tensor
        def _(e):
            pe.run(e)

        @block.scalar
        def _(e):
            act.run(e)

        @block.vector
        def _(e):
            dve.run(e)

        @block.gpsimd
        def _(e):
            pool.run(e)

        @block.sync
        def _(e):
            sp.run(e)
    es.close()
    return nc


def _const_tables():
    lg = np.log1p(-np.exp2(-5.0 - np.arange(H, dtype=np.float64)))
    pos = np.concatenate([np.arange(2048, dtype=np.float64)] + [PASTN + np.arange(32, dtype=np.float64)] * 4)

    def rot(dh):
        half = dh // 2
        inv = np.power(10000.0, -np.arange(0, dh, 2, dtype=np.float32) / np.float32(dh)).astype(np.float32)
        ang = (pos.astype(np.float32)[None, :] * inv[:, None]).astype(np.float32)
        cos, sin = np.cos(ang).astype(np.float32), np.sin(ang).astype(np.float32)
        C = np.zeros((128, NTOK), np.float32)
        S = np.zeros((128, NTOK), np.float32)
        for p in range(128):
            dd = p % dh
            f = dd % half
            C[p] = cos[f]
            S[p] = -sin[f] if dd < half else sin[f]
        return C, S

    RC, RS = rot(128)
    DCc, DSs = rot(64)
    jl = np.arange(128)[:, None].astype(np.float64)
    dl = np.arange(768)[None, :].astype(np.float64)
    strip = np.zeros((H, 128, 768), np.float32)
    qd = np.zeros((H, 128, 768), np.float32)
    DS = np.zeros((H, 128, 128), np.float32)
    qdS = np.zeros((H, 128, 128), np.float32)
    kd = np.zeros((128, H * 2 * 6), np.float32)
    kdS = np.zeros((128, H * 4), np.float32)
    rc = np.zeros((128, H * 3), np.float32)
    for h in range(H):
        g = lg[h]
        st = np.exp(g * (dl - jl))
        il = np.arange(128)[None, :].astype(np.float64)
        diag = np.exp(g * np.abs(il - jl))
        diag[(jl // 64) > (il // 64) * np.ones_like(jl)] = 0.0
        st[:, 0:128] = diag
        strip[h] = st.astype(np.float32)
        qd[h] = np.broadcast_to(np.exp(g * dl), (128, 768)).astype(np.float32)
        m = np.exp(g * np.abs(il - jl))
        m[(jl // 32) != (il // 32) * np.ones_like(jl)] = 0.0
        DS[h] = m.astype(np.float32)
        qdS[h] = np.broadcast_to(np.exp(g * (np.arange(128) % 32))[None, :], (128, 128)).astype(np.float32)
        for var, Tp in enumerate((768, 512)):
            for blk in range(6):
                j = blk * 128 + np.arange(128)
                kd[:, (h * 2 + var) * 6 + blk] = np.exp(g * (Tp - j)).astype(np.float32)
        for s4 in range(4):
            p = np.arange(128)
            v = np.exp(g * (32 - (p % 32)))
            v[(p // 32) != s4] = 0.0
            kdS[:, h * 4 + s4] = v.astype(np.float32)
        rc[:, h * 3 + 0] = np.float32(np.exp(g * 768))
        rc[:, h * 3 + 1] = np.float32(np.exp(g * 512))
        rc[:, h * 3 + 2] = np.float32(np.exp(g * 32))
    return dict(rotR_C=RC, rotR_S=RS, rotD_C=DCc, rotD_S=DSs, ret_strip=strip, ret_qd=qd, ret_kd=kd, ret_DS=DS,
                ret_qdS=qdS, ret_kdS=kdS, ret_c=rc, ident=np.eye(128, dtype=np.float32))


def _fm(v, nch):
    v = np.asarray(v, np.float32)
    lead = v.shape[:-1]
    r = v.reshape(lead + (nch, 128))
    r = np.moveaxis(r, -1, 0)
    return np.ascontiguousarray(r.reshape(128, -1))


_NC_CACHE = {}


def kernel(x_prompt, x_sample, state_ret, cache_k, cache_v, c_prompt, c_sample,
           w_ada, b_ada, g_norm, w_ffn_in, w_ffn_out, w_in_a, g_gn_a, w_out_a,
           w_ada_kv, b_ada_kv, g_kv, w_kv, w_q_b, lam_b, g_subln_b, w_out_b):
    in_maps = make_in_maps(x_prompt, x_sample, state_ret, cache_k, cache_v, c_prompt, c_sample,
                           w_ada, b_ada, g_norm, w_ffn_in, w_ffn_out, w_in_a, g_gn_a, w_out_a,
                           w_ada_kv, b_ada_kv, g_kv, w_kv, w_q_b, lam_b, g_subln_b, w_out_b)
    n = 8
    if "nc" not in _NC_CACHE:
        _NC_CACHE["nc"] = build_program()
    nc = _NC_CACHE["nc"]
    res = run_bass_kernel_spmd(nc, in_maps, core_ids=list(range(n)))
    return assemble(res.results)


def make_in_maps(x_prompt, x_sample, state_ret, cache_k, cache_v, c_prompt, c_sample,
                 w_ada, b_ada, g_norm, w_ffn_in, w_ffn_out, w_in_a, g_gn_a, w_out_a,
                 w_ada_kv, b_ada_kv, g_kv, w_kv, w_q_b, lam_b, g_subln_b, w_out_b, cores=range(8)):
    f = lambda a: np.ascontiguousarray(np.asarray(a, np.float32))
    shared = dict(
        w_ada=f(w_ada), b_adaT=_fm(b_ada, 72), g_normT=_fm(g_norm, 8), w_ffn_in=f(w_ffn_in), w_ffn_out=f(w_ffn_out),
        w_in_a=f(w_in_a[0]), g_gnT=_fm(g_gn_a[0], 16), w_out_a=f(w_out_a[0]), w_ada_kv=f(w_ada_kv),
        b_kvT=_fm(b_ada_kv, 16), g_kvT=_fm(g_kv, 8), w_kv=f(w_kv), w_q_b=f(w_q_b[0]),
        lam_bb=np.ascontiguousarray(np.broadcast_to(np.asarray(lam_b[0], np.float32).reshape(1, 256), (128, 256))),
        g_subT=f(np.asarray(g_subln_b[0]).reshape(128, 1)), w_out_b=f(w_out_b[0]),
    )
    shared.update(_const_tables())
    x_prompt = np.asarray(x_prompt, np.float32)
    x_sample = np.asarray(x_sample, np.float32)
    cache_k = np.asarray(cache_k, np.float32)
    cache_v = np.asarray(cache_v, np.float32)
    state_ret = np.asarray(state_ret, np.float32)
    in_maps = []
    for i in cores:
        xs = x_sample[4 * i:4 * i + 4].reshape(128, D)
        xT = np.ascontiguousarray(np.concatenate([x_prompt[i], xs], axis=0).T)
        c5 = np.concatenate([np.asarray(c_prompt, np.float32)[i:i + 1], np.asarray(c_sample, np.float32)[4 * i:4 * i + 4]], 0)
        cT = np.ascontiguousarray(c5.reshape(5, 8, 128).transpose(2, 1, 0).reshape(128, 40))
        kcT = np.ascontiguousarray(cache_k[4 * i:4 * i + 4].reshape(4, PASTN, H, 128).transpose(0, 2, 3, 1))
        m = dict(shared)
        m.update(xT=xT, cT=cT, st_in=np.ascontiguousarray(state_ret[0, 4 * i:4 * i + 4]), kcT=kcT,
                 vc=np.ascontiguousarray(cache_v[4 * i:4 * i + 4].reshape(4, PASTN, D)))
        in_maps.append(m)
    return in_maps


def assemble(R):
    n = 8
    y_p = np.stack([R[i]["yT"][:, :2048].T for i in range(n)])
    y_s = np.concatenate([R[i]["yT"][:, 2048:].T.reshape(4, 32, D) for i in range(n)])
    st_p = np.stack([R[i]["st_p"] for i in range(n)])[None]
    k_p = np.stack([R[i]["kT_out"][:, :2048].T.reshape(2048, 16, 64) for i in range(n)])
    v_p = np.stack([R[i]["v_out"][:2048].reshape(2048, 8, 128) for i in range(n)])
    st_s = np.concatenate([R[i]["st_s"] for i in range(n)])[None]
    k_s = np.concatenate([R[i]["kT_out"][:, 2048:].T.reshape(4, 32, 16, 64) for i in range(n)])
    v_s = np.concatenate([R[i]["v_out"][2048:].reshape(4, 32, 8, 128) for i in range(n)])
    out = (y_p, y_s, st_p, k_p, v_p, st_s, k_s, v_s)
    return tuple(np.ascontiguousarray(o, dtype=np.float32) for o in out)
```

```python
import math
import os
import contextlib
import numpy as np
import concourse.bass as bass
import concourse.mybir as mybir
from concourse.bass_utils import run_bass_kernel_spmd

F32 = mybir.dt.float32
BF16 = mybir.dt.bfloat16
ALU = mybir.AluOpType
AF = mybir.ActivationFunctionType
AX = mybir.AxisListType

D = 1024
DC = 8
DFF = 2816
FC = 22
NSEQ = 5
NTOK = 2176
TMAX = 768
EPS = 1e-6
H = 8
LAM_INIT = 0.8 - 0.6 * math.exp(-0.3 * 1)
PASTN = 4096

GROUPS = [
    (0, 768, [(0, 512, [(0, 512, 0)], False), (512, 768, [(512, 768, 0)], False)]),
    (768, 768, [(0, 512, [(0, 512, 0)], False), (512, 768, [(512, 768, 0)], False)]),
    (1536, 640, [(0, 512, [(0, 512, 0)], False),
                 (512, 640, [(512 + 32 * s, 544 + 32 * s, 1 + s) for s in range(4)], True)]),
]


def _merge(d, src):
    for k, v in src.items():
        o = d.get(k)
        if o is None or o[1] < v[1]:
            d[k] = v


class Res:
    __slots__ = ("W", "R")

    def __init__(self, init=None):
        self.W = dict(init) if init else {}
        self.R = {}


class Eng:
    def __init__(self, name, sem, inorder=False, dsems=()):
        self.name = name
        self.sem = sem
        self.n = 0
        self.seen = {}
        self.prog = []
        self.inorder = inorder
        self.dsems = list(dsems)
        self.dcnt = [0] * len(self.dsems)
        self.di = 0

    def _wait(self, deps):
        for key, (sem, val) in deps.items():
            if self.inorder and sem is self.sem:
                continue
            if self.seen.get(key, 0) >= val:
                continue
            self.seen[key] = val
            self.prog.append(("wait", sem, val))

    def op(self, fn, reads=(), writes=(), mark=True):
        deps = {}
        for r in reads:
            _merge(deps, r.W)
        for w in writes:
            _merge(deps, w.W)
            _merge(deps, w.R)
        self._wait(deps)
        tk = (self.sem, self.n + 1)
        if mark:
            self.n += 1
        self.prog.append(("op", fn, mark))
        k = id(self.sem)
        for r in reads:
            o = r.R.get(k)
            if o is None or o[1] < tk[1]:
                r.R[k] = tk
        for w in writes:
            w.W = {k: tk}
            w.R = {}

    def dma(self, out_ap, in_ap, reads=(), writes=()):
        i = self.di % len(self.dsems)
        self.di += 1
        sem = self.dsems[i]
        prev = self.dcnt[i]
        deps = {}
        for r in reads:
            _merge(deps, r.W)
        for w in writes:
            _merge(deps, w.W)
            _merge(deps, w.R)
        if prev > 0:
            deps[id(sem)] = (sem, prev)
        self._wait(deps)
        self.dcnt[i] = prev + 16
        self.prog.append(("dma", out_ap, in_ap, sem))
        tk = (sem, prev + 16)
        k = id(sem)
        for r in reads:
            o = r.R.get(k)
            if o is None or o[1] < tk[1]:
                r.R[k] = tk
        for w in writes:
            w.W[k] = tk
            w.R = {}

    def run(self, e):
        for it in self.prog:
            if it[0] == "wait":
                e.wait_ge(it[1], it[2])
            elif it[0] == "op":
                ins = it[1](e)
                if it[2]:
                    ins.then_inc(self.sem, 1)
            else:
                e.dma_start(out=it[1], in_=it[2]).then_inc(it[3], 16)


class K:
    pass


def build_program():
    nc = bass.Bass("TRN2", target_bir_lowering=False)
    es = contextlib.ExitStack()

    def din(name, shape, dt=F32):
        return nc.dram_tensor(name, list(shape), dt, kind="ExternalInput").ap()

    def dout(name, shape):
        return nc.dram_tensor(name, list(shape), F32, kind="ExternalOutput").ap()

    xT_d = din("xT", [D, NTOK])
    cT_d = din("cT", [128, DC * NSEQ])
    st_in_d = din("st_in", [4, H, 128, 256])
    kcT_d = din("kcT", [4, H, 128, PASTN])
    vc_d = din("vc", [4, PASTN, D])
    w_ada_d = din("w_ada", [2, D, 9 * D])
    b_ada_d = din("b_adaT", [128, 2 * 72])
    g_norm_d = din("g_normT", [128, 2 * 6 * 8])
    w_fi_d = din("w_ffn_in", [2, 2, D, 2 * DFF])
    w_fo_d = din("w_ffn_out", [2, 2, DFF, D])
    w_ina_d = din("w_in_a", [D, 6144])
    g_gn_d = din("g_gnT", [128, 16])
    w_outa_d = din("w_out_a", [2048, D])
    w_adakv_d = din("w_ada_kv", [D, 2048])
    b_kv_d = din("b_kvT", [128, 16])
    g_kv_d = din("g_kvT", [128, 8])
    w_kv_d = din("w_kv", [D, 2048])
    w_q_d = din("w_q_b", [D, D])
    lam_d = din("lam_bb", [128, 256])
    g_sub_d = din("g_subT", [128, 1])
    w_outb_d = din("w_out_b", [D, D])
    rotR_C_d = din("rotR_C", [128, NTOK])
    rotR_S_d = din("rotR_S", [128, NTOK])
    rotD_C_d = din("rotD_C", [128, NTOK])
    rotD_S_d = din("rotD_S", [128, NTOK])
    strip_d = din("ret_strip", [H, 128, 768])
    qd_d = din("ret_qd", [H, 128, 768])
    kd_d = din("ret_kd", [128, H * 2 * 6])
    DS_d = din("ret_DS", [H, 128, 128])
    qdS_d = din("ret_qdS", [H, 128, 128])
    kdS_d = din("ret_kdS", [128, H * 4])
    rc_d = din("ret_c", [128, H * 3])

    yT_d = dout("yT", [D, NTOK])
    kT_o = dout("kT_out", [D, NTOK])
    v_o = dout("v_out", [NTOK, D])
    stp_o = dout("st_p", [H, 128, 256])
    sts_o = dout("st_s", [4, H, 128, 256])

    def sem(name):
        return es.enter_context(nc.semaphore(name))

    pe = Eng("pe", sem("s_pe"), inorder=True)
    act = Eng("act", sem("s_act"))
    dve = Eng("dve", sem("s_dve"))
    pool = Eng("pool", sem("s_pool"), dsems=[sem(f"dq_p{i}") for i in range(8)])
    sp = Eng("sp", sem("s_sp"), dsems=[sem(f"dq_s{i}") for i in range(8)])
    engines = [pe, act, dve, pool, sp]
    pc = pool if os.environ.get('KPOOL') else dve

    def fence():
        d = {}
        for e in engines:
            if e.n > 0:
                d[id(e.sem)] = (e.sem, e.n)
            for s_, c_ in zip(e.dsems, e.dcnt):
                if c_ > 0:
                    d[id(s_)] = (s_, c_)
        return d

    def sb(name, cols, dt):
        return es.enter_context(nc.sbuf_tensor("sb_" + name, [128, cols], dt))

    xT = sb("xT", DC * TMAX, F32)
    hb = sb("hb", DC * TMAX, BF16)
    ub = sb("ub", FC * TMAX, BF16)
    osb = sb("osb", DC * TMAX, F32)
    slabs = [sb(f"slab{i}", 6144, BF16) for i in range(3)]
    tmpS = [sb(f"tmpS{i}", 512, F32) for i in range(2)]
    tmpT = [sb(f"tmpT{i}", 512, F32) for i in range(2)]
    rs_t = sb("rs_t", 512, F32)
    rstd_t = [sb(f"rstd{i}", 512, F32) for i in range(1)]
    ones = sb("ones", 128, BF16)
    modT = sb("modT", 2 * 360, F32)
    kvmod = sb("kvmod", 80, F32)
    abc = sb("abc", 2 * 3 * 3 * 40, F32)
    kvab = sb("kvab", 80, F32)
    b_ada = sb("b_ada", 144, F32)
    g_norm = sb("g_norm", 96, F32)
    g_gn = sb("g_gn", 16, F32)
    b_kv = sb("b_kv", 16, F32)
    g_kv = sb("g_kv", 8, F32)
    g_sub = sb("g_sub", 1, F32)
    g_sub2 = sb("g_sub2", 1, F32)
    lam_sb = sb("lam_sb", 256, F32)
    lam_t = sb("lam_t", 128, F32)
    lam_r = sb("lam_r", 4, F32)
    neg_lam = sb("neg_lam", 1, F32)
    c_sb = sb("c_sb", 40, F32)
    scT = sb("scT", 40, BF16)
    kd_sb = sb("kd_sb", H * 12, F32)
    kdS_sb = sb("kdS_sb", H * 4, F32)
    rc_sb = sb("rc_sb", H * 3, F32)
    st32 = sb("st32", H * 256, F32)
    stbf = sb("stbf", H * 256, BF16)
    ARENA_COLS = 11160
    arena = sb("arena", ARENA_COLS, F32)

    PS = [es.enter_context(nc.psum_tensor(f"ps{i}", [128, 512], F32)) for i in range(8)]
    psR = [Res() for _ in range(8)]
    rot_state = [0]
    rot_n = [4]

    def ps_rot():
        i = rot_state[0] % rot_n[0]
        rot_state[0] += 1
        return PS[i], psR[i]

    xR = [[Res() for _ in range(2)] for _ in range(DC)]
    hR = [[Res() for _ in range(2)] for _ in range(DC)]
    uR = [[Res() for _ in range(2)] for _ in range(FC)]
    oR = [[Res() for _ in range(2)] for _ in range(DC)]
    slR = [Res() for _ in range(3)]
    tmpSR = [Res() for _ in range(2)]
    tmpTR = [Res() for _ in range(2)]
    rsR = Res()
    rstdR = [Res() for _ in range(1)]
    constR = Res()
    modR = Res()
    stR = [Res() for _ in range(H)]
    kT_oR = [[Res() for _ in range(3)] for _ in range(H)]
    v_oR = [Res() for _ in range(3)]
    cnt = {"slab": 0, "tS": 0, "tT": 0, "rstd": 0}

    def next_slab():
        i = cnt["slab"] % 3
        cnt["slab"] += 1
        return slabs[i], slR[i]

    def col(buf, c, a, b):
        return buf[:, c * TMAX + a: c * TMAX + b]

    def mm(ps_ap, lhsT, rhs, start, stop, reads, writes, force=False):
        pe.op(lambda e: e.matmul(ps_ap, lhsT, rhs, start=start, stop=stop), reads, writes, mark=(stop or force))

    def actf(out, in_, func, reads, writes, scale=None, bias=None):
        kw = {}
        if scale is not None:
            kw["scale"] = scale
        if bias is not None:
            kw["bias"] = bias
        act.op(lambda e: e.activation(out=out, in_=in_, func=func, **kw), reads, writes)

    def tt(eng, out, in0, in1, op, reads, writes):
        eng.op(lambda e: e.tensor_tensor(out=out, in0=in0, in1=in1, op=op), reads, writes)

    def ts(eng, out, in0, s1, s2, op0, op1, reads, writes):
        eng.op(lambda e: e.tensor_scalar(out=out, in0=in0, scalar1=s1, scalar2=s2, op0=op0, op1=op1), reads, writes)

    def stt(out, in0, scalar, in1, op0, op1, reads, writes):
        dve.op(lambda e: e.scalar_tensor_tensor(out=out, in0=in0, scalar=scalar, in1=in1, op0=op0, op1=op1),
               reads, writes)

    def cp(eng, out, in_, reads, writes):
        eng.op(lambda e: e.tensor_copy(out=out, in_=in_), reads, writes)

    def wdma(out_ap, in_ap, reads, writes):
        pool.dma(out_ap, in_ap, reads, writes)

    def ldma(out_ap, in_ap, reads, writes):
        sp.dma(out_ap, in_ap, reads, writes)

    def load_wslab(src2d, kch, ncols, off=0, slab=None, slr=None):
        for k0 in range(0, kch, 8):
            k1 = min(kch, k0 + 8)
            dst = slab[:, off + k0 * ncols: off + k1 * ncols].rearrange("p (k n) -> p k n", k=k1 - k0)
            wdma(dst, src2d[k0 * 128:k1 * 128, :].rearrange("(k p) n -> p k n", p=128), [], [slr])

    dve.op(lambda e: e.memset(ones[:], 1.0), [], [constR])
    smallR = Res()
    for dst, src in ((b_ada, b_ada_d), (g_norm, g_norm_d), (g_gn, g_gn_d), (b_kv, b_kv_d), (g_kv, g_kv_d),
                     (g_sub, g_sub_d), (lam_sb, lam_d), (c_sb, cT_d), (kd_sb, kd_d), (kdS_sb, kdS_d),
                     (rc_sb, rc_d)):
        ldma(dst[:], src, [], [smallR])
    lamR = Res()
    tt(dve, lam_t[:, 0:64], lam_sb[:, 0:64], lam_sb[:, 64:128], ALU.mult, [smallR], [lamR])
    tt(dve, lam_t[:, 64:128], lam_sb[:, 128:192], lam_sb[:, 192:256], ALU.mult, [smallR], [lamR])
    dve.op(lambda e: e.reduce_sum(out=lam_r[:, 0:1], in_=lam_t[:, 0:64], axis=AX.X), [lamR], [lamR])
    dve.op(lambda e: e.reduce_sum(out=lam_r[:, 1:2], in_=lam_t[:, 64:128], axis=AX.X), [lamR], [lamR])
    actf(lam_r[:, 2:4], lam_r[:, 0:2], AF.Exp, [lamR], [lamR])
    tt(dve, neg_lam[:], lam_r[:, 3:4], lam_r[:, 2:3], ALU.subtract, [lamR], [lamR])
    ts(dve, neg_lam[:], neg_lam[:], -LAM_INIT, None, ALU.add, ALU.bypass, [lamR], [lamR])
    ts(dve, g_sub2[:], g_sub[:], 1.0 - LAM_INIT, None, ALU.mult, ALU.bypass, [smallR], [lamR])
    actf(scT[:], c_sb[:], AF.Silu, [smallR], [modR])

    def ada_mm(wsrc, ncols_total, psb, psr):
        nsl = ncols_total // 512
        for j in range(nsl):
            slab, slr = next_slab()
            load_wslab(wsrc[:, j * 512:(j + 1) * 512], 8, 512, 0, slab, slr)
            for nn in range(4):
                n = j * 4 + nn
                for k in range(8):
                    mm(psb[:, n * 5:(n + 1) * 5], slab[:, k * 512 + nn * 128: k * 512 + nn * 128 + 128],
                       scT[:, k * 5:(k + 1) * 5], k == 0, k == 7, [slr, modR], [psr])

    for l in range(2):
        ada_mm(w_ada_d[l], 9 * D, PS[4 + l], psR[4 + l])
        for s in range(NSEQ):
            tt(dve, modT[:, l * 360 + s: (l + 1) * 360: 5], PS[4 + l][:, s:360:5], b_ada[:, l * 72:(l + 1) * 72],
               ALU.add, [psR[4 + l], smallR], [modR])
    ada_mm(w_adakv_d, 2048, PS[6], psR[6])
    for s in range(NSEQ):
        tt(dve, kvmod[:, s:80:5], PS[6][:, s:80:5], b_kv[:, 0:16], ALU.add, [psR[6], smallR], [modR])

    def modv(l, m, c):
        o = l * 360 + (m * 8 + c) * 5
        return modT[:, o:o + 5]

    def abcv(l, s, kind, c, seq=None):
        o = (((l * 3 + s) * 3 + kind) * 8 + c) * 5
        if seq is None:
            return abc[:, o:o + 5]
        return abc[:, o + seq:o + seq + 1]

    abcR = Res()
    for l in range(2):
        for s in range(3):
            half = 1.0 if s == 1 else 0.5
            for c in range(DC):
                gpre = g_norm[:, (l * 6 + 2 * s) * 8 + c:(l * 6 + 2 * s) * 8 + c + 1]
                gpost = g_norm[:, (l * 6 + 2 * s + 1) * 8 + c:(l * 6 + 2 * s + 1) * 8 + c + 1]
                ts(dve, abcv(l, s, 0, c), modv(l, 3 * s + 1, c), 1.0, gpre, ALU.add, ALU.mult, [modR, smallR], [abcR])
                cp(dve, abcv(l, s, 1, c), modv(l, 3 * s, c), [modR], [abcR])
                ts(dve, abcv(l, s, 2, c), modv(l, 3 * s + 2, c), half, gpost, ALU.mult, ALU.mult, [modR, smallR], [abcR])
    for c in range(DC):
        ts(dve, kvab[:, c * 5:(c + 1) * 5], kvmod[:, (8 + c) * 5:(9 + c) * 5], 1.0, g_kv[:, c:c + 1], ALU.add, ALU.mult,
           [modR, smallR], [abcR])
        cp(dve, kvab[:, 40 + c * 5:45 + c * 5], kvmod[:, c * 5:(c + 1) * 5], [modR], [abcR])

    def rstd_from(ps_ap, psr, n, dim):
        i = 0
        ts(dve, rs_t[:, 0:n], ps_ap, 1.0 / dim, EPS, ALU.mult, ALU.add, [psr], [rsR])
        actf(rs_t[:, 0:n], rs_t[:, 0:n], AF.Ln, [rsR], [rsR])
        actf(rstd_t[i][:, 0:n], rs_t[:, 0:n], AF.Exp, [rsR], [rstdR[i]], scale=-0.5)
        return rstd_t[i], rstdR[i]

    def prenorm(tiles, Afn, Bfn):
        for ti, (a, b, segs, _) in enumerate(tiles):
            n = b - a
            for c in range(DC):
                actf(col(ub, c, a, b), col(xT, c, a, b), AF.Square, [xR[c][ti]], [uR[c][ti]])
            for c in range(DC):
                mm(PS[7][:, 0:n], ones[:], col(ub, c, a, b), c == 0, c == DC - 1, [constR, uR[c][ti]], [psR[7]])
            rt, rr = rstd_from(PS[7][:, 0:n], psR[7], n, D)
            for c in range(DC):
                j = cnt["tT"] % 2
                cnt["tT"] += 1
                tt(dve, tmpT[j][:, 0:n], col(xT, c, a, b), rt[:, 0:n], ALU.mult, [xR[c][ti], rr], [tmpTR[j]])
                for (sa, sb_, seq) in segs:
                    ts(pc, col(hb, c, sa, sb_), tmpT[j][:, sa - a:sb_ - a], Afn(c, seq), Bfn(c, seq),
                       ALU.mult, ALU.add, [tmpTR[j], abcR], [hR[c][ti]])

    def post_chunk(po, por, c, ti, tile, Cfn):
        a, b, segs, _ = tile
        n = b - a
        KSUB = int(os.environ.get("KSUB", "9"))
        if KSUB < 4:
            return
        actf(col(hb, c, a, b), po[:, 0:n], AF.Square, [por], [hR[c][ti]])
        if KSUB < 5:
            return
        for (sa, sb_, seq) in segs:
            actf(col(osb, c, sa, sb_), po[:, sa - a:sb_ - a], AF.Identity, [por, abcR], [oR[c][ti]], scale=Cfn(c, seq))

    def post_final(tiles):
        for ti, (a, b, segs, _) in enumerate(tiles):
            n = b - a
            for c in range(DC):
                mm(PS[7][:, 0:n], ones[:], col(hb, c, a, b), c == 0, c == DC - 1, [constR, hR[c][ti]], [psR[7]])
            rt, rr = rstd_from(PS[7][:, 0:n], psR[7], n, D)
            for c in range(DC):
                j = cnt["tT"] % 2
                cnt["tT"] += 1
                tt(dve, tmpT[j][:, 0:n], col(osb, c, a, b), rt[:, 0:n], ALU.mult, [oR[c][ti], rr], [tmpTR[j]])
                tt(pc, col(xT, c, a, b), col(xT, c, a, b), tmpT[j][:, 0:n], ALU.add, [tmpTR[j]], [xR[c][ti]])

    def out_proj(wsrc, kch, src_buf, srcR, tiles, Cfn):
        for ns in range(4):
            slab, slr = next_slab()
            load_wslab(wsrc[:, ns * 256:(ns + 1) * 256], kch, 256, 0, slab, slr)
            for ti, tile in enumerate(tiles):
                a, b = tile[0], tile[1]
                n = b - a
                for nn in range(2):
                    c = ns * 2 + nn
                    po, por = ps_rot()
                    for k in range(kch):
                        mm(po[:, 0:n], slab[:, k * 256 + nn * 128:k * 256 + nn * 128 + 128], col(src_buf, k, a, b),
                           k == 0, k == kch - 1, [slr, srcR[k][ti]], [por])
                    post_chunk(po, por, c, ti, tile, Cfn)
        if int(os.environ.get("KSUB", "9")) < 6:
            return
        post_final(tiles)

    def ffn(l, f, tiles):
        s = 0 if f == 0 else 2
        KSUB = int(os.environ.get("KSUB", "9"))
        prenorm(tiles, lambda c, q: abcv(l, s, 0, c, q), lambda c, q: abcv(l, s, 1, c, q))
        if KSUB < 2:
            return
        wi = w_fi_d[l, f]
        widths = [256] * 11
        c0 = 0
        for w in widths:
            slab, slr = next_slab()
            load_wslab(wi[:, c0:c0 + w], 8, w, 0, slab, slr)
            load_wslab(wi[:, DFF + c0:DFF + c0 + w], 8, w, 8 * w, slab, slr)
            for ti, (a, b, segs, _) in enumerate(tiles):
                n = b - a
                for cc in range(w // 128):
                    ch = c0 // 128 + cc
                    pa, par = ps_rot()
                    pb, pbr = ps_rot()
                    for k in range(8):
                        mm(pa[:, 0:n], slab[:, k * w + cc * 128:k * w + cc * 128 + 128], col(hb, k, a, b),
                           k == 0, k == 7, [slr, hR[k][ti]], [par])
                    for k in range(8):
                        mm(pb[:, 0:n], slab[:, 8 * w + k * w + cc * 128:8 * w + k * w + cc * 128 + 128],
                           col(hb, k, a, b), k == 0, k == 7, [slr, hR[k][ti]], [pbr])
                    j = cnt["tS"] % 2
                    cnt["tS"] += 1
                    actf(tmpS[j][:, 0:n], pa[:, 0:n], AF.Silu, [par], [tmpSR[j]])
                    tt(dve, col(ub, ch, a, b), tmpS[j][:, 0:n], pb[:, 0:n], ALU.mult, [tmpSR[j], pbr], [uR[ch][ti]])
            c0 += w
        if KSUB < 3:
            return
        out_proj(w_fo_d[l, f], FC, ub, uR, tiles, lambda c, q: abcv(l, s, 2, c, q))

    ar = {"off": 0, "fence": {}}

    def arena_reset():
        ar["off"] = 0
        ar["fence"] = fence()

    def aalloc(cols, dt):
        words = cols if dt == F32 else (cols + 1) // 2
        o = ar["off"]
        ar["off"] += words
        assert ar["off"] <= ARENA_COLS, ar["off"]
        v = arena[:, o:o + words]
        if dt != F32:
            v = v.bitcast(dt)
        return v, Res(ar["fence"])

    def rotary(ps_ap, psr, n, Ct, St, tabR, cols, half, scale, out_ap, outR, tmps, split=None):
        (xs, xsR), (sw, swR), (t1, t1R) = tmps
        actf(xs[:, 0:n], ps_ap, AF.Copy, [psr], [xsR], scale=scale)
        nb = 128 // (2 * half)
        for bb in range(nb):
            p0 = bb * 2 * half
            cp(pc, sw[p0:p0 + half, 0:n], xs[p0 + half:p0 + 2 * half, 0:n], [xsR], [swR])
            cp(pc, sw[p0 + half:p0 + 2 * half, 0:n], xs[p0:p0 + half, 0:n], [xsR], [swR])
        tt(dve, t1[:, 0:n], xs[:, 0:n], Ct[:, cols[0]:cols[1]], ALU.mult, [xsR, tabR], [t1R])
        tt(pc, sw[:, 0:n], sw[:, 0:n], St[:, cols[0]:cols[1]], ALU.mult, [swR, tabR], [swR])
        if split is None:
            tt(dve, out_ap, t1[:, 0:n], sw[:, 0:n], ALU.add, [t1R, swR], [outR])
        else:
            tt(dve, split[0], t1[0:64, 0:n], sw[0:64, 0:n], ALU.add, [t1R, swR], [outR])
            tt(dve, split[1], t1[64:128, 0:n], sw[64:128, 0:n], ALU.add, [t1R, swR], [outR])

    def retention(gi, g0, T, tiles):
        l, s = 0, 1
        prenorm(tiles, lambda c, q: abcv(l, s, 0, c, q), lambda c, q: abcv(l, s, 1, c, q))
        arena_reset()
        Tp = sum(t[1] - t[0] for t in tiles if not t[3])
        npb = Tp // 128
        var = 0 if Tp == 768 else 1
        has_sample = any(t[3] for t in tiles)
        Ct, tabR = aalloc(TMAX, F32)
        St, _ = aalloc(TMAX, F32)
        ldma(Ct[:, 0:T], rotR_C_d[:, g0:g0 + T], [], [tabR])
        ldma(St[:, 0:T], rotR_S_d[:, g0:g0 + T], [], [tabR])
        rtmps = [(tmpS[0], tmpSR[0]), aalloc(512, F32), (tmpT[0], tmpTR[0])]
        qT, qTR = aalloc(TMAX, BF16)
        kT, kTR = aalloc(TMAX, BF16)
        qdT, qdTR = aalloc(TMAX, BF16)
        sg, sgR = aalloc(2 * TMAX, BF16)
        vtok, vtokR = aalloc(6 * 256, BF16)
        kdtok, kdtokR = aalloc(6 * 128, BF16)
        kdS = [aalloc(128, BF16) for _ in range(4)]
        strip, stripR = aalloc(768, F32)
        qd, qdR = aalloc(768, F32)
        DS, DSR = aalloc(128, F32)
        qdS, _ = aalloc(128, F32)
        og, ogR = aalloc(2 * 512, F32)
        PT = [aalloc(512, BF16) for _ in range(2)]
        sqg, sqgR = aalloc(2 * 512, BF16)
        stS32 = [aalloc(256, F32) for _ in range(1)]
        stSbf = [aalloc(256, BF16) for _ in range(4)]
        stO = [aalloc(256, F32) for _ in range(1)]
        ptc = 0
        for h in range(H):
            slab, slr = next_slab()
            W = 768
            load_wslab(w_ina_d[:, h * 128:(h + 1) * 128], 8, 128, 0, slab, slr)
            load_wslab(w_ina_d[:, 1024 + h * 128:1024 + (h + 1) * 128], 8, 128, 8 * 128, slab, slr)
            load_wslab(w_ina_d[:, 2048 + h * 256:2048 + (h + 1) * 256], 8, 256, 16 * 128, slab, slr)
            load_wslab(w_ina_d[:, 4096 + h * 256:4096 + (h + 1) * 256], 8, 256, 16 * 128 + 8 * 256, slab, slr)
            OQ, OK_, OV, OG = 0, 1024, 2048, 2048 + 2048
            ldma(strip[:], strip_d[h], [], [stripR])
            ldma(qd[:], qd_d[h], [], [qdR])
            if has_sample:
                ldma(DS[:], DS_d[h], [], [DSR])
                ldma(qdS[:], qdS_d[h], [], [DSR])
                for s4 in range(4):
                    wdma(stSbf[s4][0][:], st_in_d[s4, h], [], [stSbf[s4][1]])
            for ti, (a, b, segs, smp) in enumerate(tiles):
                n = b - a
                pq, pqr = ps_rot()
                for k in range(8):
                    mm(pq[:, 0:n], slab[:, OQ + k * 128:OQ + k * 128 + 128], col(hb, k, a, b), k == 0, k == 7,
                       [slr, hR[k][ti]], [pqr])
                rotary(pq[:, 0:n], pqr, n, Ct, St, tabR, (a, b), 64, 1.0, qT[:, a:b], qTR, rtmps)
                pk, pkr = ps_rot()
                for k in range(8):
                    mm(pk[:, 0:n], slab[:, OK_ + k * 128:OK_ + k * 128 + 128], col(hb, k, a, b), k == 0, k == 7,
                       [slr, hR[k][ti]], [pkr])
                rotary(pk[:, 0:n], pkr, n, Ct, St, tabR, (a, b), 64, 128 ** -0.5, kT[:, a:b], kTR, rtmps)
                for vc in range(2):
                    pg, pgr = ps_rot()
                    for k in range(8):
                        mm(pg[:, 0:n], slab[:, OG + k * 256 + vc * 128:OG + k * 256 + vc * 128 + 128],
                           col(hb, k, a, b), k == 0, k == 7, [slr, hR[k][ti]], [pgr])
                    actf(sg[:, vc * TMAX + a:vc * TMAX + b], pg[:, 0:n], AF.Silu, [pgr], [sgR])
                for tb in range(n // 128):
                    blk = (a // 128) + tb
                    ca = a + tb * 128
                    pv, pvr = ps_rot()
                    for k in range(8):
                        mm(pv[:, 0:256], col(hb, k, ca, ca + 128), slab[:, OV + k * 256:OV + k * 256 + 256],
                           k == 0, k == 7, [slr, hR[k][ti]], [pvr])
                    actf(vtok[:, blk * 256:(blk + 1) * 256], pv[:, 0:256], AF.Copy, [pvr], [vtokR])
                    pt, ptr = ps_rot()
                    mm(pt[:, 0:128], kT[:, ca:ca + 128], ident[:], True, True, [kTR, constR], [ptr])
                    if not smp:
                        actf(kdtok[:, blk * 128:(blk + 1) * 128], pt[:, 0:128], AF.Identity, [ptr, smallR], [kdtokR],
                             scale=kd_sb[:, (h * 2 + var) * 6 + blk:(h * 2 + var) * 6 + blk + 1])
                    else:
                        for s4 in range(4):
                            actf(kdS[s4][0][:], pt[:, 0:128], AF.Identity, [ptr, smallR], [kdS[s4][1]],
                                 scale=kdS_sb[:, h * 4 + s4:h * 4 + s4 + 1])
            for ti, (a, b, segs, smp) in enumerate(tiles):
                if smp:
                    tt(dve, qdT[:, a:b], qT[:, a:b], qdS[:], ALU.mult, [qTR, DSR], [qdTR])
                elif gi > 0:
                    tt(dve, qdT[:, a:b], qT[:, a:b], qd[:, a:b], ALU.mult, [qTR, qdR], [qdTR])
            for ti, (a, b, segs, smp) in enumerate(tiles):
                n = b - a
                po = [(PS[4], psR[4]), (PS[5], psR[5])]
                if not smp:
                    nkb = b // 128
                    rslots = {}

                    def RA(kb):
                        nonlocal ptc
                        c0 = max(a, kb * 128)
                        w_ = b - c0
                        ps_, psr_ = ps_rot()
                        mm(ps_[:, 0:w_], kT[:, kb * 128:(kb + 1) * 128], qT[:, c0:b], True, True, [kTR, qTR], [psr_])
                        pt_, ptr_ = PT[ptc % 2]
                        ptc += 1
                        tt(dve, pt_[:, 0:w_], ps_[:, 0:w_], strip[:, c0 - kb * 128:b - kb * 128], ALU.mult,
                           [psr_, stripR], [ptr_])
                        rslots[kb] = (pt_, ptr_, c0, w_)

                    def RB(kb):
                        pt_, ptr_, c0, w_ = rslots.pop(kb)
                        last = (kb == nkb - 1) and gi == 0
                        for vc in range(2):
                            mm(po[vc][0][:, c0 - a:n], vtok[:, kb * 256 + vc * 128:kb * 256 + vc * 128 + 128],
                               pt_[:, 0:w_], kb == 0, last, [vtokR, ptr_], [po[vc][1]], force=(vc == 1))

                    RA(0)
                    for kb in range(nkb):
                        if kb + 1 < nkb:
                            RA(kb + 1)
                        RB(kb)
                    if gi > 0:
                        for vc in range(2):
                            mm(po[vc][0][:, 0:n], stbf[:, h * 256 + vc * 128:h * 256 + vc * 128 + 128], qdT[:, a:b],
                               False, True, [stR[h], qdTR], [po[vc][1]])
                else:
                    blk = a // 128
                    ps_, psr_ = ps_rot()
                    mm(ps_[:, 0:128], kT[:, a:b], qT[:, a:b], True, True, [kTR, qTR], [psr_])
                    pt_, ptr_ = PT[ptc % 2]
                    ptc += 1
                    tt(dve, pt_[:, 0:128], ps_[:, 0:128], DS[:], ALU.mult, [psr_, DSR], [ptr_])
                    for vc in range(2):
                        mm(po[vc][0][:, 0:128], vtok[:, blk * 256 + vc * 128:blk * 256 + vc * 128 + 128],
                           pt_[:, 0:128], True, False, [vtokR, ptr_], [po[vc][1]])
                    for vc in range(2):
                        for s4 in range(4):
                            mm(po[vc][0][:, 32 * s4:32 * s4 + 32], stSbf[s4][0][:, vc * 128:vc * 128 + 128],
                               qdT[:, a + 32 * s4:a + 32 * s4 + 32], False, s4 == 3, [stSbf[s4][1], qdTR],
                               [po[vc][1]])
                for vc in range(2):
                    actf(sqg[:, vc * 512:vc * 512 + n], po[vc][0][:, 0:n], AF.Square, [po[vc][1]], [sqgR])
                    actf(og[:, vc * 512:vc * 512 + n], po[vc][0][:, 0:n], AF.Identity, [po[vc][1], smallR], [ogR],
                         scale=g_gn[:, h * 2 + vc:h * 2 + vc + 1])
                    tt(dve, og[:, vc * 512:vc * 512 + n], og[:, vc * 512:vc * 512 + n],
                       sg[:, vc * TMAX + a:vc * TMAX + b], ALU.mult, [sgR], [ogR])
                for vc in range(2):
                    mm(PS[7][:, 0:n], ones[:], sqg[:, vc * 512:vc * 512 + n], vc == 0, vc == 1, [constR, sqgR], [psR[7]])
                rt, rr = rstd_from(PS[7][:, 0:n], psR[7], n, 256)
                for vc in range(2):
                    tt(pc, col(ub, h * 2 + vc, a, b), og[:, vc * 512:vc * 512 + n], rt[:, 0:n], ALU.mult,
                       [ogR, rr], [uR[h * 2 + vc][ti]])
            pst, pstr = PS[6], psR[6]
            for blk in range(npb):
                mm(pst[:, 0:256], kdtok[:, blk * 128:(blk + 1) * 128], vtok[:, blk * 256:(blk + 1) * 256],
                   blk == 0, blk == npb - 1, [kdtokR, vtokR], [pstr])
            if gi == 0:
                cp(dve, st32[:, h * 256:(h + 1) * 256], pst[:, 0:256], [pstr], [stR[h]])
            else:
                ts(dve, st32[:, h * 256:(h + 1) * 256], st32[:, h * 256:(h + 1) * 256], 1.0,
                   rc_sb[:, h * 3 + var:h * 3 + var + 1], ALU.mult, ALU.mult, [smallR], [stR[h]])
                tt(dve, st32[:, h * 256:(h + 1) * 256], st32[:, h * 256:(h + 1) * 256], pst[:, 0:256], ALU.add,
                   [pstr], [stR[h]])
            actf(stbf[:, h * 256:(h + 1) * 256], st32[:, h * 256:(h + 1) * 256], AF.Copy, [stR[h]], [stR[h]])
            if gi == len(GROUPS) - 1:
                ldma(stp_o[h], st32[:, h * 256:(h + 1) * 256], [stR[h]], [])
            if has_sample:
                blk = npb
                for s4 in range(4):
                    mm(pst[:, 0:256], kdS[s4][0][:], vtok[:, blk * 256:(blk + 1) * 256], True, True,
                       [kdS[s4][1], vtokR], [pstr])
                    so, sor = stO[0]
                    ldma(stS32[0][0][:], st_in_d[s4, h], [], [stS32[0][1]])
                    ts(dve, so[:], stS32[0][0][:], 1.0, rc_sb[:, h * 3 + 2:h * 3 + 3], ALU.mult, ALU.mult,
                       [smallR, stS32[0][1]], [sor])
                    tt(dve, so[:], so[:], pst[:, 0:256], ALU.add, [pstr], [sor])
                    ldma(sts_o[s4, h], so[:], [sor], [])
        out_proj(w_outa_d, 16, ub, uR, tiles, lambda c, q: abcv(l, s, 2, c, q))

    def shared_kv(gi, g0, T, tiles):
        prenorm(tiles, lambda c, q: kvab[:, c * 5 + q:c * 5 + q + 1], lambda c, q: kvab[:, 40 + c * 5 + q:41 + c * 5 + q])
        arena_reset()
        Ct, tabR = aalloc(TMAX, F32)
        St, _ = aalloc(TMAX, F32)
        ldma(Ct[:, 0:T], rotD_C_d[:, g0:g0 + T], [], [tabR])
        ldma(St[:, 0:T], rotD_S_d[:, g0:g0 + T], [], [tabR])
        rtmps = [(tmpS[0], tmpSR[0]), aalloc(512, F32), (tmpT[0], tmpTR[0])]
        kf = [aalloc(512, F32) for _ in range(2)]
        vf = [aalloc(512, F32) for _ in range(2)]
        kc = 0
        for j in range(2):
            slab, slr = next_slab()
            load_wslab(w_kv_d[:, j * 512:(j + 1) * 512], 8, 512, 0, slab, slr)
            for ti, (a, b, segs, smp) in enumerate(tiles):
                n = b - a
                for hh in range(4):
                    hd = j * 4 + hh
                    pk, pkr = ps_rot()
                    for k in range(8):
                        mm(pk[:, 0:n], slab[:, k * 512 + hh * 128:k * 512 + hh * 128 + 128], col(hb, k, a, b),
                           k == 0, k == 7, [slr, hR[k][ti]], [pkr])
                    ko, kor = kf[kc % 2]
                    kc += 1
                    rotary(pk[:, 0:n], pkr, n, Ct, St, tabR, (a, b), 32, 1.0, ko[:, 0:n], kor, rtmps)
                    ldma(kT_o[hd * 128:(hd + 1) * 128, g0 + a:g0 + b], ko[:, 0:n], [kor], [kT_oR[hd][gi]])
        vcn = 0
        for j in range(2):
            slab, slr = next_slab()
            load_wslab(w_kv_d[:, 1024 + j * 512:1024 + (j + 1) * 512], 8, 512, 0, slab, slr)
            for ti, (a, b, segs, smp) in enumerate(tiles):
                n = b - a
                for tb in range(n // 128):
                    ca = a + tb * 128
                    pv, pvr = ps_rot()
                    for k in range(8):
                        mm(pv[:, 0:512], col(hb, k, ca, ca + 128), slab[:, k * 512:(k + 1) * 512], k == 0, k == 7,
                           [slr, hR[k][ti]], [pvr])
                    vo, vor = vf[vcn % 2]
                    vcn += 1
                    actf(vo[:], pv[:, 0:512], AF.Copy, [pvr], [vor])
                    ldma(v_o[g0 + ca:g0 + ca + 128, j * 512:(j + 1) * 512], vo[:], [vor], [v_oR[gi]])

    def diff_attn(gi, g0, T, tiles):
        l, s = 1, 1
        SC = 64 ** -0.5
        prenorm(tiles, lambda c, q: abcv(l, s, 0, c, q), lambda c, q: abcv(l, s, 1, c, q))
        arena_reset()
        Tp = sum(t[1] - t[0] for t in tiles if not t[3])
        nkeys = g0 + Tp
        nkb_all = nkeys // 128
        Ct, tabR = aalloc(TMAX, F32)
        St, _ = aalloc(TMAX, F32)
        ldma(Ct[:, 0:T], rotD_C_d[:, g0:g0 + T], [], [tabR])
        ldma(St[:, 0:T], rotD_S_d[:, g0:g0 + T], [], [tabR])
        rtmps = [(tmpS[0], tmpSR[0]), aalloc(512, F32), (tmpT[0], tmpTR[0])]
        KT, KTR = aalloc(2048, BF16)
        Vs, VsR = aalloc(16 * 128, BF16)
        qT, qTR = aalloc(1024, BF16)
        dve.op(lambda e: e.memset(qT[:], 0.0), [], [qTR])
        E = [aalloc(512, BF16) for _ in range(4)]
        r_sb, r_R = aalloc(512, F32)
        a0, a0R = aalloc(256, F32)
        a1, a1R = aalloc(256, F32)
        o_sb, o_R = a0, a0R
        sqo, sqoR = aalloc(256, BF16)
        has_sample = any(t[3] for t in tiles)
        if has_sample:
            KCb = [aalloc(2048, BF16) for _ in range(2)]
            VCb = [aalloc(16 * 128, BF16) for _ in range(2)]
            KN, KNR = aalloc(32, BF16)
            VN, VNR = aalloc(128, BF16)
        st8 = {"ec": 0, "acc": 0, "ld": 0}
        pending = []
        acc_sets = [((PS[4], psR[4]), (PS[5], psR[5])), ((PS[6], psR[6]), (PS[3], psR[3]))]
        rot_n[0] = 3

        def run_pending():
            while pending:
                pending.pop(0)()

        def pipeline(A, B):
            n_ = len(A)
            DEPTH = 2
            for j in range(min(DEPTH, n_)):
                A[j]()
            for i in range(n_):
                if i + DEPTH < n_:
                    A[i + DEPTH]()
                if i == min(1, n_ - 1):
                    run_pending()
                B[i]()

        def fin1(acc_o, aoR, acc_s, asR, nq):
            dve.op(lambda e: e.reciprocal(out=r_sb[:, 0:2 * nq], in_=acc_s[:, 0:2 * nq]), [asR], [r_R])
            tt(dve, a0[:, 0:nq], acc_o[:, 0:nq], r_sb[:, 0:nq], ALU.mult, [aoR, r_R], [a0R])
            actf(a1[:, 0:nq], acc_o[:, nq:2 * nq], AF.Identity, [aoR, lamR], [a1R], scale=neg_lam[:, 0:1])
            tt(dve, a1[:, 0:nq], a1[:, 0:nq], r_sb[:, nq:2 * nq], ALU.mult, [r_R], [a1R])
            tt(dve, o_sb[:, 0:nq], a0[:, 0:nq], a1[:, 0:nq], ALU.add, [a1R], [o_R])
            actf(sqo[:, 0:nq], o_sb[:, 0:nq], AF.Square, [o_R], [sqoR])

        def fin2(h, nq, a_, b_, ti):
            mm(PS[7][:, 0:nq], ones[:], sqo[:, 0:nq], True, True, [constR, sqoR], [psR[7]])
            rt, rr = rstd_from(PS[7][:, 0:nq], psR[7], nq, 128)
            ts(dve, o_sb[:, 0:nq], o_sb[:, 0:nq], 1.0, g_sub2[:, 0:1], ALU.mult, ALU.mult, [lamR], [o_R])
            tt(dve, col(ub, h, a_, b_), o_sb[:, 0:nq], rt[:, 0:nq], ALU.mult, [o_R, rr], [uR[h][ti]])

        def prompt_subtile(h, ti, a, qa, nq):
            gq0 = g0 + a + qa
            kb_last = (gq0 + nq - 1) // 128
            (acc_o, aoR), (acc_s, asR) = acc_sets[st8["acc"] % 2]
            st8["acc"] += 1
            slots = {}

            def A(kb):
                c0 = max(0, kb * 128 - gq0)
                lg, lgr = ps_rot()
                for t in range(2):
                    mm(lg[:, t * nq + c0:(t + 1) * nq], KT[:, kb * 128:(kb + 1) * 128],
                       qT[:, t * 512 + qa + c0:t * 512 + qa + nq], True, True, [KTR, qTR], [lgr])
                Et, EtR = E[st8["ec"] % 4]
                st8["ec"] += 1
                slots[kb] = (Et, EtR)
                if c0 == 0:
                    actf(Et[:, 0:2 * nq], lg[:, 0:2 * nq], AF.Exp, [lgr], [EtR], scale=SC)
                else:
                    for t in range(2):
                        actf(Et[:, t * nq + c0:(t + 1) * nq], lg[:, t * nq + c0:(t + 1) * nq], AF.Exp,
                             [lgr], [EtR], scale=SC)
                if kb * 128 >= gq0:
                    for t in range(2):
                        x0 = t * nq + c0
                        dve.op(lambda e, Et=Et, x0=x0: e.memset(Et[64:128, x0:x0 + 64], 0.0), [], [EtR])

            def B(kb):
                c0 = max(0, kb * 128 - gq0)
                Et, EtR = slots.pop(kb)
                if c0 == 0:
                    mm(acc_o[:, 0:2 * nq], Vs[:, kb * 128:(kb + 1) * 128], Et[:, 0:2 * nq],
                       kb == 0, kb == kb_last, [VsR, EtR], [aoR])
                    mm(acc_s[:, 0:2 * nq], ones[:], Et[:, 0:2 * nq],
                       kb == 0, kb == kb_last, [constR, EtR], [asR], force=True)
                else:
                    for t in range(2):
                        mm(acc_o[:, t * nq + c0:(t + 1) * nq], Vs[:, kb * 128:(kb + 1) * 128],
                           Et[:, t * nq + c0:(t + 1) * nq], False, kb == kb_last and t == 1, [VsR, EtR], [aoR])
                    for t in range(2):
                        mm(acc_s[:, t * nq + c0:(t + 1) * nq], ones[:], Et[:, t * nq + c0:(t + 1) * nq],
                           False, kb == kb_last and t == 1, [constR, EtR], [asR], force=(t == 1))

            kbs = list(range(kb_last + 1))
            pipeline([(lambda kb=kb: A(kb)) for kb in kbs], [(lambda kb=kb: B(kb)) for kb in kbs])
            fin1(acc_o, aoR, acc_s, asR, nq)
            pending.append(lambda: fin2(h, nq, a + qa, a + qa + nq, ti))

        def sample_loads(h, step):
            if step >= 8:
                h, step = h + 1, step - 8
            if h >= H:
                return
            s4, half = step // 2, step % 2
            bi = step % 2
            KCh, KChR = KCb[bi]
            VCh, VChR = VCb[bi]
            wdma(KCh[:], kcT_d[s4, h][:, half * 2048:(half + 1) * 2048], [], [KChR])
            for b0 in range(0, 16, 8):
                r0 = half * 2048 + b0 * 128
                wdma(VCh[:, b0 * 128:(b0 + 8) * 128].rearrange("p (b v) -> p b v", b=8),
                     vc_d[s4, r0:r0 + 1024, h * 128:(h + 1) * 128].rearrange("(b p) v -> p b v", p=128),
                     [], [VChR])

        def sample_seq(h, ti, a, s4):
            nq = 32
            qc = 32 * s4
            (acc_o, aoR), (acc_s, asR) = acc_sets[st8["acc"] % 2]
            st8["acc"] += 1
            gc = NTOK - 128 + 32 * s4
            wdma(KN[:], kT_o[h * 128:(h + 1) * 128, gc:gc + 32], [kT_oR[h][gi]], [KNR])
            wdma(VN[0:32, :], v_o[gc:gc + 32, h * 128:(h + 1) * 128], [v_oR[gi]], [VNR])
            slots = {}

            def A(g):
                if g < 4:
                    half = g // 2
                    step = s4 * 2 + half
                    KCh, KChR = KCb[step % 2]
                    lg, lgr = ps_rot()
                    for kbi in range(8):
                        kbl = (g % 2) * 8 + kbi
                        for t in range(2):
                            mm(lg[:, (kbi * 2 + t) * 32:(kbi * 2 + t) * 32 + 32], KCh[:, kbl * 128:(kbl + 1) * 128],
                               qT[:, t * 512 + qc:t * 512 + qc + 32], True, True, [KChR, qTR], [lgr])
                    Et, EtR = E[st8["ec"] % 4]
                    st8["ec"] += 1
                    slots[g] = (Et, EtR)
                    actf(Et[:, 0:512], lg[:, 0:512], AF.Exp, [lgr], [EtR], scale=SC)
                else:
                    lg, lgr = ps_rot()
                    for t in range(2):
                        mm(lg[0:32, t * 32:t * 32 + 32], KN[:, 0:32], qT[:, t * 512 + qc:t * 512 + qc + 32],
                           True, True, [KNR, qTR], [lgr])
                    Et, EtR = E[st8["ec"] % 4]
                    st8["ec"] += 1
                    slots[g] = (Et, EtR)
                    actf(Et[0:32, 0:64], lg[0:32, 0:64], AF.Exp, [lgr], [EtR], scale=SC)

            def B(g):
                Et, EtR = slots.pop(g)
                if g < 4:
                    step = s4 * 2 + g // 2
                    VCh, VChR = VCb[step % 2]
                    for kbi in range(8):
                        kbl = (g % 2) * 8 + kbi
                        mm(acc_o[:, 0:64], VCh[:, kbl * 128:(kbl + 1) * 128], Et[:, kbi * 64:kbi * 64 + 64],
                           g == 0 and kbi == 0, False, [VChR, EtR], [aoR])
                    for kbi in range(8):
                        mm(acc_s[:, 0:64], ones[:], Et[:, kbi * 64:kbi * 64 + 64], g == 0 and kbi == 0, False,
                           [constR, EtR], [asR], force=(kbi == 7))
                    if g % 2 == 1:
                        sample_loads(h, step + 2)
                else:
                    mm(acc_o[:, 0:64], VN[0:32, :], Et[0:32, 0:64], False, True, [VNR, EtR], [aoR])
                    mm(acc_s[:, 0:64], ones[0:32, :], Et[0:32, 0:64], False, True, [constR, EtR], [asR])

            gs = list(range(5))
            pipeline([(lambda g=g: A(g)) for g in gs], [(lambda g=g: B(g)) for g in gs])
            fin1(acc_o, aoR, acc_s, asR, nq)
            pending.append(lambda: fin2(h, nq, a + qc, a + qc + 32, ti))

        KR = 9
        for h in range(H):
            slab, slr = next_slab()
            load_wslab(w_q_d[:, h * 128:(h + 1) * 128], 8, 128, 0, slab, slr)
            kdeps = [kT_oR[h][g] for g in range(gi + 1)]
            vdeps = [v_oR[g] for g in range(gi + 1)]
            wdma(KT[:, 0:nkeys], kT_o[h * 128:(h + 1) * 128, 0:nkeys], kdeps, [KTR])
            for b0 in range(0, nkb_all, 8):
                b1 = min(nkb_all, b0 + 8)
                wdma(Vs[:, b0 * 128:b1 * 128].rearrange("p (b v) -> p b v", b=b1 - b0),
                     v_o[b0 * 128:b1 * 128, h * 128:(h + 1) * 128].rearrange("(b p) v -> p b v", p=128), vdeps, [VsR])
            if has_sample and h == 0:
                sample_loads(0, 0)
                sample_loads(0, 1)
            for ti, (a, b, segs, smp) in enumerate(tiles):
                n = b - a
                pq, pqr = ps_rot()
                for k in range(8):
                    mm(pq[:, 0:n], slab[:, k * 128:(k + 1) * 128], col(hb, k, a, b), k == 0, k == 7,
                       [slr, hR[k][ti]], [pqr])
                rotary(pq[:, 0:n], pqr, n, Ct, St, tabR, (a, b), 32, 1.0, None, qTR, rtmps,
                       split=(qT[0:64, 0:n], qT[64:128, 512:512 + n]))
                if not smp:
                    for qa in range(0, n, 256):
                        prompt_subtile(h, ti, a, qa, 256)
                else:
                    for s4 in range(4):
                        sample_seq(h, ti, a, s4)
        run_pending()
        rot_n[0] = 4
        if KR < 9:
            return
        out_proj(w_outb_d, 8, ub, uR, tiles, lambda c, q: abcv(l, s, 2, c, q))

    ident = sb("ident", 128, BF16)
    ident_d = din("ident", [128, 128])
    wdma(ident[:], ident_d, [], [constR])

    if os.environ.get("KNOSAMPLE"):
        GROUPS[2] = (1536, 512, [(0, 512, [(0, 512, 0)], False)])
    STAGE = int(os.environ.get("KSTAGE", "99"))
    NG = int(os.environ.get("KGROUPS", "3"))
    for gi, (g0, T, tiles) in enumerate(GROUPS[:NG]):
        for c in range(DC):
            for ti, (a, b, segs, _) in enumerate(tiles):
                ldma(col(xT, c, a, b), xT_d[c * 128:(c + 1) * 128, g0 + a:g0 + b], [], [xR[c][ti]])
        if STAGE >= 2:
            ffn(0, 0, tiles)
        if STAGE >= 3:
            retention(gi, g0, T, tiles)
        if STAGE >= 4:
            ffn(0, 1, tiles)
        if STAGE >= 5:
            shared_kv(gi, g0, T, tiles)
        if STAGE >= 6:
            ffn(1, 0, tiles)
        if STAGE >= 7:
            diff_attn(gi, g0, T, tiles)
        if STAGE >= 8:
            ffn(1, 1, tiles)
        for c in range(DC):
            for ti, (a, b, segs, _) in enumerate(tiles):
                ldma(yT_d[c * 128:(c + 1) * 128, g0 + a:g0 + b], col(xT, c, a, b), [xR[c][ti]], [])

    fin = {}
    for e in (pool, sp):
        for s_, c_ in zip(e.dsems, e.dcnt):
            if c_ > 0:
                fin[id(s_)] = (s_, c_)
    sp._wait(fin)

    with nc.Block() as block:
        @block.tensor
        def _(e):
            pe.run(e)

        @block.scalar
        def _(e):
            act.run(e)

        @block.vector
        def _(e):
            dve.run(e)

        @block.gpsimd
        def _(e):
            pool.run(e)

        @block.sync
        def _(e):
            sp.run(e)
    es.close()
    return nc


def _const_tables():
    lg = np.log1p(-np.exp2(-5.0 - np.arange(H, dtype=np.float64)))
    pos = np.concatenate([np.arange(2048, dtype=np.float64)] + [PASTN + np.arange(32, dtype=np.float64)] * 4)

    def rot(dh):
        half = dh // 2
        inv = np.power(10000.0, -np.arange(0, dh, 2, dtype=np.float32) / np.float32(dh)).astype(np.float32)
        ang = (pos.astype(np.float32)[None, :] * inv[:, None]).astype(np.float32)
        cos, sin = np.cos(ang).astype(np.float32), np.sin(ang).astype(np.float32)
        C = np.zeros((128, NTOK), np.float32)
        S = np.zeros((128, NTOK), np.float32)
        for p in range(128):
            dd = p % dh
            f = dd % half
            C[p] = cos[f]
            S[p] = -sin[f] if dd < half else sin[f]
        return C, S

    RC, RS = rot(128)
    DCc, DSs = rot(64)
    jl = np.arange(128)[:, None].astype(np.float64)
    dl = np.arange(768)[None, :].astype(np.float64)
    strip = np.zeros((H, 128, 768), np.float32)
    qd = np.zeros((H, 128, 768), np.float32)
    DS = np.zeros((H, 128, 128), np.float32)
    qdS = np.zeros((H, 128, 128), np.float32)
    kd = np.zeros((128, H * 2 * 6), np.float32)
    kdS = np.zeros((128, H * 4), np.float32)
    rc = np.zeros((128, H * 3), np.float32)
    for h in range(H):
        g = lg[h]
        st = np.exp(g * (dl - jl))
        il = np.arange(128)[None, :].astype(np.float64)
        diag = np.exp(g * np.abs(il - jl))
        diag[(jl // 64) > (il // 64) * np.ones_like(jl)] = 0.0
        st[:, 0:128] = diag
        strip[h] = st.astype(np.float32)
        qd[h] = np.broadcast_to(np.exp(g * dl), (128, 768)).astype(np.float32)
        m = np.exp(g * np.abs(il - jl))
        m[(jl // 32) != (il // 32) * np.ones_like(jl)] = 0.0
        DS[h] = m.astype(np.float32)
        qdS[h] = np.broadcast_to(np.exp(g * (np.arange(128) % 32))[None, :], (128, 128)).astype(np.float32)
        for var, Tp in enumerate((768, 512)):
            for blk in range(6):
                j = blk * 128 + np.arange(128)
                kd[:, (h * 2 + var) * 6 + blk] = np.exp(g * (Tp - j)).astype(np.float32)
        for s4 in range(4):
            p = np.arange(128)
            v = np.exp(g * (32 - (p % 32)))
            v[(p // 32) != s4] = 0.0
            kdS[:, h * 4 + s4] = v.astype(np.float32)
        rc[:, h * 3 + 0] = np.float32(np.exp(g * 768))
        rc[:, h * 3 + 1] = np.float32(np.exp(g * 512))
        rc[:, h * 3 + 2] = np.float32(np.exp(g * 32))
    return dict(rotR_C=RC, rotR_S=RS, rotD_C=DCc, rotD_S=DSs, ret_strip=strip, ret_qd=qd, ret_kd=kd, ret_DS=DS,
                ret_qdS=qdS, ret_kdS=kdS, ret_c=rc, ident=np.eye(128, dtype=np.float32))


def _fm(v, nch):
    v = np.asarray(v, np.float32)
    lead = v.shape[:-1]
    r = v.reshape(lead + (nch, 128))
    r = np.moveaxis(r, -1, 0)
    return np.ascontiguousarray(r.reshape(128, -1))


_NC_CACHE = {}


def kernel(x_prompt, x_sample, state_ret, cache_k, cache_v, c_prompt, c_sample,
           w_ada, b_ada, g_norm, w_ffn_in, w_ffn_out, w_in_a, g_gn_a, w_out_a,
           w_ada_kv, b_ada_kv, g_kv, w_kv, w_q_b, lam_b, g_subln_b, w_out_b):
    in_maps = make_in_maps(x_prompt, x_sample, state_ret, cache_k, cache_v, c_prompt, c_sample,
                           w_ada, b_ada, g_norm, w_ffn_in, w_ffn_out, w_in_a, g_gn_a, w_out_a,
                           w_ada_kv, b_ada_kv, g_kv, w_kv, w_q_b, lam_b, g_subln_b, w_out_b)
    n = 8
    if "nc" not in _NC_CACHE:
        _NC_CACHE["nc"] = build_program()
    nc = _NC_CACHE["nc"]
    res = run_bass_kernel_spmd(nc, in_maps, core_ids=list(range(n)))
    return assemble(res.results)


def make_in_maps(x_prompt, x_sample, state_ret, cache_k, cache_v, c_prompt, c_sample,
                 w_ada, b_ada, g_norm, w_ffn_in, w_ffn_out, w_in_a, g_gn_a, w_out_a,
                 w_ada_kv, b_ada_kv, g_kv, w_kv, w_q_b, lam_b, g_subln_b, w_out_b, cores=range(8)):
    f = lambda a: np.ascontiguousarray(np.asarray(a, np.float32))
    shared = dict(
        w_ada=f(w_ada), b_adaT=_fm(b_ada, 72), g_normT=_fm(g_norm, 8), w_ffn_in=f(w_ffn_in), w_ffn_out=f(w_ffn_out),
        w_in_a=f(w_in_a[0]), g_gnT=_fm(g_gn_a[0], 16), w_out_a=f(w_out_a[0]), w_ada_kv=f(w_ada_kv),
        b_kvT=_fm(b_ada_kv, 16), g_kvT=_fm(g_kv, 8), w_kv=f(w_kv), w_q_b=f(w_q_b[0]),
        lam_bb=np.ascontiguousarray(np.broadcast_to(np.asarray(lam_b[0], np.float32).reshape(1, 256), (128, 256))),
        g_subT=f(np.asarray(g_subln_b[0]).reshape(128, 1)), w_out_b=f(w_out_b[0]),
    )
    shared.update(_const_tables())
    x_prompt = np.asarray(x_prompt, np.float32)
    x_sample = np.asarray(x_sample, np.float32)
    cache_k = np.asarray(cache_k, np.float32)
    cache_v = np.asarray(cache_v, np.float32)
    state_ret = np.asarray(state_ret, np.float32)
    in_maps = []
    for i in cores:
        xs = x_sample[4 * i:4 * i + 4].reshape(128, D)
        xT = np.ascontiguousarray(np.concatenate([x_prompt[i], xs], axis=0).T)
        c5 = np.concatenate([np.asarray(c_prompt, np.float32)[i:i + 1], np.asarray(c_sample, np.float32)[4 * i:4 * i + 4]], 0)
        cT = np.ascontiguousarray(c5.reshape(5, 8, 128).transpose(2, 1, 0).reshape(128, 40))
        kcT = np.ascontiguousarray(cache_k[4 * i:4 * i + 4].reshape(4, PASTN, H, 128).transpose(0, 2, 3, 1))
        m = dict(shared)
        m.update(xT=xT, cT=cT, st_in=np.ascontiguousarray(state_ret[0, 4 * i:4 * i + 4]), kcT=kcT,
                 vc=np.ascontiguousarray(cache_v[4 * i:4 * i + 4].reshape(4, PASTN, D)))
        in_maps.append(m)
    return in_maps


def assemble(R):
    n = 8
    y_p = np.stack([R[i]["yT"][:, :2048].T for i in range(n)])
    y_s = np.concatenate([R[i]["yT"][:, 2048:].T.reshape(4, 32, D) for i in range(n)])
    st_p = np.stack([R[i]["st_p"] for i in range(n)])[None]
    k_p = np.stack([R[i]["kT_out"][:, :2048].T.reshape(2048, 16, 64) for i in range(n)])
    v_p = np.stack([R[i]["v_out"][:2048].reshape(2048, 8, 128) for i in range(n)])
    st_s = np.concatenate([R[i]["st_s"] for i in range(n)])[None]
    k_s = np.concatenate([R[i]["kT_out"][:, 2048:].T.reshape(4, 32, 16, 64) for i in range(n)])
    v_s = np.concatenate([R[i]["v_out"][2048:].reshape(4, 32, 8, 128) for i in range(n)])
    out = (y_p, y_s, st_p, k_p, v_p, st_s, k_s, v_s)
    return tuple(np.ascontiguousarray(o, dtype=np.float32) for o in out)
```

```python
import math
import os
import contextlib
import numpy as np
import concourse.bass as bass
import concourse.mybir as mybir
from concourse.bass_utils import run_bass_kernel_spmd

F32 = mybir.dt.float32
BF16 = mybir.dt.bfloat16
ALU = mybir.AluOpType
AF = mybir.ActivationFunctionType
AX = mybir.AxisListType

D = 1024
DC = 8
DFF = 2816
FC = 22
NSEQ = 5
NTOK = 2176
TMAX = 768
EPS = 1e-6
H = 8
LAM_INIT = 0.8 - 0.6 * math.exp(-0.3 * 1)
PASTN = 4096

GROUPS = [
    (0, 768, [(0, 512, [(0, 512, 0)], False), (512, 768, [(512, 768, 0)], False)]),
    (768, 768, [(0, 512, [(0, 512, 0)], False), (512, 768, [(512, 768, 0)], False)]),
    (1536, 640, [(0, 512, [(0, 512, 0)], False),
                 (512, 640, [(512 + 32 * s, 544 + 32 * s, 1 + s) for s in range(4)], True)]),
]


def _merge(d, src):
    for k, v in src.items():
        o = d.get(k)
        if o is None or o[1] < v[1]:
            d[k] = v


class Res:
    __slots__ = ("W", "R")

    def __init__(self, init=None):
        self.W = dict(init) if init else {}
        self.R = {}


class Eng:
    def __init__(self, name, sem, inorder=False, dsems=()):
        self.name = name
        self.sem = sem
        self.n = 0
        self.seen = {}
        self.prog = []
        self.inorder = inorder
        self.dsems = list(dsems)
        self.dcnt = [0] * len(self.dsems)
        self.di = 0

    def _wait(self, deps):
        for key, (sem, val) in deps.items():
            if self.inorder and sem is self.sem:
                continue
            if self.seen.get(key, 0) >= val:
                continue
            self.seen[key] = val
            self.prog.append(("wait", sem, val))

    def op(self, fn, reads=(), writes=(), mark=True):
        deps = {}
        for r in reads:
            _merge(deps, r.W)
        for w in writes:
            _merge(deps, w.W)
            _merge(deps, w.R)
        self._wait(deps)
        tk = (self.sem, self.n + 1)
        if mark:
            self.n += 1
        self.prog.append(("op", fn, mark))
        k = id(self.sem)
        for r in reads:
            o = r.R.get(k)
            if o is None or o[1] < tk[1]:
                r.R[k] = tk
        for w in writes:
            w.W = {k: tk}
            w.R = {}

    def dma(self, out_ap, in_ap, reads=(), writes=()):
        i = self.di % len(self.dsems)
        self.di += 1
        sem = self.dsems[i]
        prev = self.dcnt[i]
        deps = {}
        for r in reads:
            _merge(deps, r.W)
        for w in writes:
            _merge(deps, w.W)
            _merge(deps, w.R)
        if prev > 0:
            deps[id(sem)] = (sem, prev)
        self._wait(deps)
        self.dcnt[i] = prev + 16
        self.prog.append(("dma", out_ap, in_ap, sem))
        tk = (sem, prev + 16)
        k = id(sem)
        for r in reads:
            o = r.R.get(k)
            if o is None or o[1] < tk[1]:
                r.R[k] = tk
        for w in writes:
            w.W[k] = tk
            w.R = {}

    def run(self, e):
        for it in self.prog:
            if it[0] == "wait":
                e.wait_ge(it[1], it[2])
            elif it[0] == "op":
                ins = it[1](e)
                if it[2]:
                    ins.then_inc(self.sem, 1)
            else:
                e.dma_start(out=it[1], in_=it[2]).then_inc(it[3], 16)


class K:
    pass


def build_program():
    nc = bass.Bass("TRN2", target_bir_lowering=False)
    es = contextlib.ExitStack()

    def din(name, shape, dt=F32):
        return nc.dram_tensor(name, list(shape), dt, kind="ExternalInput").ap()

    def dout(name, shape):
        return nc.dram_tensor(name, list(shape), F32, kind="ExternalOutput").ap()

    xT_d = din("xT", [D, NTOK])
    cT_d = din("cT", [128, DC * NSEQ])
    st_in_d = din("st_in", [4, H, 128, 256])
    kcT_d = din("kcT", [4, H, 128, PASTN])
    vc_d = din("vc", [4, H, 128, 32 * 128])
    w_ada_d = din("w_ada", [2, D, 9 * D])
    b_ada_d = din("b_adaT", [128, 2 * 72])
    g_norm_d = din("g_normT", [128, 2 * 6 * 8])
    w_fi_d = din("w_ffn_in", [2, 2, D, 2 * DFF])
    w_fo_d = din("w_ffn_out", [2, 2, DFF, D])
    w_ina_d = din("w_in_a", [D, 6144])
    g_gn_d = din("g_gnT", [128, 16])
    w_outa_d = din("w_out_a", [2048, D])
    w_adakv_d = din("w_ada_kv", [D, 2048])
    b_kv_d = din("b_kvT", [128, 16])
    g_kv_d = din("g_kvT", [128, 8])
    w_kv_d = din("w_kv", [D, 2048])
    w_q_d = din("w_q_b", [D, D])
    lam_d = din("lam_bb", [128, 256])
    g_sub_d = din("g_subT", [128, 1])
    w_outb_d = din("w_out_b", [D, D])
    rotR_C_d = din("rotR_C", [128, NTOK])
    rotR_S_d = din("rotR_S", [128, NTOK])
    rotD_C_d = din("rotD_C", [128, NTOK])
    rotD_S_d = din("rotD_S", [128, NTOK])
    strip_d = din("ret_strip", [H, 128, 768])
    qd_d = din("ret_qd", [H, 128, 768])
    kd_d = din("ret_kd", [128, H * 2 * 6])
    DS_d = din("ret_DS", [H, 128, 128])
    qdS_d = din("ret_qdS", [H, 128, 128])
    kdS_d = din("ret_kdS", [128, H * 4])
    rc_d = din("ret_c", [128, H * 3])

    yT_d = dout("yT", [D, NTOK])
    kT_o = dout("kT_out", [D, NTOK])
    v_o = dout("v_out", [NTOK, D])
    stp_o = dout("st_p", [H, 128, 256])
    sts_o = dout("st_s", [4, H, 128, 256])

    def sem(name):
        return es.enter_context(nc.semaphore(name))

    pe = Eng("pe", sem("s_pe"), inorder=True)
    act = Eng("act", sem("s_act"))
    dve = Eng("dve", sem("s_dve"))
    pool = Eng("pool", sem("s_pool"), dsems=[sem(f"dq_p{i}") for i in range(8)])
    sp = Eng("sp", sem("s_sp"), dsems=[sem(f"dq_s{i}") for i in range(8)])
    engines = [pe, act, dve, pool, sp]
    pc = pool if os.environ.get('KPOOL') else dve

    def fence():
        d = {}
        for e in engines:
            if e.n > 0:
                d[id(e.sem)] = (e.sem, e.n)
            for s_, c_ in zip(e.dsems, e.dcnt):
                if c_ > 0:
                    d[id(s_)] = (s_, c_)
        return d

    def sb(name, cols, dt):
        return es.enter_context(nc.sbuf_tensor("sb_" + name, [128, cols], dt))

    xT = sb("xT", DC * TMAX, F32)
    hb = sb("hb", DC * TMAX, BF16)
    ub = sb("ub", FC * TMAX, BF16)
    osb = sb("osb", DC * TMAX, F32)
    slabs = [sb(f"slab{i}", 6144, BF16) for i in range(3)]
    tmpS = [sb(f"tmpS{i}", 512, F32) for i in range(2)]
    tmpT = [sb(f"tmpT{i}", 512, F32) for i in range(2)]
    rs_t = sb("rs_t", 512, F32)
    rstd_t = [sb(f"rstd{i}", 512, F32) for i in range(1)]
    ones = sb("ones", 128, BF16)
    modT = sb("modT", 2 * 360, F32)
    kvmod = sb("kvmod", 80, F32)
    abc = sb("abc", 2 * 3 * 3 * 40, F32)
    kvab = sb("kvab", 80, F32)
    b_ada = sb("b_ada", 144, F32)
    g_norm = sb("g_norm", 96, F32)
    g_gn = sb("g_gn", 16, F32)
    b_kv = sb("b_kv", 16, F32)
    g_kv = sb("g_kv", 8, F32)
    g_sub = sb("g_sub", 1, F32)
    g_sub2 = sb("g_sub2", 1, F32)
    lam_sb = sb("lam_sb", 256, F32)
    lam_t = sb("lam_t", 128, F32)
    lam_r = sb("lam_r", 4, F32)
    neg_lam = sb("neg_lam", 1, F32)
    c_sb = sb("c_sb", 40, F32)
    scT = sb("scT", 40, BF16)
    kd_sb = sb("kd_sb", H * 12, F32)
    kdS_sb = sb("kdS_sb", H * 4, F32)
    rc_sb = sb("rc_sb", H * 3, F32)
    st32 = sb("st32", H * 256, F32)
    stbf = sb("stbf", H * 256, BF16)
    ARENA_COLS = 11160
    arena = sb("arena", ARENA_COLS, F32)

    PS = [es.enter_context(nc.psum_tensor(f"ps{i}", [128, 512], F32)) for i in range(8)]
    psR = [Res() for _ in range(8)]
    rot_state = [0]
    rot_n = [4]

    def ps_rot():
        i = rot_state[0] % rot_n[0]
        rot_state[0] += 1
        return PS[i], psR[i]

    xR = [[Res() for _ in range(2)] for _ in range(DC)]
    hR = [[Res() for _ in range(2)] for _ in range(DC)]
    uR = [[Res() for _ in range(2)] for _ in range(FC)]
    oR = [[Res() for _ in range(2)] for _ in range(DC)]
    slR = [Res() for _ in range(3)]
    tmpSR = [Res() for _ in range(2)]
    tmpTR = [Res() for _ in range(2)]
    rsR = Res()
    rstdR = [Res() for _ in range(1)]
    constR = Res()
    modR = Res()
    stR = [Res() for _ in range(H)]
    kT_oR = [[Res() for _ in range(3)] for _ in range(H)]
    v_oR = [Res() for _ in range(3)]
    cnt = {"slab": 0, "tS": 0, "tT": 0, "rstd": 0}

    def next_slab():
        i = cnt["slab"] % 3
        cnt["slab"] += 1
        return slabs[i], slR[i]

    def col(buf, c, a, b):
        return buf[:, c * TMAX + a: c * TMAX + b]

    def mm(ps_ap, lhsT, rhs, start, stop, reads, writes, force=False):
        pe.op(lambda e: e.matmul(ps_ap, lhsT, rhs, start=start, stop=stop), reads, writes, mark=(stop or force))

    def actf(out, in_, func, reads, writes, scale=None, bias=None):
        kw = {}
        if scale is not None:
            kw["scale"] = scale
        if bias is not None:
            kw["bias"] = bias
        act.op(lambda e: e.activation(out=out, in_=in_, func=func, **kw), reads, writes)

    def tt(eng, out, in0, in1, op, reads, writes):
        eng.op(lambda e: e.tensor_tensor(out=out, in0=in0, in1=in1, op=op), reads, writes)

    def ts(eng, out, in0, s1, s2, op0, op1, reads, writes):
        eng.op(lambda e: e.tensor_scalar(out=out, in0=in0, scalar1=s1, scalar2=s2, op0=op0, op1=op1), reads, writes)

    def stt(out, in0, scalar, in1, op0, op1, reads, writes):
        dve.op(lambda e: e.scalar_tensor_tensor(out=out, in0=in0, scalar=scalar, in1=in1, op0=op0, op1=op1),
               reads, writes)

    def cp(eng, out, in_, reads, writes):
        eng.op(lambda e: e.tensor_copy(out=out, in_=in_), reads, writes)

    def wdma(out_ap, in_ap, reads, writes):
        pool.dma(out_ap, in_ap, reads, writes)

    def ldma(out_ap, in_ap, reads, writes):
        sp.dma(out_ap, in_ap, reads, writes)

    def load_wslab(src2d, kch, ncols, off=0, slab=None, slr=None):
        for k0 in range(0, kch, 8):
            k1 = min(kch, k0 + 8)
            dst = slab[:, off + k0 * ncols: off + k1 * ncols].rearrange("p (k n) -> p k n", k=k1 - k0)
            wdma(dst, src2d[k0 * 128:k1 * 128, :].rearrange("(k p) n -> p k n", p=128), [], [slr])

    dve.op(lambda e: e.memset(ones[:], 1.0), [], [constR])
    smallR = Res()
    for dst, src in ((b_ada, b_ada_d), (g_norm, g_norm_d), (g_gn, g_gn_d), (b_kv, b_kv_d), (g_kv, g_kv_d),
                     (g_sub, g_sub_d), (lam_sb, lam_d), (c_sb, cT_d), (kd_sb, kd_d), (kdS_sb, kdS_d),
                     (rc_sb, rc_d)):
        ldma(dst[:], src, [], [smallR])
    lamR = Res()
    tt(dve, lam_t[:, 0:64], lam_sb[:, 0:64], lam_sb[:, 64:128], ALU.mult, [smallR], [lamR])
    tt(dve, lam_t[:, 64:128], lam_sb[:, 128:192], lam_sb[:, 192:256], ALU.mult, [smallR], [lamR])
    dve.op(lambda e: e.reduce_sum(out=lam_r[:, 0:1], in_=lam_t[:, 0:64], axis=AX.X), [lamR], [lamR])
    dve.op(lambda e: e.reduce_sum(out=lam_r[:, 1:2], in_=lam_t[:, 64:128], axis=AX.X), [lamR], [lamR])
    actf(lam_r[:, 2:4], lam_r[:, 0:2], AF.Exp, [lamR], [lamR])
    tt(dve, neg_lam[:], lam_r[:, 3:4], lam_r[:, 2:3], ALU.subtract, [lamR], [lamR])
    ts(dve, neg_lam[:], neg_lam[:], -LAM_INIT, None, ALU.add, ALU.bypass, [lamR], [lamR])
    ts(dve, g_sub2[:], g_sub[:], 1.0 - LAM_INIT, None, ALU.mult, ALU.bypass, [smallR], [lamR])
    actf(scT[:], c_sb[:], AF.Silu, [smallR], [modR])

    def ada_mm(wsrc, ncols_total, psb, psr):
        nsl = ncols_total // 512
        for j in range(nsl):
            slab, slr = next_slab()
            load_wslab(wsrc[:, j * 512:(j + 1) * 512], 8, 512, 0, slab, slr)
            for nn in range(4):
                n = j * 4 + nn
                for k in range(8):
                    mm(psb[:, n * 5:(n + 1) * 5], slab[:, k * 512 + nn * 128: k * 512 + nn * 128 + 128],
                       scT[:, k * 5:(k + 1) * 5], k == 0, k == 7, [slr, modR], [psr])

    for l in range(2):
        ada_mm(w_ada_d[l], 9 * D, PS[4 + l], psR[4 + l])
        for s in range(NSEQ):
            tt(dve, modT[:, l * 360 + s: (l + 1) * 360: 5], PS[4 + l][:, s:360:5], b_ada[:, l * 72:(l + 1) * 72],
               ALU.add, [psR[4 + l], smallR], [modR])
    ada_mm(w_adakv_d, 2048, PS[6], psR[6])
    for s in range(NSEQ):
        tt(dve, kvmod[:, s:80:5], PS[6][:, s:80:5], b_kv[:, 0:16], ALU.add, [psR[6], smallR], [modR])

    def modv(l, m, c):
        o = l * 360 + (m * 8 + c) * 5
        return modT[:, o:o + 5]

    def abcv(l, s, kind, c, seq=None):
        o = (((l * 3 + s) * 3 + kind) * 8 + c) * 5
        if seq is None:
            return abc[:, o:o + 5]
        return abc[:, o + seq:o + seq + 1]

    abcR = Res()
    for l in range(2):
        for s in range(3):
            half = 1.0 if s == 1 else 0.5
            for c in range(DC):
                gpre = g_norm[:, (l * 6 + 2 * s) * 8 + c:(l * 6 + 2 * s) * 8 + c + 1]
                gpost = g_norm[:, (l * 6 + 2 * s + 1) * 8 + c:(l * 6 + 2 * s + 1) * 8 + c + 1]
                ts(dve, abcv(l, s, 0, c), modv(l, 3 * s + 1, c), 1.0, gpre, ALU.add, ALU.mult, [modR, smallR], [abcR])
                cp(dve, abcv(l, s, 1, c), modv(l, 3 * s, c), [modR], [abcR])
                ts(dve, abcv(l, s, 2, c), modv(l, 3 * s + 2, c), half, gpost, ALU.mult, ALU.mult, [modR, smallR], [abcR])
    for c in range(DC):
        ts(dve, kvab[:, c * 5:(c + 1) * 5], kvmod[:, (8 + c) * 5:(9 + c) * 5], 1.0, g_kv[:, c:c + 1], ALU.add, ALU.mult,
           [modR, smallR], [abcR])
        cp(dve, kvab[:, 40 + c * 5:45 + c * 5], kvmod[:, c * 5:(c + 1) * 5], [modR], [abcR])

    def rstd_from(ps_ap, psr, n, dim):
        i = 0
        ts(dve, rs_t[:, 0:n], ps_ap, 1.0 / dim, EPS, ALU.mult, ALU.add, [psr], [rsR])
        actf(rs_t[:, 0:n], rs_t[:, 0:n], AF.Ln, [rsR], [rsR])
        actf(rstd_t[i][:, 0:n], rs_t[:, 0:n], AF.Exp, [rsR], [rstdR[i]], scale=-0.5)
        return rstd_t[i], rstdR[i]

    def prenorm(tiles, Afn, Bfn):
        for ti, (a, b, segs, _) in enumerate(tiles):
            n = b - a
            for c in range(DC):
                actf(col(ub, c, a, b), col(xT, c, a, b), AF.Square, [xR[c][ti]], [uR[c][ti]])
            for c in range(DC):
                mm(PS[7][:, 0:n], ones[:], col(ub, c, a, b), c == 0, c == DC - 1, [constR, uR[c][ti]], [psR[7]])
            rt, rr = rstd_from(PS[7][:, 0:n], psR[7], n, D)
            for c in range(DC):
                j = cnt["tT"] % 2
                cnt["tT"] += 1
                tt(dve, tmpT[j][:, 0:n], col(xT, c, a, b), rt[:, 0:n], ALU.mult, [xR[c][ti], rr], [tmpTR[j]])
                for (sa, sb_, seq) in segs:
                    ts(pc, col(hb, c, sa, sb_), tmpT[j][:, sa - a:sb_ - a], Afn(c, seq), Bfn(c, seq),
                       ALU.mult, ALU.add, [tmpTR[j], abcR], [hR[c][ti]])

    def post_chunk(po, por, c, ti, tile, Cfn):
        a, b, segs, _ = tile
        n = b - a
        KSUB = int(os.environ.get("KSUB", "9"))
        if KSUB < 4:
            return
        actf(col(hb, c, a, b), po[:, 0:n], AF.Square, [por], [hR[c][ti]])
        if KSUB < 5:
            return
        for (sa, sb_, seq) in segs:
            actf(col(osb, c, sa, sb_), po[:, sa - a:sb_ - a], AF.Identity, [por, abcR], [oR[c][ti]], scale=Cfn(c, seq))

    def post_final(tiles):
        for ti, (a, b, segs, _) in enumerate(tiles):
            n = b - a
            for c in range(DC):
                mm(PS[7][:, 0:n], ones[:], col(hb, c, a, b), c == 0, c == DC - 1, [constR, hR[c][ti]], [psR[7]])
            rt, rr = rstd_from(PS[7][:, 0:n], psR[7], n, D)
            for c in range(DC):
                j = cnt["tT"] % 2
                cnt["tT"] += 1
                tt(dve, tmpT[j][:, 0:n], col(osb, c, a, b), rt[:, 0:n], ALU.mult, [oR[c][ti], rr], [tmpTR[j]])
                tt(pc, col(xT, c, a, b), col(xT, c, a, b), tmpT[j][:, 0:n], ALU.add, [tmpTR[j]], [xR[c][ti]])

    def out_proj(wsrc, kch, src_buf, srcR, tiles, Cfn):
        for ns in range(4):
            slab, slr = next_slab()
            load_wslab(wsrc[:, ns * 256:(ns + 1) * 256], kch, 256, 0, slab, slr)
            for ti, tile in enumerate(tiles):
                a, b = tile[0], tile[1]
                n = b - a
                for nn in range(2):
                    c = ns * 2 + nn
                    po, por = ps_rot()
                    for k in range(kch):
                        mm(po[:, 0:n], slab[:, k * 256 + nn * 128:k * 256 + nn * 128 + 128], col(src_buf, k, a, b),
                           k == 0, k == kch - 1, [slr, srcR[k][ti]], [por])
                    post_chunk(po, por, c, ti, tile, Cfn)
        if int(os.environ.get("KSUB", "9")) < 6:
            return
        post_final(tiles)

    def ffn(l, f, tiles):
        s = 0 if f == 0 else 2
        KSUB = int(os.environ.get("KSUB", "9"))
        prenorm(tiles, lambda c, q: abcv(l, s, 0, c, q), lambda c, q: abcv(l, s, 1, c, q))
        if KSUB < 2:
            return
        wi = w_fi_d[l, f]
        widths = [256] * 11
        c0 = 0
        for w in widths:
            slab, slr = next_slab()
            load_wslab(wi[:, c0:c0 + w], 8, w, 0, slab, slr)
            load_wslab(wi[:, DFF + c0:DFF + c0 + w], 8, w, 8 * w, slab, slr)
            for ti, (a, b, segs, _) in enumerate(tiles):
                n = b - a
                for cc in range(w // 128):
                    ch = c0 // 128 + cc
                    pa, par = ps_rot()
                    pb, pbr = ps_rot()
                    for k in range(8):
                        mm(pa[:, 0:n], slab[:, k * w + cc * 128:k * w + cc * 128 + 128], col(hb, k, a, b),
                           k == 0, k == 7, [slr, hR[k][ti]], [par])
                    for k in range(8):
                        mm(pb[:, 0:n], slab[:, 8 * w + k * w + cc * 128:8 * w + k * w + cc * 128 + 128],
                           col(hb, k, a, b), k == 0, k == 7, [slr, hR[k][ti]], [pbr])
                    j = cnt["tS"] % 2
                    cnt["tS"] += 1
                    actf(tmpS[j][:, 0:n], pa[:, 0:n], AF.Silu, [par], [tmpSR[j]])
                    tt(dve, col(ub, ch, a, b), tmpS[j][:, 0:n], pb[:, 0:n], ALU.mult, [tmpSR[j], pbr], [uR[ch][ti]])
            c0 += w
        if KSUB < 3:
            return
        out_proj(w_fo_d[l, f], FC, ub, uR, tiles, lambda c, q: abcv(l, s, 2, c, q))

    ar = {"off": 0, "fence": {}}

    def arena_reset():
        ar["off"] = 0
        ar["fence"] = fence()

    def aalloc(cols, dt):
        words = cols if dt == F32 else (cols + 1) // 2
        o = ar["off"]
        ar["off"] += words
        assert ar["off"] <= ARENA_COLS, ar["off"]
        v = arena[:, o:o + words]
        if dt != F32:
            v = v.bitcast(dt)
        return v, Res(ar["fence"])

    def rotary(ps_ap, psr, n, Ct, St, tabR, cols, half, scale, out_ap, outR, tmps, split=None):
        (xs, xsR), (sw, swR), (t1, t1R) = tmps
        actf(xs[:, 0:n], ps_ap, AF.Copy, [psr], [xsR], scale=scale)
        nb = 128 // (2 * half)
        for bb in range(nb):
            p0 = bb * 2 * half
            cp(pc, sw[p0:p0 + half, 0:n], xs[p0 + half:p0 + 2 * half, 0:n], [xsR], [swR])
            cp(pc, sw[p0 + half:p0 + 2 * half, 0:n], xs[p0:p0 + half, 0:n], [xsR], [swR])
        tt(dve, t1[:, 0:n], xs[:, 0:n], Ct[:, cols[0]:cols[1]], ALU.mult, [xsR, tabR], [t1R])
        tt(pc, sw[:, 0:n], sw[:, 0:n], St[:, cols[0]:cols[1]], ALU.mult, [swR, tabR], [swR])
        if split is None:
            tt(dve, out_ap, t1[:, 0:n], sw[:, 0:n], ALU.add, [t1R, swR], [outR])
        else:
            tt(dve, split[0], t1[0:64, 0:n], sw[0:64, 0:n], ALU.add, [t1R, swR], [outR])
            tt(dve, split[1], t1[64:128, 0:n], sw[64:128, 0:n], ALU.add, [t1R, swR], [outR])

    def retention(gi, g0, T, tiles):
        l, s = 0, 1
        prenorm(tiles, lambda c, q: abcv(l, s, 0, c, q), lambda c, q: abcv(l, s, 1, c, q))
        arena_reset()
        Tp = sum(t[1] - t[0] for t in tiles if not t[3])
        npb = Tp // 128
        var = 0 if Tp == 768 else 1
        has_sample = any(t[3] for t in tiles)
        Ct, tabR = aalloc(TMAX, F32)
        St, _ = aalloc(TMAX, F32)
        ldma(Ct[:, 0:T], rotR_C_d[:, g0:g0 + T], [], [tabR])
        ldma(St[:, 0:T], rotR_S_d[:, g0:g0 + T], [], [tabR])
        rtmps = [(tmpS[0], tmpSR[0]), aalloc(512, F32), (tmpT[0], tmpTR[0])]
        qT, qTR = aalloc(TMAX, BF16)
        kT, kTR = aalloc(TMAX, BF16)
        qdT, qdTR = aalloc(TMAX, BF16)
        sg, sgR = aalloc(2 * TMAX, BF16)
        vtok, vtokR = aalloc(6 * 256, BF16)
        kdtok, kdtokR = aalloc(6 * 128, BF16)
        kdS = [aalloc(128, BF16) for _ in range(4)]
        strip, stripR = aalloc(768, F32)
        qd, qdR = aalloc(768, F32)
        DS, DSR = aalloc(128, F32)
        qdS, _ = aalloc(128, F32)
        og, ogR = aalloc(2 * 512, F32)
        PT = [aalloc(512, BF16) for _ in range(2)]
        sqg, sqgR = aalloc(2 * 512, BF16)
        stS32 = [aalloc(256, F32) for _ in range(1)]
        stSbf = [aalloc(256, BF16) for _ in range(4)]
        stO = [aalloc(256, F32) for _ in range(1)]
        ptc = 0
        for h in range(H):
            slab, slr = next_slab()
            W = 768
            load_wslab(w_ina_d[:, h * 128:(h + 1) * 128], 8, 128, 0, slab, slr)
            load_wslab(w_ina_d[:, 1024 + h * 128:1024 + (h + 1) * 128], 8, 128, 8 * 128, slab, slr)
            load_wslab(w_ina_d[:, 2048 + h * 256:2048 + (h + 1) * 256], 8, 256, 16 * 128, slab, slr)
            load_wslab(w_ina_d[:, 4096 + h * 256:4096 + (h + 1) * 256], 8, 256, 16 * 128 + 8 * 256, slab, slr)
            OQ, OK_, OV, OG = 0, 1024, 2048, 2048 + 2048
            ldma(strip[:], strip_d[h], [], [stripR])
            ldma(qd[:], qd_d[h], [], [qdR])
            if has_sample:
                ldma(DS[:], DS_d[h], [], [DSR])
                ldma(qdS[:], qdS_d[h], [], [DSR])
                for s4 in range(4):
                    wdma(stSbf[s4][0][:], st_in_d[s4, h], [], [stSbf[s4][1]])
            for ti, (a, b, segs, smp) in enumerate(tiles):
                n = b - a
                pq, pqr = ps_rot()
                for k in range(8):
                    mm(pq[:, 0:n], slab[:, OQ + k * 128:OQ + k * 128 + 128], col(hb, k, a, b), k == 0, k == 7,
                       [slr, hR[k][ti]], [pqr])
                rotary(pq[:, 0:n], pqr, n, Ct, St, tabR, (a, b), 64, 1.0, qT[:, a:b], qTR, rtmps)
                pk, pkr = ps_rot()
                for k in range(8):
                    mm(pk[:, 0:n], slab[:, OK_ + k * 128:OK_ + k * 128 + 128], col(hb, k, a, b), k == 0, k == 7,
                       [slr, hR[k][ti]], [pkr])
                rotary(pk[:, 0:n], pkr, n, Ct, St, tabR, (a, b), 64, 128 ** -0.5, kT[:, a:b], kTR, rtmps)
                for vc in range(2):
                    pg, pgr = ps_rot()
                    for k in range(8):
                        mm(pg[:, 0:n], slab[:, OG + k * 256 + vc * 128:OG + k * 256 + vc * 128 + 128],
                           col(hb, k, a, b), k == 0, k == 7, [slr, hR[k][ti]], [pgr])
                    actf(sg[:, vc * TMAX + a:vc * TMAX + b], pg[:, 0:n], AF.Silu, [pgr], [sgR])
                for tb in range(n // 128):
                    blk = (a // 128) + tb
                    ca = a + tb * 128
                    pv, pvr = ps_rot()
                    for k in range(8):
                        mm(pv[:, 0:256], col(hb, k, ca, ca + 128), slab[:, OV + k * 256:OV + k * 256 + 256],
                           k == 0, k == 7, [slr, hR[k][ti]], [pvr])
                    actf(vtok[:, blk * 256:(blk + 1) * 256], pv[:, 0:256], AF.Copy, [pvr], [vtokR])
                    pt, ptr = ps_rot()
                    mm(pt[:, 0:128], kT[:, ca:ca + 128], ident[:], True, True, [kTR, constR], [ptr])
                    if not smp:
                        actf(kdtok[:, blk * 128:(blk + 1) * 128], pt[:, 0:128], AF.Identity, [ptr, smallR], [kdtokR],
                             scale=kd_sb[:, (h * 2 + var) * 6 + blk:(h * 2 + var) * 6 + blk + 1])
                    else:
                        for s4 in range(4):
                            actf(kdS[s4][0][:], pt[:, 0:128], AF.Identity, [ptr, smallR], [kdS[s4][1]],
                                 scale=kdS_sb[:, h * 4 + s4:h * 4 + s4 + 1])
            for ti, (a, b, segs, smp) in enumerate(tiles):
                if smp:
                    tt(dve, qdT[:, a:b], qT[:, a:b], qdS[:], ALU.mult, [qTR, DSR], [qdTR])
                elif gi > 0:
                    tt(dve, qdT[:, a:b], qT[:, a:b], qd[:, a:b], ALU.mult, [qTR, qdR], [qdTR])
            for ti, (a, b, segs, smp) in enumerate(tiles):
                n = b - a
                po = [(PS[4], psR[4]), (PS[5], psR[5])]
                if not smp:
                    nkb = b // 128
                    rslots = {}

                    def RA(kb):
                        nonlocal ptc
                        c0 = max(a, kb * 128)
                        w_ = b - c0
                        ps_, psr_ = ps_rot()
                        mm(ps_[:, 0:w_], kT[:, kb * 128:(kb + 1) * 128], qT[:, c0:b], True, True, [kTR, qTR], [psr_])
                        pt_, ptr_ = PT[ptc % 2]
                        ptc += 1
                        tt(dve, pt_[:, 0:w_], ps_[:, 0:w_], strip[:, c0 - kb * 128:b - kb * 128], ALU.mult,
                           [psr_, stripR], [ptr_])
                        rslots[kb] = (pt_, ptr_, c0, w_)

                    def RB(kb):
                        pt_, ptr_, c0, w_ = rslots.pop(kb)
                        last = (kb == nkb - 1) and gi == 0
                        for vc in range(2):
                            mm(po[vc][0][:, c0 - a:n], vtok[:, kb * 256 + vc * 128:kb * 256 + vc * 128 + 128],
                               pt_[:, 0:w_], kb == 0, last, [vtokR, ptr_], [po[vc][1]], force=(vc == 1))

                    RA(0)
                    for kb in range(nkb):
                        if kb + 1 < nkb:
                            RA(kb + 1)
                        RB(kb)
                    if gi > 0:
                        for vc in range(2):
                            mm(po[vc][0][:, 0:n], stbf[:, h * 256 + vc * 128:h * 256 + vc * 128 + 128], qdT[:, a:b],
                               False, True, [stR[h], qdTR], [po[vc][1]])
                else:
                    blk = a // 128
                    ps_, psr_ = ps_rot()
                    mm(ps_[:, 0:128], kT[:, a:b], qT[:, a:b], True, True, [kTR, qTR], [psr_])
                    pt_, ptr_ = PT[ptc % 2]
                    ptc += 1
                    tt(dve, pt_[:, 0:128], ps_[:, 0:128], DS[:], ALU.mult, [psr_, DSR], [ptr_])
                    for vc in range(2):
                        mm(po[vc][0][:, 0:128], vtok[:, blk * 256 + vc * 128:blk * 256 + vc * 128 + 128],
                           pt_[:, 0:128], True, False, [vtokR, ptr_], [po[vc][1]])
                    for vc in range(2):
                        for s4 in range(4):
                            mm(po[vc][0][:, 32 * s4:32 * s4 + 32], stSbf[s4][0][:, vc * 128:vc * 128 + 128],
                               qdT[:, a + 32 * s4:a + 32 * s4 + 32], False, s4 == 3, [stSbf[s4][1], qdTR],
                               [po[vc][1]])
                for vc in range(2):
                    actf(sqg[:, vc * 512:vc * 512 + n], po[vc][0][:, 0:n], AF.Square, [po[vc][1]], [sqgR])
                    actf(og[:, vc * 512:vc * 512 + n], po[vc][0][:, 0:n], AF.Identity, [po[vc][1], smallR], [ogR],
                         scale=g_gn[:, h * 2 + vc:h * 2 + vc + 1])
                    tt(dve, og[:, vc * 512:vc * 512 + n], og[:, vc * 512:vc * 512 + n],
                       sg[:, vc * TMAX + a:vc * TMAX + b], ALU.mult, [sgR], [ogR])
                for vc in range(2):
                    mm(PS[7][:, 0:n], ones[:], sqg[:, vc * 512:vc * 512 + n], vc == 0, vc == 1, [constR, sqgR], [psR[7]])
                rt, rr = rstd_from(PS[7][:, 0:n], psR[7], n, 256)
                for vc in range(2):
                    tt(pc, col(ub, h * 2 + vc, a, b), og[:, vc * 512:vc * 512 + n], rt[:, 0:n], ALU.mult,
                       [ogR, rr], [uR[h * 2 + vc][ti]])
            pst, pstr = PS[6], psR[6]
            for blk in range(npb):
                mm(pst[:, 0:256], kdtok[:, blk * 128:(blk + 1) * 128], vtok[:, blk * 256:(blk + 1) * 256],
                   blk == 0, blk == npb - 1, [kdtokR, vtokR], [pstr])
            if gi == 0:
                cp(dve, st32[:, h * 256:(h + 1) * 256], pst[:, 0:256], [pstr], [stR[h]])
            else:
                ts(dve, st32[:, h * 256:(h + 1) * 256], st32[:, h * 256:(h + 1) * 256], 1.0,
                   rc_sb[:, h * 3 + var:h * 3 + var + 1], ALU.mult, ALU.mult, [smallR], [stR[h]])
                tt(dve, st32[:, h * 256:(h + 1) * 256], st32[:, h * 256:(h + 1) * 256], pst[:, 0:256], ALU.add,
                   [pstr], [stR[h]])
            actf(stbf[:, h * 256:(h + 1) * 256], st32[:, h * 256:(h + 1) * 256], AF.Copy, [stR[h]], [stR[h]])
            if gi == len(GROUPS) - 1:
                ldma(stp_o[h], st32[:, h * 256:(h + 1) * 256], [stR[h]], [])
            if has_sample:
                blk = npb
                for s4 in range(4):
                    mm(pst[:, 0:256], kdS[s4][0][:], vtok[:, blk * 256:(blk + 1) * 256], True, True,
                       [kdS[s4][1], vtokR], [pstr])
                    so, sor = stO[0]
                    ldma(stS32[0][0][:], st_in_d[s4, h], [], [stS32[0][1]])
                    ts(dve, so[:], stS32[0][0][:], 1.0, rc_sb[:, h * 3 + 2:h * 3 + 3], ALU.mult, ALU.mult,
                       [smallR, stS32[0][1]], [sor])
                    tt(dve, so[:], so[:], pst[:, 0:256], ALU.add, [pstr], [sor])
                    ldma(sts_o[s4, h], so[:], [sor], [])
        out_proj(w_outa_d, 16, ub, uR, tiles, lambda c, q: abcv(l, s, 2, c, q))

    def shared_kv(gi, g0, T, tiles):
        prenorm(tiles, lambda c, q: kvab[:, c * 5 + q:c * 5 + q + 1], lambda c, q: kvab[:, 40 + c * 5 + q:41 + c * 5 + q])
        arena_reset()
        Ct, tabR = aalloc(TMAX, F32)
        St, _ = aalloc(TMAX, F32)
        ldma(Ct[:, 0:T], rotD_C_d[:, g0:g0 + T], [], [tabR])
        ldma(St[:, 0:T], rotD_S_d[:, g0:g0 + T], [], [tabR])
        rtmps = [(tmpS[0], tmpSR[0]), aalloc(512, F32), (tmpT[0], tmpTR[0])]
        kf = [aalloc(512, F32) for _ in range(2)]
        vf = [aalloc(512, F32) for _ in range(2)]
        kc = 0
        for j in range(2):
            slab, slr = next_slab()
            load_wslab(w_kv_d[:, j * 512:(j + 1) * 512], 8, 512, 0, slab, slr)
            for ti, (a, b, segs, smp) in enumerate(tiles):
                n = b - a
                for hh in range(4):
                    hd = j * 4 + hh
                    pk, pkr = ps_rot()
                    for k in range(8):
                        mm(pk[:, 0:n], slab[:, k * 512 + hh * 128:k * 512 + hh * 128 + 128], col(hb, k, a, b),
                           k == 0, k == 7, [slr, hR[k][ti]], [pkr])
                    ko, kor = kf[kc % 2]
                    kc += 1
                    rotary(pk[:, 0:n], pkr, n, Ct, St, tabR, (a, b), 32, 1.0, ko[:, 0:n], kor, rtmps)
                    ldma(kT_o[hd * 128:(hd + 1) * 128, g0 + a:g0 + b], ko[:, 0:n], [kor], [kT_oR[hd][gi]])
        vcn = 0
        for j in range(2):
            slab, slr = next_slab()
            load_wslab(w_kv_d[:, 1024 + j * 512:1024 + (j + 1) * 512], 8, 512, 0, slab, slr)
            for ti, (a, b, segs, smp) in enumerate(tiles):
                n = b - a
                for tb in range(n // 128):
                    ca = a + tb * 128
                    pv, pvr = ps_rot()
                    for k in range(8):
                        mm(pv[:, 0:512], col(hb, k, ca, ca + 128), slab[:, k * 512:(k + 1) * 512], k == 0, k == 7,
                           [slr, hR[k][ti]], [pvr])
                    vo, vor = vf[vcn % 2]
                    vcn += 1
                    actf(vo[:], pv[:, 0:512], AF.Copy, [pvr], [vor])
                    ldma(v_o[g0 + ca:g0 + ca + 128, j * 512:(j + 1) * 512], vo[:], [vor], [v_oR[gi]])

    def diff_attn(gi, g0, T, tiles):
        l, s = 1, 1
        SC = 64 ** -0.5
        prenorm(tiles, lambda c, q: abcv(l, s, 0, c, q), lambda c, q: abcv(l, s, 1, c, q))
        arena_reset()
        Tp = sum(t[1] - t[0] for t in tiles if not t[3])
        nkeys = g0 + Tp
        nkb_all = nkeys // 128
        Ct, tabR = aalloc(TMAX, F32)
        St, _ = aalloc(TMAX, F32)
        ldma(Ct[:, 0:T], rotD_C_d[:, g0:g0 + T], [], [tabR])
        ldma(St[:, 0:T], rotD_S_d[:, g0:g0 + T], [], [tabR])
        rtmps = [(tmpS[0], tmpSR[0]), aalloc(512, F32), (tmpT[0], tmpTR[0])]
        nb = 1 if any(t[3] for t in tiles) else 2
        KTb = [aalloc(2048, BF16) for _ in range(nb)]
        Vsb = [aalloc(16 * 128, BF16) for _ in range(nb)]
        qTb = [aalloc(1024, BF16) for _ in range(nb)]
        for qb_, qbR_ in qTb:
            dve.op(lambda e, qb_=qb_: e.memset(qb_[:], 0.0), [], [qbR_])
        KT, KTR = KTb[0]
        Vs, VsR = Vsb[0]
        qT, qTR = qTb[0]
        E = [aalloc(512, BF16) for _ in range(4)]
        r_sb, r_R = aalloc(512, F32)
        a0, a0R = aalloc(256, F32)
        a1, a1R = aalloc(256, F32)
        o_sb, o_R = a0, a0R
        sqo, sqoR = aalloc(256, BF16)
        has_sample = any(t[3] for t in tiles)
        if has_sample:
            KCb = [aalloc(2048, BF16) for _ in range(2)]
            VCb = [aalloc(16 * 128, BF16) for _ in range(2)]
            KN, KNR = aalloc(32, BF16)
            VN, VNR = aalloc(128, BF16)
        st8 = {"ec": 0, "acc": 0, "ld": 0}
        pending = []
        acc_sets = [((PS[4], psR[4]), (PS[5], psR[5])), ((PS[6], psR[6]), (PS[3], psR[3]))]
        rot_n[0] = 3

        def run_pending():
            while pending:
                pending.pop(0)()

        def pipeline(A, B):
            n_ = len(A)
            DEPTH = 2
            for j in range(min(DEPTH, n_)):
                A[j]()
            for i in range(n_):
                if i + DEPTH < n_:
                    A[i + DEPTH]()
                if i == min(1, n_ - 1):
                    run_pending()
                B[i]()

        def fin1(acc_o, aoR, acc_s, asR, nq):
            dve.op(lambda e: e.reciprocal(out=r_sb[:, 0:2 * nq], in_=acc_s[:, 0:2 * nq]), [asR], [r_R])
            tt(dve, a0[:, 0:nq], acc_o[:, 0:nq], r_sb[:, 0:nq], ALU.mult, [aoR, r_R], [a0R])
            actf(a1[:, 0:nq], acc_o[:, nq:2 * nq], AF.Identity, [aoR, lamR], [a1R], scale=neg_lam[:, 0:1])
            tt(dve, a1[:, 0:nq], a1[:, 0:nq], r_sb[:, nq:2 * nq], ALU.mult, [r_R], [a1R])
            tt(dve, o_sb[:, 0:nq], a0[:, 0:nq], a1[:, 0:nq], ALU.add, [a1R], [o_R])
            actf(sqo[:, 0:nq], o_sb[:, 0:nq], AF.Square, [o_R], [sqoR])

        def fin2(h, nq, a_, b_, ti):
            mm(PS[7][:, 0:nq], ones[:], sqo[:, 0:nq], True, True, [constR, sqoR], [psR[7]])
            rt, rr = rstd_from(PS[7][:, 0:nq], psR[7], nq, 128)
            ts(dve, o_sb[:, 0:nq], o_sb[:, 0:nq], 1.0, g_sub2[:, 0:1], ALU.mult, ALU.mult, [lamR], [o_R])
            tt(dve, col(ub, h, a_, b_), o_sb[:, 0:nq], rt[:, 0:nq], ALU.mult, [o_R, rr], [uR[h][ti]])

        def prompt_subtile(h, ti, a, qa, nq):
            gq0 = g0 + a + qa
            kb_last = (gq0 + nq - 1) // 128
            (acc_o, aoR), (acc_s, asR) = acc_sets[st8["acc"] % 2]
            st8["acc"] += 1
            slots = {}

            def A(kb):
                c0 = max(0, kb * 128 - gq0)
                lg, lgr = ps_rot()
                for t in range(2):
                    mm(lg[:, t * nq + c0:(t + 1) * nq], KT[:, kb * 128:(kb + 1) * 128],
                       qT[:, t * 512 + qa + c0:t * 512 + qa + nq], True, True, [KTR, qTR], [lgr])
                Et, EtR = E[st8["ec"] % 4]
                st8["ec"] += 1
                slots[kb] = (Et, EtR)
                if c0 == 0:
                    actf(Et[:, 0:2 * nq], lg[:, 0:2 * nq], AF.Exp, [lgr], [EtR], scale=SC)
                else:
                    for t in range(2):
                        actf(Et[:, t * nq + c0:(t + 1) * nq], lg[:, t * nq + c0:(t + 1) * nq], AF.Exp,
                             [lgr], [EtR], scale=SC)
                if kb * 128 >= gq0:
                    for t in range(2):
                        x0 = t * nq + c0
                        dve.op(lambda e, Et=Et, x0=x0: e.memset(Et[64:128, x0:x0 + 64], 0.0), [], [EtR])

            def B(kb):
                c0 = max(0, kb * 128 - gq0)
                Et, EtR = slots.pop(kb)
                if c0 == 0:
                    mm(acc_o[:, 0:2 * nq], Vs[:, kb * 128:(kb + 1) * 128], Et[:, 0:2 * nq],
                       kb == 0, kb == kb_last, [VsR, EtR], [aoR])
                    mm(acc_s[:, 0:2 * nq], ones[:], Et[:, 0:2 * nq],
                       kb == 0, kb == kb_last, [constR, EtR], [asR], force=True)
                else:
                    for t in range(2):
                        mm(acc_o[:, t * nq + c0:(t + 1) * nq], Vs[:, kb * 128:(kb + 1) * 128],
                           Et[:, t * nq + c0:(t + 1) * nq], False, kb == kb_last and t == 1, [VsR, EtR], [aoR])
                    for t in range(2):
                        mm(acc_s[:, t * nq + c0:(t + 1) * nq], ones[:], Et[:, t * nq + c0:(t + 1) * nq],
                           False, kb == kb_last and t == 1, [constR, EtR], [asR], force=(t == 1))

            kbs = list(range(kb_last + 1))
            pipeline([(lambda kb=kb: A(kb)) for kb in kbs], [(lambda kb=kb: B(kb)) for kb in kbs])
            fin1(acc_o, aoR, acc_s, asR, nq)
            pending.append(lambda: fin2(h, nq, a + qa, a + qa + nq, ti))

        def sample_loads(h, step):
            if step >= 8:
                h, step = h + 1, step - 8
            if h >= H:
                return
            s4, half = step // 2, step % 2
            bi = step % 2
            KCh, KChR = KCb[bi]
            VCh, VChR = VCb[bi]
            wdma(KCh[:], kcT_d[s4, h][:, half * 2048:(half + 1) * 2048], [], [KChR])
            wdma(VCh[:], vc_d[s4, h][:, half * 2048:(half + 1) * 2048], [], [VChR])

        def sample_seq(h, ti, a, s4):
            nq = 32
            qc = 32 * s4
            (acc_o, aoR), (acc_s, asR) = acc_sets[st8["acc"] % 2]
            st8["acc"] += 1
            gc = NTOK - 128 + 32 * s4
            wdma(KN[:], kT_o[h * 128:(h + 1) * 128, gc:gc + 32], [kT_oR[h][gi]], [KNR])
            wdma(VN[0:32, :], v_o[gc:gc + 32, h * 128:(h + 1) * 128], [v_oR[gi]], [VNR])
            slots = {}

            def A(g):
                if g < 4:
                    half = g // 2
                    step = s4 * 2 + half
                    KCh, KChR = KCb[step % 2]
                    lg, lgr = ps_rot()
                    for kbi in range(8):
                        kbl = (g % 2) * 8 + kbi
                        for t in range(2):
                            mm(lg[:, (kbi * 2 + t) * 32:(kbi * 2 + t) * 32 + 32], KCh[:, kbl * 128:(kbl + 1) * 128],
                               qT[:, t * 512 + qc:t * 512 + qc + 32], True, True, [KChR, qTR], [lgr])
                    Et, EtR = E[st8["ec"] % 4]
                    st8["ec"] += 1
                    slots[g] = (Et, EtR)
                    actf(Et[:, 0:512], lg[:, 0:512], AF.Exp, [lgr], [EtR], scale=SC)
                else:
                    lg, lgr = ps_rot()
                    for t in range(2):
                        mm(lg[0:32, t * 32:t * 32 + 32], KN[:, 0:32], qT[:, t * 512 + qc:t * 512 + qc + 32],
                           True, True, [KNR, qTR], [lgr])
                    Et, EtR = E[st8["ec"] % 4]
                    st8["ec"] += 1
                    slots[g] = (Et, EtR)
                    actf(Et[0:32, 0:64], lg[0:32, 0:64], AF.Exp, [lgr], [EtR], scale=SC)

            def B(g):
                Et, EtR = slots.pop(g)
                if g < 4:
                    step = s4 * 2 + g // 2
                    VCh, VChR = VCb[step % 2]
                    for kbi in range(8):
                        kbl = (g % 2) * 8 + kbi
                        mm(acc_o[:, 0:64], VCh[:, kbl * 128:(kbl + 1) * 128], Et[:, kbi * 64:kbi * 64 + 64],
                           g == 0 and kbi == 0, False, [VChR, EtR], [aoR])
                    for kbi in range(8):
                        mm(acc_s[:, 0:64], ones[:], Et[:, kbi * 64:kbi * 64 + 64], g == 0 and kbi == 0, False,
                           [constR, EtR], [asR], force=(kbi == 7))
                    if g % 2 == 1:
                        sample_loads(h, step + 2)
                else:
                    mm(acc_o[:, 0:64], VN[0:32, :], Et[0:32, 0:64], False, True, [VNR, EtR], [aoR])
                    mm(acc_s[:, 0:64], ones[0:32, :], Et[0:32, 0:64], False, True, [constR, EtR], [asR])

            gs = list(range(5))
            pipeline([(lambda g=g: A(g)) for g in gs], [(lambda g=g: B(g)) for g in gs])
            fin1(acc_o, aoR, acc_s, asR, nq)
            pending.append(lambda: fin2(h, nq, a + qc, a + qc + 32, ti))

        units = [(h, ti) for h in range(H) for ti in range(len(tiles))]
        hslab = {}

        def prep(ui):
            h, ti = units[ui]
            a, b, segs, smp = tiles[ti]
            n = b - a
            if ti == 0:
                hslab[h] = next_slab()
                slab, slr = hslab[h]
                load_wslab(w_q_d[:, h * 128:(h + 1) * 128], 8, 128, 0, slab, slr)
                kdeps = [kT_oR[h][g] for g in range(gi + 1)]
                vdeps = [v_oR[g] for g in range(gi + 1)]
                KT_, KTR_ = KTb[h % nb]
                Vs_, VsR_ = Vsb[h % nb]
                wdma(KT_[:, 0:nkeys], kT_o[h * 128:(h + 1) * 128, 0:nkeys], kdeps, [KTR_])
                for b0 in range(0, nkb_all, 8):
                    b1 = min(nkb_all, b0 + 8)
                    wdma(Vs_[:, b0 * 128:b1 * 128].rearrange("p (b v) -> p b v", b=b1 - b0),
                         v_o[b0 * 128:b1 * 128, h * 128:(h + 1) * 128].rearrange("(b p) v -> p b v", p=128),
                         vdeps, [VsR_])
                if has_sample and h == 0:
                    sample_loads(0, 0)
                    sample_loads(0, 1)
            slab, slr = hslab[h]
            q_, qR_ = qTb[ui % nb]
            pq, pqr = ps_rot()
            for k in range(8):
                mm(pq[:, 0:n], slab[:, k * 128:(k + 1) * 128], col(hb, k, a, b), k == 0, k == 7,
                   [slr, hR[k][ti]], [pqr])
            rotary(pq[:, 0:n], pqr, n, Ct, St, tabR, (a, b), 32, 1.0, None, qR_, rtmps,
                   split=(q_[0:64, 0:n], q_[64:128, 512:512 + n]))

        def attend(ui):
            nonlocal KT, KTR, Vs, VsR, qT, qTR
            h, ti = units[ui]
            a, b, segs, smp = tiles[ti]
            n = b - a
            KT, KTR = KTb[h % nb]
            Vs, VsR = Vsb[h % nb]
            qT, qTR = qTb[ui % nb]
            if not smp:
                for qa in range(0, n, 256):
                    prompt_subtile(h, ti, a, qa, 256)
            else:
                for s4 in range(4):
                    sample_seq(h, ti, a, s4)

        prep(0)
        for ui in range(len(units)):
            if nb == 2 and ui + 1 < len(units):
                prep(ui + 1)
            attend(ui)
            if nb == 1 and ui + 1 < len(units):
                prep(ui + 1)
        run_pending()
        rot_n[0] = 4
        out_proj(w_outb_d, 8, ub, uR, tiles, lambda c, q: abcv(l, s, 2, c, q))

    ident = sb("ident", 128, BF16)
    ident_d = din("ident", [128, 128])
    wdma(ident[:], ident_d, [], [constR])

    if os.environ.get("KNOSAMPLE"):
        GROUPS[2] = (1536, 512, [(0, 512, [(0, 512, 0)], False)])
    STAGE = int(os.environ.get("KSTAGE", "99"))
    NG = int(os.environ.get("KGROUPS", "3"))
    for gi, (g0, T, tiles) in enumerate(GROUPS[:NG]):
        for c in range(DC):
            for ti, (a, b, segs, _) in enumerate(tiles):
                ldma(col(xT, c, a, b), xT_d[c * 128:(c + 1) * 128, g0 + a:g0 + b], [], [xR[c][ti]])
        if STAGE >= 2:
            ffn(0, 0, tiles)
        if STAGE >= 3:
            retention(gi, g0, T, tiles)
        if STAGE >= 4:
            ffn(0, 1, tiles)
        if STAGE >= 5:
            shared_kv(gi, g0, T, tiles)
        if STAGE >= 6:
            ffn(1, 0, tiles)
        if STAGE >= 7:
            diff_attn(gi, g0, T, tiles)
        if STAGE >= 8:
            ffn(1, 1, tiles)
        for c in range(DC):
            for ti, (a, b, segs, _) in enumerate(tiles):
                ldma(yT_d[c * 128:(c + 1) * 128, g0 + a:g0 + b], col(xT, c, a, b), [xR[c][ti]], [])

    fin = {}
    for e in (pool, sp):
        for s_, c_ in zip(e.dsems, e.dcnt):
            if c_ > 0:
                fin[id(s_)] = (s_, c_)
    sp._wait(fin)

    with nc.Block() as block:
        @block.tensor
        def _(e):
            pe.run(e)

        @block.scalar
        def _(e):
            act.run(e)

        @block.vector
        def _(e):
            dve.run(e)

        @block.gpsimd
        def _(e):
            pool.run(e)

        @block.sync
        def _(e):
            sp.run(e)
    es.close()
    return nc


def _const_tables():
    lg = np.log1p(-np.exp2(-5.0 - np.arange(H, dtype=np.float64)))
    pos = np.concatenate([np.arange(2048, dtype=np.float64)] + [PASTN + np.arange(32, dtype=np.float64)] * 4)

    def rot(dh):
        half = dh // 2
        inv = np.power(10000.0, -np.arange(0, dh, 2, dtype=np.float32) / np.float32(dh)).astype(np.float32)
        ang = (pos.astype(np.float32)[None, :] * inv[:, None]).astype(np.float32)
        cos, sin = np.cos(ang).astype(np.float32), np.sin(ang).astype(np.float32)
        C = np.zeros((128, NTOK), np.float32)
        S = np.zeros((128, NTOK), np.float32)
        for p in range(128):
            dd = p % dh
            f = dd % half
            C[p] = cos[f]
            S[p] = -sin[f] if dd < half else sin[f]
        return C, S

    RC, RS = rot(128)
    DCc, DSs = rot(64)
    jl = np.arange(128)[:, None].astype(np.float64)
    dl = np.arange(768)[None, :].astype(np.float64)
    strip = np.zeros((H, 128, 768), np.float32)
    qd = np.zeros((H, 128, 768), np.float32)
    DS = np.zeros((H, 128, 128), np.float32)
    qdS = np.zeros((H, 128, 128), np.float32)
    kd = np.zeros((128, H * 2 * 6), np.float32)
    kdS = np.zeros((128, H * 4), np.float32)
    rc = np.zeros((128, H * 3), np.float32)
    for h in range(H):
        g = lg[h]
        st = np.exp(g * (dl - jl))
        il = np.arange(128)[None, :].astype(np.float64)
        diag = np.exp(g * np.abs(il - jl))
        diag[(jl // 64) > (il // 64) * np.ones_like(jl)] = 0.0
        st[:, 0:128] = diag
        strip[h] = st.astype(np.float32)
        qd[h] = np.broadcast_to(np.exp(g * dl), (128, 768)).astype(np.float32)
        m = np.exp(g * np.abs(il - jl))
        m[(jl // 32) != (il // 32) * np.ones_like(jl)] = 0.0
        DS[h] = m.astype(np.float32)
        qdS[h] = np.broadcast_to(np.exp(g * (np.arange(128) % 32))[None, :], (128, 128)).astype(np.float32)
        for var, Tp in enumerate((768, 512)):
            for blk in range(6):
                j = blk * 128 + np.arange(128)
                kd[:, (h * 2 + var) * 6 + blk] = np.exp(g * (Tp - j)).astype(np.float32)
        for s4 in range(4):
            p = np.arange(128)
            v = np.exp(g * (32 - (p % 32)))
            v[(p // 32) != s4] = 0.0
            kdS[:, h * 4 + s4] = v.astype(np.float32)
        rc[:, h * 3 + 0] = np.float32(np.exp(g * 768))
        rc[:, h * 3 + 1] = np.float32(np.exp(g * 512))
        rc[:, h * 3 + 2] = np.float32(np.exp(g * 32))
    return dict(rotR_C=RC, rotR_S=RS, rotD_C=DCc, rotD_S=DSs, ret_strip=strip, ret_qd=qd, ret_kd=kd, ret_DS=DS,
                ret_qdS=qdS, ret_kdS=kdS, ret_c=rc, ident=np.eye(128, dtype=np.float32))


def _fm(v, nch):
    v = np.asarray(v, np.float32)
    lead = v.shape[:-1]
    r = v.reshape(lead + (nch, 128))
    r = np.moveaxis(r, -1, 0)
    return np.ascontiguousarray(r.reshape(128, -1))


_NC_CACHE = {}


def kernel(x_prompt, x_sample, state_ret, cache_k, cache_v, c_prompt, c_sample,
           w_ada, b_ada, g_norm, w_ffn_in, w_ffn_out, w_in_a, g_gn_a, w_out_a,
           w_ada_kv, b_ada_kv, g_kv, w_kv, w_q_b, lam_b, g_subln_b, w_out_b):
    in_maps = make_in_maps(x_prompt, x_sample, state_ret, cache_k, cache_v, c_prompt, c_sample,
                           w_ada, b_ada, g_norm, w_ffn_in, w_ffn_out, w_in_a, g_gn_a, w_out_a,
                           w_ada_kv, b_ada_kv, g_kv, w_kv, w_q_b, lam_b, g_subln_b, w_out_b)
    n = 8
    if "nc" not in _NC_CACHE:
        _NC_CACHE["nc"] = build_program()
    nc = _NC_CACHE["nc"]
    res = run_bass_kernel_spmd(nc, in_maps, core_ids=list(range(n)))
    return assemble(res.results)


def make_in_maps(x_prompt, x_sample, state_ret, cache_k, cache_v, c_prompt, c_sample,
                 w_ada, b_ada, g_norm, w_ffn_in, w_ffn_out, w_in_a, g_gn_a, w_out_a,
                 w_ada_kv, b_ada_kv, g_kv, w_kv, w_q_b, lam_b, g_subln_b, w_out_b, cores=range(8)):
    f = lambda a: np.ascontiguousarray(np.asarray(a, np.float32))
    shared = dict(
        w_ada=f(w_ada), b_adaT=_fm(b_ada, 72), g_normT=_fm(g_norm, 8), w_ffn_in=f(w_ffn_in), w_ffn_out=f(w_ffn_out),
        w_in_a=f(w_in_a[0]), g_gnT=_fm(g_gn_a[0], 16), w_out_a=f(w_out_a[0]), w_ada_kv=f(w_ada_kv),
        b_kvT=_fm(b_ada_kv, 16), g_kvT=_fm(g_kv, 8), w_kv=f(w_kv), w_q_b=f(w_q_b[0]),
        lam_bb=np.ascontiguousarray(np.broadcast_to(np.asarray(lam_b[0], np.float32).reshape(1, 256), (128, 256))),
        g_subT=f(np.asarray(g_subln_b[0]).reshape(128, 1)), w_out_b=f(w_out_b[0]),
    )
    shared.update(_const_tables())
    x_prompt = np.asarray(x_prompt, np.float32)
    x_sample = np.asarray(x_sample, np.float32)
    cache_k = np.asarray(cache_k, np.float32)
    cache_v = np.asarray(cache_v, np.float32)
    state_ret = np.asarray(state_ret, np.float32)
    in_maps = []
    for i in cores:
        xs = x_sample[4 * i:4 * i + 4].reshape(128, D)
        xT = np.ascontiguousarray(np.concatenate([x_prompt[i], xs], axis=0).T)
        c5 = np.concatenate([np.asarray(c_prompt, np.float32)[i:i + 1], np.asarray(c_sample, np.float32)[4 * i:4 * i + 4]], 0)
        cT = np.ascontiguousarray(c5.reshape(5, 8, 128).transpose(2, 1, 0).reshape(128, 40))
        kcT = np.ascontiguousarray(cache_k[4 * i:4 * i + 4].reshape(4, PASTN, H, 128).transpose(0, 2, 3, 1))
        m = dict(shared)
        m.update(xT=xT, cT=cT, st_in=np.ascontiguousarray(state_ret[0, 4 * i:4 * i + 4]), kcT=kcT,
                 vc=np.ascontiguousarray(cache_v[4 * i:4 * i + 4].reshape(4, 32, 128, H, 128)
                                         .transpose(0, 3, 2, 1, 4).reshape(4, H, 128, 32 * 128)))
        in_maps.append(m)
    return in_maps


def assemble(R):
    n = 8
    y_p = np.stack([R[i]["yT"][:, :2048].T for i in range(n)])
    y_s = np.concatenate([R[i]["yT"][:, 2048:].T.reshape(4, 32, D) for i in range(n)])
    st_p = np.stack([R[i]["st_p"] for i in range(n)])[None]
    k_p = np.stack([R[i]["kT_out"][:, :2048].T.reshape(2048, 16, 64) for i in range(n)])
    v_p = np.stack([R[i]["v_out"][:2048].reshape(2048, 8, 128) for i in range(n)])
    st_s = np.concatenate([R[i]["st_s"] for i in range(n)])[None]
    k_s = np.concatenate([R[i]["kT_out"][:, 2048:].T.reshape(4, 32, 16, 64) for i in range(n)])
    v_s = np.concatenate([R[i]["v_out"][2048:].reshape(4, 32, 8, 128) for i in range(n)])
    out = (y_p, y_s, st_p, k_p, v_p, st_s, k_s, v_s)
    return tuple(np.ascontiguousarray(o, dtype=np.float32) for o in out)
```

```python
import math
import os
import contextlib
import numpy as np
import concourse.bass as bass
import concourse.mybir as mybir
from concourse.bass_utils import run_bass_kernel_spmd

F32 = mybir.dt.float32
BF16 = mybir.dt.bfloat16
ALU = mybir.AluOpType
AF = mybir.ActivationFunctionType
AX = mybir.AxisListType

D = 1024
DC = 8
DFF = 2816
FC = 22
NSEQ = 5
NTOK = 2176
TMAX = 768
EPS = 1e-6
H = 8
LAM_INIT = 0.8 - 0.6 * math.exp(-0.3 * 1)
PASTN = 4096

GROUPS = [
    (0, 768, [(0, 512, [(0, 512, 0)], False), (512, 768, [(512, 768, 0)], False)]),
    (768, 768, [(0, 512, [(0, 512, 0)], False), (512, 768, [(512, 768, 0)], False)]),
    (1536, 640, [(0, 512, [(0, 512, 0)], False),
                 (512, 640, [(512 + 32 * s, 544 + 32 * s, 1 + s) for s in range(4)], True)]),
]


def _merge(d, src):
    for k, v in src.items():
        o = d.get(k)
        if o is None or o[1] < v[1]:
            d[k] = v


class Res:
    __slots__ = ("W", "R")

    def __init__(self, init=None):
        self.W = dict(init) if init else {}
        self.R = {}


class Eng:
    def __init__(self, name, sem, inorder=False, dsems=()):
        self.name = name
        self.sem = sem
        self.n = 0
        self.seen = {}
        self.prog = []
        self.inorder = inorder
        self.dsems = list(dsems)
        self.dcnt = [0] * len(self.dsems)
        self.di = 0

    def _wait(self, deps):
        for key, (sem, val) in deps.items():
            if self.inorder and sem is self.sem:
                continue
            if self.seen.get(key, 0) >= val:
                continue
            self.seen[key] = val
            self.prog.append(("wait", sem, val))

    def op(self, fn, reads=(), writes=(), mark=True):
        deps = {}
        for r in reads:
            _merge(deps, r.W)
        for w in writes:
            _merge(deps, w.W)
            _merge(deps, w.R)
        self._wait(deps)
        tk = (self.sem, self.n + 1)
        if mark:
            self.n += 1
        self.prog.append(("op", fn, mark))
        k = id(self.sem)
        for r in reads:
            o = r.R.get(k)
            if o is None or o[1] < tk[1]:
                r.R[k] = tk
        for w in writes:
            w.W = {k: tk}
            w.R = {}

    def dma(self, out_ap, in_ap, reads=(), writes=()):
        i = self.di % len(self.dsems)
        self.di += 1
        sem = self.dsems[i]
        prev = self.dcnt[i]
        deps = {}
        for r in reads:
            _merge(deps, r.W)
        for w in writes:
            _merge(deps, w.W)
            _merge(deps, w.R)
        if prev > 0:
            deps[id(sem)] = (sem, prev)
        self._wait(deps)
        self.dcnt[i] = prev + 16
        self.prog.append(("dma", out_ap, in_ap, sem))
        tk = (sem, prev + 16)
        k = id(sem)
        for r in reads:
            o = r.R.get(k)
            if o is None or o[1] < tk[1]:
                r.R[k] = tk
        for w in writes:
            w.W[k] = tk
            w.R = {}

    def run(self, e):
        for it in self.prog:
            if it[0] == "wait":
                e.wait_ge(it[1], it[2])
            elif it[0] == "op":
                ins = it[1](e)
                if it[2]:
                    ins.then_inc(self.sem, 1)
            else:
                e.dma_start(out=it[1], in_=it[2]).then_inc(it[3], 16)


class K:
    pass


def build_program():
    nc = bass.Bass("TRN2", target_bir_lowering=False)
    es = contextlib.ExitStack()

    def din(name, shape, dt=F32):
        return nc.dram_tensor(name, list(shape), dt, kind="ExternalInput").ap()

    def dout(name, shape):
        return nc.dram_tensor(name, list(shape), F32, kind="ExternalOutput").ap()

    xT_d = din("xT", [D, NTOK])
    cT_d = din("cT", [128, DC * NSEQ])
    st_in_d = din("st_in", [4, H, 128, 256])
    kcT_d = din("kcT", [4, H, 128, PASTN])
    vc_d = din("vc", [4, H, 128, 32 * 128])
    w_ada_d = din("w_ada", [2, D, 9 * D])
    b_ada_d = din("b_adaT", [128, 2 * 72])
    g_norm_d = din("g_normT", [128, 2 * 6 * 8])
    w_fi_d = din("w_ffn_in", [2, 2, D, 2 * DFF])
    w_fo_d = din("w_ffn_out", [2, 2, DFF, D])
    w_ina_d = din("w_in_a", [D, 6144])
    g_gn_d = din("g_gnT", [128, 16])
    w_outa_d = din("w_out_a", [2048, D])
    w_adakv_d = din("w_ada_kv", [D, 2048])
    b_kv_d = din("b_kvT", [128, 16])
    g_kv_d = din("g_kvT", [128, 8])
    w_kv_d = din("w_kv", [D, 2048])
    w_q_d = din("w_q_b", [D, D])
    lam_d = din("lam_bb", [128, 256])
    g_sub_d = din("g_subT", [128, 1])
    w_outb_d = din("w_out_b", [D, D])
    rotR_C_d = din("rotR_C", [128, NTOK])
    rotR_S_d = din("rotR_S", [128, NTOK])
    rotD_C_d = din("rotD_C", [128, NTOK])
    rotD_S_d = din("rotD_S", [128, NTOK])
    strip_d = din("ret_strip", [H, 128, 768])
    qd_d = din("ret_qd", [H, 128, 768])
    kd_d = din("ret_kd", [128, H * 2 * 6])
    DS_d = din("ret_DS", [H, 128, 128])
    qdS_d = din("ret_qdS", [H, 128, 128])
    kdS_d = din("ret_kdS", [128, H * 4])
    rc_d = din("ret_c", [128, H * 3])

    yT_d = dout("yT", [D, NTOK])
    kT_o = dout("kT_out", [D, NTOK])
    v_o = dout("v_out", [NTOK, D])
    stp_o = dout("st_p", [H, 128, 256])
    sts_o = dout("st_s", [4, H, 128, 256])

    def sem(name):
        return es.enter_context(nc.semaphore(name))

    pe = Eng("pe", sem("s_pe"), inorder=True)
    act = Eng("act", sem("s_act"))
    dve = Eng("dve", sem("s_dve"))
    pool = Eng("pool", sem("s_pool"), dsems=[sem(f"dq_p{i}") for i in range(8)])
    sp = Eng("sp", sem("s_sp"), dsems=[sem(f"dq_s{i}") for i in range(8)])
    engines = [pe, act, dve, pool, sp]
    pc = pool if os.environ.get('KPOOL') else dve

    def fence():
        d = {}
        for e in engines:
            if e.n > 0:
                d[id(e.sem)] = (e.sem, e.n)
            for s_, c_ in zip(e.dsems, e.dcnt):
                if c_ > 0:
                    d[id(s_)] = (s_, c_)
        return d

    def sb(name, cols, dt):
        return es.enter_context(nc.sbuf_tensor("sb_" + name, [128, cols], dt))

    xT = sb("xT", DC * TMAX, F32)
    hb = sb("hb", DC * TMAX, BF16)
    ub = sb("ub", FC * TMAX, BF16)
    osb = sb("osb", DC * TMAX, F32)
    slabs = [sb(f"slab{i}", 6144, BF16) for i in range(3)]
    tmpS = [sb(f"tmpS{i}", 512, F32) for i in range(2)]
    tmpT = [sb(f"tmpT{i}", 512, F32) for i in range(2)]
    rs_t = sb("rs_t", 512, F32)
    rstd_t = [sb(f"rstd{i}", 512, F32) for i in range(1)]
    ones = sb("ones", 128, BF16)
    modT = sb("modT", 2 * 360, F32)
    kvmod = sb("kvmod", 80, F32)
    abc = sb("abc", 2 * 3 * 3 * 40, F32)
    kvab = sb("kvab", 80, F32)
    b_ada = sb("b_ada", 144, F32)
    g_norm = sb("g_norm", 96, F32)
    g_gn = sb("g_gn", 16, F32)
    b_kv = sb("b_kv", 16, F32)
    g_kv = sb("g_kv", 8, F32)
    g_sub = sb("g_sub", 1, F32)
    g_sub2 = sb("g_sub2", 1, F32)
    lam_sb = sb("lam_sb", 256, F32)
    lam_t = sb("lam_t", 128, F32)
    lam_r = sb("lam_r", 4, F32)
    neg_lam = sb("neg_lam", 1, F32)
    c_sb = sb("c_sb", 40, F32)
    scT = sb("scT", 40, BF16)
    kd_sb = sb("kd_sb", H * 12, F32)
    kdS_sb = sb("kdS_sb", H * 4, F32)
    rc_sb = sb("rc_sb", H * 3, F32)
    st32 = sb("st32", H * 256, F32)
    stbf = sb("stbf", H * 256, BF16)
    ARENA_COLS = 11160
    arena = sb("arena", ARENA_COLS, F32)

    PS = [es.enter_context(nc.psum_tensor(f"ps{i}", [128, 512], F32)) for i in range(8)]
    psR = [Res() for _ in range(8)]
    rot_state = [0]
    rot_n = [4]

    def ps_rot():
        i = rot_state[0] % rot_n[0]
        rot_state[0] += 1
        return PS[i], psR[i]

    xR = [[Res() for _ in range(2)] for _ in range(DC)]
    hR = [[Res() for _ in range(2)] for _ in range(DC)]
    uR = [[Res() for _ in range(2)] for _ in range(FC)]
    oR = [[Res() for _ in range(2)] for _ in range(DC)]
    slR = [Res() for _ in range(3)]
    tmpSR = [Res() for _ in range(2)]
    tmpTR = [Res() for _ in range(2)]
    rsR = Res()
    rstdR = [Res() for _ in range(1)]
    constR = Res()
    modR = Res()
    stR = [Res() for _ in range(H)]
    kT_oR = [[Res() for _ in range(3)] for _ in range(H)]
    v_oR = [Res() for _ in range(3)]
    cnt = {"slab": 0, "tS": 0, "tT": 0, "rstd": 0}

    def next_slab():
        i = cnt["slab"] % 3
        cnt["slab"] += 1
        return slabs[i], slR[i]

    def col(buf, c, a, b):
        return buf[:, c * TMAX + a: c * TMAX + b]

    def mm(ps_ap, lhsT, rhs, start, stop, reads, writes, force=False):
        pe.op(lambda e: e.matmul(ps_ap, lhsT, rhs, start=start, stop=stop), reads, writes, mark=(stop or force))

    def actf(out, in_, func, reads, writes, scale=None, bias=None):
        kw = {}
        if scale is not None:
            kw["scale"] = scale
        if bias is not None:
            kw["bias"] = bias
        act.op(lambda e: e.activation(out=out, in_=in_, func=func, **kw), reads, writes)

    def tt(eng, out, in0, in1, op, reads, writes):
        eng.op(lambda e: e.tensor_tensor(out=out, in0=in0, in1=in1, op=op), reads, writes)

    def ts(eng, out, in0, s1, s2, op0, op1, reads, writes):
        eng.op(lambda e: e.tensor_scalar(out=out, in0=in0, scalar1=s1, scalar2=s2, op0=op0, op1=op1), reads, writes)

    def stt(out, in0, scalar, in1, op0, op1, reads, writes):
        dve.op(lambda e: e.scalar_tensor_tensor(out=out, in0=in0, scalar=scalar, in1=in1, op0=op0, op1=op1),
               reads, writes)

    def cp(eng, out, in_, reads, writes):
        eng.op(lambda e: e.tensor_copy(out=out, in_=in_), reads, writes)

    def wdma(out_ap, in_ap, reads, writes):
        pool.dma(out_ap, in_ap, reads, writes)

    def ldma(out_ap, in_ap, reads, writes):
        sp.dma(out_ap, in_ap, reads, writes)

    def load_wslab(src2d, kch, ncols, off=0, slab=None, slr=None):
        for k0 in range(0, kch, 8):
            k1 = min(kch, k0 + 8)
            dst = slab[:, off + k0 * ncols: off + k1 * ncols].rearrange("p (k n) -> p k n", k=k1 - k0)
            wdma(dst, src2d[k0 * 128:k1 * 128, :].rearrange("(k p) n -> p k n", p=128), [], [slr])

    dve.op(lambda e: e.memset(ones[:], 1.0), [], [constR])
    smallR = Res()
    for dst, src in ((b_ada, b_ada_d), (g_norm, g_norm_d), (g_gn, g_gn_d), (b_kv, b_kv_d), (g_kv, g_kv_d),
                     (g_sub, g_sub_d), (lam_sb, lam_d), (c_sb, cT_d), (kd_sb, kd_d), (kdS_sb, kdS_d),
                     (rc_sb, rc_d)):
        ldma(dst[:], src, [], [smallR])
    lamR = Res()
    tt(dve, lam_t[:, 0:64], lam_sb[:, 0:64], lam_sb[:, 64:128], ALU.mult, [smallR], [lamR])
    tt(dve, lam_t[:, 64:128], lam_sb[:, 128:192], lam_sb[:, 192:256], ALU.mult, [smallR], [lamR])
    dve.op(lambda e: e.reduce_sum(out=lam_r[:, 0:1], in_=lam_t[:, 0:64], axis=AX.X), [lamR], [lamR])
    dve.op(lambda e: e.reduce_sum(out=lam_r[:, 1:2], in_=lam_t[:, 64:128], axis=AX.X), [lamR], [lamR])
    actf(lam_r[:, 2:4], lam_r[:, 0:2], AF.Exp, [lamR], [lamR])
    tt(dve, neg_lam[:], lam_r[:, 3:4], lam_r[:, 2:3], ALU.subtract, [lamR], [lamR])
    ts(dve, neg_lam[:], neg_lam[:], -LAM_INIT, None, ALU.add, ALU.bypass, [lamR], [lamR])
    ts(dve, g_sub2[:], g_sub[:], 1.0 - LAM_INIT, None, ALU.mult, ALU.bypass, [smallR], [lamR])
    actf(scT[:], c_sb[:], AF.Silu, [smallR], [modR])

    def ada_mm(wsrc, ncols_total, psb, psr):
        nsl = ncols_total // 512
        for j in range(nsl):
            slab, slr = next_slab()
            load_wslab(wsrc[:, j * 512:(j + 1) * 512], 8, 512, 0, slab, slr)
            for nn in range(4):
                n = j * 4 + nn
                for k in range(8):
                    mm(psb[:, n * 5:(n + 1) * 5], slab[:, k * 512 + nn * 128: k * 512 + nn * 128 + 128],
                       scT[:, k * 5:(k + 1) * 5], k == 0, k == 7, [slr, modR], [psr])

    for l in range(2):
        ada_mm(w_ada_d[l], 9 * D, PS[4 + l], psR[4 + l])
        for s in range(NSEQ):
            tt(dve, modT[:, l * 360 + s: (l + 1) * 360: 5], PS[4 + l][:, s:360:5], b_ada[:, l * 72:(l + 1) * 72],
               ALU.add, [psR[4 + l], smallR], [modR])
    ada_mm(w_adakv_d, 2048, PS[6], psR[6])
    for s in range(NSEQ):
        tt(dve, kvmod[:, s:80:5], PS[6][:, s:80:5], b_kv[:, 0:16], ALU.add, [psR[6], smallR], [modR])

    def modv(l, m, c):
        o = l * 360 + (m * 8 + c) * 5
        return modT[:, o:o + 5]

    def abcv(l, s, kind, c, seq=None):
        o = (((l * 3 + s) * 3 + kind) * 8 + c) * 5
        if seq is None:
            return abc[:, o:o + 5]
        return abc[:, o + seq:o + seq + 1]

    abcR = Res()
    for l in range(2):
        for s in range(3):
            half = 1.0 if s == 1 else 0.5
            for c in range(DC):
                gpre = g_norm[:, (l * 6 + 2 * s) * 8 + c:(l * 6 + 2 * s) * 8 + c + 1]
                gpost = g_norm[:, (l * 6 + 2 * s + 1) * 8 + c:(l * 6 + 2 * s + 1) * 8 + c + 1]
                ts(dve, abcv(l, s, 0, c), modv(l, 3 * s + 1, c), 1.0, gpre, ALU.add, ALU.mult, [modR, smallR], [abcR])
                cp(dve, abcv(l, s, 1, c), modv(l, 3 * s, c), [modR], [abcR])
                ts(dve, abcv(l, s, 2, c), modv(l, 3 * s + 2, c), half, gpost, ALU.mult, ALU.mult, [modR, smallR], [abcR])
    for c in range(DC):
        ts(dve, kvab[:, c * 5:(c + 1) * 5], kvmod[:, (8 + c) * 5:(9 + c) * 5], 1.0, g_kv[:, c:c + 1], ALU.add, ALU.mult,
           [modR, smallR], [abcR])
        cp(dve, kvab[:, 40 + c * 5:45 + c * 5], kvmod[:, c * 5:(c + 1) * 5], [modR], [abcR])

    def rstd_from(ps_ap, psr, n, dim):
        i = 0
        ts(dve, rs_t[:, 0:n], ps_ap, 1.0 / dim, EPS, ALU.mult, ALU.add, [psr], [rsR])
        actf(rs_t[:, 0:n], rs_t[:, 0:n], AF.Ln, [rsR], [rsR])
        actf(rstd_t[i][:, 0:n], rs_t[:, 0:n], AF.Exp, [rsR], [rstdR[i]], scale=-0.5)
        return rstd_t[i], rstdR[i]

    def prenorm(tiles, Afn, Bfn):
        for ti, (a, b, segs, _) in enumerate(tiles):
            n = b - a
            for c in range(DC):
                actf(col(ub, c, a, b), col(xT, c, a, b), AF.Square, [xR[c][ti]], [uR[c][ti]])
            for c in range(DC):
                mm(PS[7][:, 0:n], ones[:], col(ub, c, a, b), c == 0, c == DC - 1, [constR, uR[c][ti]], [psR[7]])
            rt, rr = rstd_from(PS[7][:, 0:n], psR[7], n, D)
            for c in range(DC):
                j = cnt["tT"] % 2
                cnt["tT"] += 1
                tt(dve, tmpT[j][:, 0:n], col(xT, c, a, b), rt[:, 0:n], ALU.mult, [xR[c][ti], rr], [tmpTR[j]])
                for (sa, sb_, seq) in segs:
                    ts(pc, col(hb, c, sa, sb_), tmpT[j][:, sa - a:sb_ - a], Afn(c, seq), Bfn(c, seq),
                       ALU.mult, ALU.add, [tmpTR[j], abcR], [hR[c][ti]])

    def post_chunk(po, por, c, ti, tile, Cfn):
        a, b, segs, _ = tile
        n = b - a
        KSUB = int(os.environ.get("KSUB", "9"))
        if KSUB < 4:
            return
        actf(col(hb, c, a, b), po[:, 0:n], AF.Square, [por], [hR[c][ti]])
        if KSUB < 5:
            return
        for (sa, sb_, seq) in segs:
            actf(col(osb, c, sa, sb_), po[:, sa - a:sb_ - a], AF.Identity, [por, abcR], [oR[c][ti]], scale=Cfn(c, seq))

    def post_final(tiles):
        for ti, (a, b, segs, _) in enumerate(tiles):
            n = b - a
            for c in range(DC):
                mm(PS[7][:, 0:n], ones[:], col(hb, c, a, b), c == 0, c == DC - 1, [constR, hR[c][ti]], [psR[7]])
            rt, rr = rstd_from(PS[7][:, 0:n], psR[7], n, D)
            for c in range(DC):
                j = cnt["tT"] % 2
                cnt["tT"] += 1
                tt(dve, tmpT[j][:, 0:n], col(osb, c, a, b), rt[:, 0:n], ALU.mult, [oR[c][ti], rr], [tmpTR[j]])
                tt(pc, col(xT, c, a, b), col(xT, c, a, b), tmpT[j][:, 0:n], ALU.add, [tmpTR[j]], [xR[c][ti]])

    def out_proj(wsrc, kch, src_buf, srcR, tiles, Cfn):
        for ns in range(4):
            slab, slr = next_slab()
            load_wslab(wsrc[:, ns * 256:(ns + 1) * 256], kch, 256, 0, slab, slr)
            for ti, tile in enumerate(tiles):
                a, b = tile[0], tile[1]
                n = b - a
                for nn in range(2):
                    c = ns * 2 + nn
                    po, por = ps_rot()
                    for k in range(kch):
                        mm(po[:, 0:n], slab[:, k * 256 + nn * 128:k * 256 + nn * 128 + 128], col(src_buf, k, a, b),
                           k == 0, k == kch - 1, [slr, srcR[k][ti]], [por])
                    post_chunk(po, por, c, ti, tile, Cfn)
        if int(os.environ.get("KSUB", "9")) < 6:
            return
        post_final(tiles)

    def ffn(l, f, tiles):
        s = 0 if f == 0 else 2
        KSUB = int(os.environ.get("KSUB", "9"))
        prenorm(tiles, lambda c, q: abcv(l, s, 0, c, q), lambda c, q: abcv(l, s, 1, c, q))
        if KSUB < 2:
            return
        wi = w_fi_d[l, f]
        widths = [256] * 11
        c0 = 0
        for w in widths:
            slab, slr = next_slab()
            load_wslab(wi[:, c0:c0 + w], 8, w, 0, slab, slr)
            load_wslab(wi[:, DFF + c0:DFF + c0 + w], 8, w, 8 * w, slab, slr)
            for ti, (a, b, segs, _) in enumerate(tiles):
                n = b - a
                for cc in range(w // 128):
                    ch = c0 // 128 + cc
                    pa, par = ps_rot()
                    pb, pbr = ps_rot()
                    for k in range(8):
                        mm(pa[:, 0:n], slab[:, k * w + cc * 128:k * w + cc * 128 + 128], col(hb, k, a, b),
                           k == 0, k == 7, [slr, hR[k][ti]], [par])
                    for k in range(8):
                        mm(pb[:, 0:n], slab[:, 8 * w + k * w + cc * 128:8 * w + k * w + cc * 128 + 128],
                           col(hb, k, a, b), k == 0, k == 7, [slr, hR[k][ti]], [pbr])
                    j = cnt["tS"] % 2
                    cnt["tS"] += 1
                    actf(tmpS[j][:, 0:n], pa[:, 0:n], AF.Silu, [par], [tmpSR[j]])
                    tt(dve, col(ub, ch, a, b), tmpS[j][:, 0:n], pb[:, 0:n], ALU.mult, [tmpSR[j], pbr], [uR[ch][ti]])
            c0 += w
        if KSUB < 3:
            return
        out_proj(w_fo_d[l, f], FC, ub, uR, tiles, lambda c, q: abcv(l, s, 2, c, q))

    ar = {"off": 0, "fence": {}}

    def arena_reset():
        ar["off"] = 0
        ar["fence"] = fence()

    def aalloc(cols, dt):
        words = cols if dt == F32 else (cols + 1) // 2
        o = ar["off"]
        ar["off"] += words
        assert ar["off"] <= ARENA_COLS, ar["off"]
        v = arena[:, o:o + words]
        if dt != F32:
            v = v.bitcast(dt)
        return v, Res(ar["fence"])

    def rotary(ps_ap, psr, n, Ct, St, tabR, cols, half, scale, out_ap, outR, tmps, split=None):
        (xs, xsR), (sw, swR), (t1, t1R) = tmps
        actf(xs[:, 0:n], ps_ap, AF.Copy, [psr], [xsR], scale=scale)
        nb = 128 // (2 * half)
        for bb in range(nb):
            p0 = bb * 2 * half
            cp(pc, sw[p0:p0 + half, 0:n], xs[p0 + half:p0 + 2 * half, 0:n], [xsR], [swR])
            cp(pc, sw[p0 + half:p0 + 2 * half, 0:n], xs[p0:p0 + half, 0:n], [xsR], [swR])
        tt(dve, t1[:, 0:n], xs[:, 0:n], Ct[:, cols[0]:cols[1]], ALU.mult, [xsR, tabR], [t1R])
        tt(pc, sw[:, 0:n], sw[:, 0:n], St[:, cols[0]:cols[1]], ALU.mult, [swR, tabR], [swR])
        if split is None:
            tt(dve, out_ap, t1[:, 0:n], sw[:, 0:n], ALU.add, [t1R, swR], [outR])
        else:
            tt(dve, split[0], t1[0:64, 0:n], sw[0:64, 0:n], ALU.add, [t1R, swR], [outR])
            tt(dve, split[1], t1[64:128, 0:n], sw[64:128, 0:n], ALU.add, [t1R, swR], [outR])

    def retention(gi, g0, T, tiles):
        l, s = 0, 1
        prenorm(tiles, lambda c, q: abcv(l, s, 0, c, q), lambda c, q: abcv(l, s, 1, c, q))
        arena_reset()
        Tp = sum(t[1] - t[0] for t in tiles if not t[3])
        npb = Tp // 128
        var = 0 if Tp == 768 else 1
        has_sample = any(t[3] for t in tiles)
        Ct, tabR = aalloc(TMAX, F32)
        St, _ = aalloc(TMAX, F32)
        ldma(Ct[:, 0:T], rotR_C_d[:, g0:g0 + T], [], [tabR])
        ldma(St[:, 0:T], rotR_S_d[:, g0:g0 + T], [], [tabR])
        rtmps = [(tmpS[0], tmpSR[0]), aalloc(512, F32), (tmpT[0], tmpTR[0])]
        qT, qTR = aalloc(TMAX, BF16)
        kT, kTR = aalloc(TMAX, BF16)
        qdT, qdTR = aalloc(TMAX, BF16)
        sg, sgR = aalloc(2 * TMAX, BF16)
        vtok, vtokR = aalloc(6 * 256, BF16)
        kdtok, kdtokR = aalloc(6 * 128, BF16)
        kdS = [aalloc(128, BF16) for _ in range(4)]
        strip, stripR = aalloc(768, F32)
        qd, qdR = aalloc(768, F32)
        DS, DSR = aalloc(128, F32)
        qdS, _ = aalloc(128, F32)
        og, ogR = aalloc(2 * 512, F32)
        PT = [aalloc(512, BF16) for _ in range(2)]
        sqg, sqgR = aalloc(2 * 512, BF16)
        stS32 = [aalloc(256, F32) for _ in range(1)]
        stSbf = [aalloc(256, BF16) for _ in range(4)]
        stO = [aalloc(256, F32) for _ in range(1)]
        ptc = 0
        for h in range(H):
            slab, slr = next_slab()
            W = 768
            load_wslab(w_ina_d[:, h * 128:(h + 1) * 128], 8, 128, 0, slab, slr)
            load_wslab(w_ina_d[:, 1024 + h * 128:1024 + (h + 1) * 128], 8, 128, 8 * 128, slab, slr)
            load_wslab(w_ina_d[:, 2048 + h * 256:2048 + (h + 1) * 256], 8, 256, 16 * 128, slab, slr)
            load_wslab(w_ina_d[:, 4096 + h * 256:4096 + (h + 1) * 256], 8, 256, 16 * 128 + 8 * 256, slab, slr)
            OQ, OK_, OV, OG = 0, 1024, 2048, 2048 + 2048
            ldma(strip[:], strip_d[h], [], [stripR])
            ldma(qd[:], qd_d[h], [], [qdR])
            if has_sample:
                ldma(DS[:], DS_d[h], [], [DSR])
                ldma(qdS[:], qdS_d[h], [], [DSR])
                for s4 in range(4):
                    wdma(stSbf[s4][0][:], st_in_d[s4, h], [], [stSbf[s4][1]])
            for ti, (a, b, segs, smp) in enumerate(tiles):
                n = b - a
                pq, pqr = ps_rot()
                for k in range(8):
                    mm(pq[:, 0:n], slab[:, OQ + k * 128:OQ + k * 128 + 128], col(hb, k, a, b), k == 0, k == 7,
                       [slr, hR[k][ti]], [pqr])
                rotary(pq[:, 0:n], pqr, n, Ct, St, tabR, (a, b), 64, 1.0, qT[:, a:b], qTR, rtmps)
                pk, pkr = ps_rot()
                for k in range(8):
                    mm(pk[:, 0:n], slab[:, OK_ + k * 128:OK_ + k * 128 + 128], col(hb, k, a, b), k == 0, k == 7,
                       [slr, hR[k][ti]], [pkr])
                rotary(pk[:, 0:n], pkr, n, Ct, St, tabR, (a, b), 64, 128 ** -0.5, kT[:, a:b], kTR, rtmps)
                for vc in range(2):
                    pg, pgr = ps_rot()
                    for k in range(8):
                        mm(pg[:, 0:n], slab[:, OG + k * 256 + vc * 128:OG + k * 256 + vc * 128 + 128],
                           col(hb, k, a, b), k == 0, k == 7, [slr, hR[k][ti]], [pgr])
                    actf(sg[:, vc * TMAX + a:vc * TMAX + b], pg[:, 0:n], AF.Silu, [pgr], [sgR])
                for tb in range(n // 128):
                    blk = (a // 128) + tb
                    ca = a + tb * 128
                    pv, pvr = ps_rot()
                    for k in range(8):
                        mm(pv[:, 0:256], col(hb, k, ca, ca + 128), slab[:, OV + k * 256:OV + k * 256 + 256],
                           k == 0, k == 7, [slr, hR[k][ti]], [pvr])
                    actf(vtok[:, blk * 256:(blk + 1) * 256], pv[:, 0:256], AF.Copy, [pvr], [vtokR])
                    pt, ptr = ps_rot()
                    mm(pt[:, 0:128], kT[:, ca:ca + 128], ident[:], True, True, [kTR, constR], [ptr])
                    if not smp:
                        actf(kdtok[:, blk * 128:(blk + 1) * 128], pt[:, 0:128], AF.Identity, [ptr, smallR], [kdtokR],
                             scale=kd_sb[:, (h * 2 + var) * 6 + blk:(h * 2 + var) * 6 + blk + 1])
                    else:
                        for s4 in range(4):
                            actf(kdS[s4][0][:], pt[:, 0:128], AF.Identity, [ptr, smallR], [kdS[s4][1]],
                                 scale=kdS_sb[:, h * 4 + s4:h * 4 + s4 + 1])
            for ti, (a, b, segs, smp) in enumerate(tiles):
                if smp:
                    tt(dve, qdT[:, a:b], qT[:, a:b], qdS[:], ALU.mult, [qTR, DSR], [qdTR])
                elif gi > 0:
                    tt(dve, qdT[:, a:b], qT[:, a:b], qd[:, a:b], ALU.mult, [qTR, qdR], [qdTR])
            for ti, (a, b, segs, smp) in enumerate(tiles):
                n = b - a
                po = [(PS[4], psR[4]), (PS[5], psR[5])]
                if not smp:
                    nkb = b // 128
                    rslots = {}

                    def RA(kb):
                        nonlocal ptc
                        c0 = max(a, kb * 128)
                        w_ = b - c0
                        ps_, psr_ = ps_rot()
                        mm(ps_[:, 0:w_], kT[:, kb * 128:(kb + 1) * 128], qT[:, c0:b], True, True, [kTR, qTR], [psr_])
                        pt_, ptr_ = PT[ptc % 2]
                        ptc += 1
                        tt(dve, pt_[:, 0:w_], ps_[:, 0:w_], strip[:, c0 - kb * 128:b - kb * 128], ALU.mult,
                           [psr_, stripR], [ptr_])
                        rslots[kb] = (pt_, ptr_, c0, w_)

                    def RB(kb):
                        pt_, ptr_, c0, w_ = rslots.pop(kb)
                        last = (kb == nkb - 1) and gi == 0
                        for vc in range(2):
                            mm(po[vc][0][:, c0 - a:n], vtok[:, kb * 256 + vc * 128:kb * 256 + vc * 128 + 128],
                               pt_[:, 0:w_], kb == 0, last, [vtokR, ptr_], [po[vc][1]], force=(vc == 1))

                    RA(0)
                    for kb in range(nkb):
                        if kb + 1 < nkb:
                            RA(kb + 1)
                        RB(kb)
                    if gi > 0:
                        for vc in range(2):
                            mm(po[vc][0][:, 0:n], stbf[:, h * 256 + vc * 128:h * 256 + vc * 128 + 128], qdT[:, a:b],
                               False, True, [stR[h], qdTR], [po[vc][1]])
                else:
                    blk = a // 128
                    ps_, psr_ = ps_rot()
                    mm(ps_[:, 0:128], kT[:, a:b], qT[:, a:b], True, True, [kTR, qTR], [psr_])
                    pt_, ptr_ = PT[ptc % 2]
                    ptc += 1
                    tt(dve, pt_[:, 0:128], ps_[:, 0:128], DS[:], ALU.mult, [psr_, DSR], [ptr_])
                    for vc in range(2):
                        mm(po[vc][0][:, 0:128], vtok[:, blk * 256 + vc * 128:blk * 256 + vc * 128 + 128],
                           pt_[:, 0:128], True, False, [vtokR, ptr_], [po[vc][1]])
                    for vc in range(2):
                        for s4 in range(4):
                            mm(po[vc][0][:, 32 * s4:32 * s4 + 32], stSbf[s4][0][:, vc * 128:vc * 128 + 128],
                               qdT[:, a + 32 * s4:a + 32 * s4 + 32], False, s4 == 3, [stSbf[s4][1], qdTR],
                               [po[vc][1]])
                for vc in range(2):
                    actf(sqg[:, vc * 512:vc * 512 + n], po[vc][0][:, 0:n], AF.Square, [po[vc][1]], [sqgR])
                    actf(og[:, vc * 512:vc * 512 + n], po[vc][0][:, 0:n], AF.Identity, [po[vc][1], smallR], [ogR],
                         scale=g_gn[:, h * 2 + vc:h * 2 + vc + 1])
                    tt(dve, og[:, vc * 512:vc * 512 + n], og[:, vc * 512:vc * 512 + n],
                       sg[:, vc * TMAX + a:vc * TMAX + b], ALU.mult, [sgR], [ogR])
                for vc in range(2):
                    mm(PS[7][:, 0:n], ones[:], sqg[:, vc * 512:vc * 512 + n], vc == 0, vc == 1, [constR, sqgR], [psR[7]])
                rt, rr = rstd_from(PS[7][:, 0:n], psR[7], n, 256)
                for vc in range(2):
                    tt(pc, col(ub, h * 2 + vc, a, b), og[:, vc * 512:vc * 512 + n], rt[:, 0:n], ALU.mult,
                       [ogR, rr], [uR[h * 2 + vc][ti]])
            pst, pstr = PS[6], psR[6]
            for blk in range(npb):
                mm(pst[:, 0:256], kdtok[:, blk * 128:(blk + 1) * 128], vtok[:, blk * 256:(blk + 1) * 256],
                   blk == 0, blk == npb - 1, [kdtokR, vtokR], [pstr])
            if gi == 0:
                cp(dve, st32[:, h * 256:(h + 1) * 256], pst[:, 0:256], [pstr], [stR[h]])
            else:
                ts(dve, st32[:, h * 256:(h + 1) * 256], st32[:, h * 256:(h + 1) * 256], 1.0,
                   rc_sb[:, h * 3 + var:h * 3 + var + 1], ALU.mult, ALU.mult, [smallR], [stR[h]])
                tt(dve, st32[:, h * 256:(h + 1) * 256], st32[:, h * 256:(h + 1) * 256], pst[:, 0:256], ALU.add,
                   [pstr], [stR[h]])
            actf(stbf[:, h * 256:(h + 1) * 256], st32[:, h * 256:(h + 1) * 256], AF.Copy, [stR[h]], [stR[h]])
            if gi == len(GROUPS) - 1:
                ldma(stp_o[h], st32[:, h * 256:(h + 1) * 256], [stR[h]], [])
            if has_sample:
                blk = npb
                for s4 in range(4):
                    mm(pst[:, 0:256], kdS[s4][0][:], vtok[:, blk * 256:(blk + 1) * 256], True, True,
                       [kdS[s4][1], vtokR], [pstr])
                    so, sor = stO[0]
                    ldma(stS32[0][0][:], st_in_d[s4, h], [], [stS32[0][1]])
                    ts(dve, so[:], stS32[0][0][:], 1.0, rc_sb[:, h * 3 + 2:h * 3 + 3], ALU.mult, ALU.mult,
                       [smallR, stS32[0][1]], [sor])
                    tt(dve, so[:], so[:], pst[:, 0:256], ALU.add, [pstr], [sor])
                    ldma(sts_o[s4, h], so[:], [sor], [])
        out_proj(w_outa_d, 16, ub, uR, tiles, lambda c, q: abcv(l, s, 2, c, q))

    def shared_kv(gi, g0, T, tiles):
        prenorm(tiles, lambda c, q: kvab[:, c * 5 + q:c * 5 + q + 1], lambda c, q: kvab[:, 40 + c * 5 + q:41 + c * 5 + q])
        arena_reset()
        Ct, tabR = aalloc(TMAX, F32)
        St, _ = aalloc(TMAX, F32)
        ldma(Ct[:, 0:T], rotD_C_d[:, g0:g0 + T], [], [tabR])
        ldma(St[:, 0:T], rotD_S_d[:, g0:g0 + T], [], [tabR])
        rtmps = [(tmpS[0], tmpSR[0]), aalloc(512, F32), (tmpT[0], tmpTR[0])]
        kf = [aalloc(512, F32) for _ in range(2)]
        vf = [aalloc(512, F32) for _ in range(2)]
        kc = 0
        for j in range(2):
            slab, slr = next_slab()
            load_wslab(w_kv_d[:, j * 512:(j + 1) * 512], 8, 512, 0, slab, slr)
            for ti, (a, b, segs, smp) in enumerate(tiles):
                n = b - a
                for hh in range(4):
                    hd = j * 4 + hh
                    pk, pkr = ps_rot()
                    for k in range(8):
                        mm(pk[:, 0:n], slab[:, k * 512 + hh * 128:k * 512 + hh * 128 + 128], col(hb, k, a, b),
                           k == 0, k == 7, [slr, hR[k][ti]], [pkr])
                    ko, kor = kf[kc % 2]
                    kc += 1
                    rotary(pk[:, 0:n], pkr, n, Ct, St, tabR, (a, b), 32, 1.0, ko[:, 0:n], kor, rtmps)
                    ldma(kT_o[hd * 128:(hd + 1) * 128, g0 + a:g0 + b], ko[:, 0:n], [kor], [kT_oR[hd][gi]])
        vcn = 0
        for j in range(2):
            slab, slr = next_slab()
            load_wslab(w_kv_d[:, 1024 + j * 512:1024 + (j + 1) * 512], 8, 512, 0, slab, slr)
            for ti, (a, b, segs, smp) in enumerate(tiles):
                n = b - a
                for tb in range(n // 128):
                    ca = a + tb * 128
                    pv, pvr = ps_rot()
                    for k in range(8):
                        mm(pv[:, 0:512], col(hb, k, ca, ca + 128), slab[:, k * 512:(k + 1) * 512], k == 0, k == 7,
                           [slr, hR[k][ti]], [pvr])
                    vo, vor = vf[vcn % 2]
                    vcn += 1
                    actf(vo[:], pv[:, 0:512], AF.Copy, [pvr], [vor])
                    ldma(v_o[g0 + ca:g0 + ca + 128, j * 512:(j + 1) * 512], vo[:], [vor], [v_oR[gi]])

    def diff_attn(gi, g0, T, tiles):
        l, s = 1, 1
        SC = 64 ** -0.5
        prenorm(tiles, lambda c, q: abcv(l, s, 0, c, q), lambda c, q: abcv(l, s, 1, c, q))
        arena_reset()
        Tp = sum(t[1] - t[0] for t in tiles if not t[3])
        nkeys = g0 + Tp
        nkb_all = nkeys // 128
        hs_ = any(t[3] for t in tiles)
        Ct, tabR = aalloc(T if hs_ else TMAX, F32)
        St, _ = aalloc(T if hs_ else TMAX, F32)
        ldma(Ct[:, 0:T], rotD_C_d[:, g0:g0 + T], [], [tabR])
        ldma(St[:, 0:T], rotD_S_d[:, g0:g0 + T], [], [tabR])
        rtmps = [(tmpS[0], tmpSR[0]), aalloc(512, F32), (tmpT[0], tmpTR[0])]
        nb = 1 if any(t[3] for t in tiles) else 2
        KTb = [aalloc(2048, BF16) for _ in range(nb)]
        Vsb = [aalloc(16 * 128, BF16) for _ in range(nb)]
        nbq = 2
        if nb == 1:
            assert tiles[-1][3]
        qTb = [aalloc(1024, BF16) for _ in range(nbq)]
        for qb_, qbR_ in qTb:
            dve.op(lambda e, qb_=qb_: e.memset(qb_[:], 0.0), [], [qbR_])
        KT, KTR = KTb[0]
        Vs, VsR = Vsb[0]
        qT, qTR = qTb[0]
        E = [aalloc(512, BF16) for _ in range(3 if hs_ else 4)]
        r_sb, r_R = aalloc(512, F32)
        a0, a0R = aalloc(256, F32)
        a1, a1R = aalloc(256, F32)
        o_sb, o_R = a0, a0R
        sqo, sqoR = aalloc(256, BF16)
        has_sample = any(t[3] for t in tiles)
        if has_sample:
            KCb = [aalloc(2048, BF16) for _ in range(2)]
            VCb = [aalloc(16 * 128, BF16) for _ in range(2)]
            KN, KNR = aalloc(32, BF16)
            VN, VNR = aalloc(128, BF16)
        st8 = {"ec": 0, "acc": 0, "ld": 0}
        pending = []
        acc_sets = [((PS[4], psR[4]), (PS[5], psR[5])), ((PS[6], psR[6]), (PS[3], psR[3]))]
        rot_n[0] = 3

        def run_pending():
            while pending:
                pending.pop(0)()

        def pipeline(A, B):
            n_ = len(A)
            DEPTH = 2
            for j in range(min(DEPTH, n_)):
                A[j]()
            for i in range(n_):
                if i + DEPTH < n_:
                    A[i + DEPTH]()
                if i == min(1, n_ - 1):
                    run_pending()
                B[i]()

        def fin1(acc_o, aoR, acc_s, asR, nq):
            dve.op(lambda e: e.reciprocal(out=r_sb[:, 0:2 * nq], in_=acc_s[:, 0:2 * nq]), [asR], [r_R])
            tt(dve, a0[:, 0:nq], acc_o[:, 0:nq], r_sb[:, 0:nq], ALU.mult, [aoR, r_R], [a0R])
            actf(a1[:, 0:nq], acc_o[:, nq:2 * nq], AF.Identity, [aoR, lamR], [a1R], scale=neg_lam[:, 0:1])
            tt(dve, a1[:, 0:nq], a1[:, 0:nq], r_sb[:, nq:2 * nq], ALU.mult, [r_R], [a1R])
            tt(dve, o_sb[:, 0:nq], a0[:, 0:nq], a1[:, 0:nq], ALU.add, [a1R], [o_R])
            actf(sqo[:, 0:nq], o_sb[:, 0:nq], AF.Square, [o_R], [sqoR])

        def fin2(h, nq, a_, b_, ti):
            mm(PS[7][:, 0:nq], ones[:], sqo[:, 0:nq], True, True, [constR, sqoR], [psR[7]])
            rt, rr = rstd_from(PS[7][:, 0:nq], psR[7], nq, 128)
            ts(dve, o_sb[:, 0:nq], o_sb[:, 0:nq], 1.0, g_sub2[:, 0:1], ALU.mult, ALU.mult, [lamR], [o_R])
            tt(dve, col(ub, h, a_, b_), o_sb[:, 0:nq], rt[:, 0:nq], ALU.mult, [o_R, rr], [uR[h][ti]])

        def prompt_subtile(h, ti, a, qa, nq):
            gq0 = g0 + a + qa
            kb_last = (gq0 + nq - 1) // 128
            (acc_o, aoR), (acc_s, asR) = acc_sets[st8["acc"] % 2]
            st8["acc"] += 1
            slots = {}

            def A(kb):
                c0 = max(0, kb * 128 - gq0)
                lg, lgr = ps_rot()
                for t in range(2):
                    mm(lg[:, t * nq + c0:(t + 1) * nq], KT[:, kb * 128:(kb + 1) * 128],
                       qT[:, t * 512 + qa + c0:t * 512 + qa + nq], True, True, [KTR, qTR], [lgr])
                Et, EtR = E[st8["ec"] % len(E)]
                st8["ec"] += 1
                slots[kb] = (Et, EtR)
                if c0 == 0:
                    actf(Et[:, 0:2 * nq], lg[:, 0:2 * nq], AF.Exp, [lgr], [EtR], scale=SC)
                else:
                    for t in range(2):
                        actf(Et[:, t * nq + c0:(t + 1) * nq], lg[:, t * nq + c0:(t + 1) * nq], AF.Exp,
                             [lgr], [EtR], scale=SC)
                if kb * 128 >= gq0:
                    for t in range(2):
                        x0 = t * nq + c0
                        dve.op(lambda e, Et=Et, x0=x0: e.memset(Et[64:128, x0:x0 + 64], 0.0), [], [EtR])

            def B(kb):
                c0 = max(0, kb * 128 - gq0)
                Et, EtR = slots.pop(kb)
                if c0 == 0:
                    mm(acc_o[:, 0:2 * nq], Vs[:, kb * 128:(kb + 1) * 128], Et[:, 0:2 * nq],
                       kb == 0, kb == kb_last, [VsR, EtR], [aoR])
                    mm(acc_s[:, 0:2 * nq], ones[:], Et[:, 0:2 * nq],
                       kb == 0, kb == kb_last, [constR, EtR], [asR], force=True)
                else:
                    for t in range(2):
                        mm(acc_o[:, t * nq + c0:(t + 1) * nq], Vs[:, kb * 128:(kb + 1) * 128],
                           Et[:, t * nq + c0:(t + 1) * nq], False, kb == kb_last and t == 1, [VsR, EtR], [aoR])
                    for t in range(2):
                        mm(acc_s[:, t * nq + c0:(t + 1) * nq], ones[:], Et[:, t * nq + c0:(t + 1) * nq],
                           False, kb == kb_last and t == 1, [constR, EtR], [asR], force=(t == 1))

            kbs = list(range(kb_last + 1))
            pipeline([(lambda kb=kb: A(kb)) for kb in kbs], [(lambda kb=kb: B(kb)) for kb in kbs])
            fin1(acc_o, aoR, acc_s, asR, nq)
            pending.append(lambda: fin2(h, nq, a + qa, a + qa + nq, ti))

        def sample_loads(h, step):
            if step >= 8:
                h, step = h + 1, step - 8
            if h >= H:
                return
            s4, half = step // 2, step % 2
            bi = step % 2
            KCh, KChR = KCb[bi]
            VCh, VChR = VCb[bi]
            wdma(KCh[:], kcT_d[s4, h][:, half * 2048:(half + 1) * 2048], [], [KChR])
            wdma(VCh[:], vc_d[s4, h][:, half * 2048:(half + 1) * 2048], [], [VChR])

        def sample_seq(h, ti, a, s4):
            nq = 32
            qc = 32 * s4
            (acc_o, aoR), (acc_s, asR) = acc_sets[st8["acc"] % 2]
            st8["acc"] += 1
            gc = NTOK - 128 + 32 * s4
            wdma(KN[:], kT_o[h * 128:(h + 1) * 128, gc:gc + 32], [kT_oR[h][gi]], [KNR])
            wdma(VN[0:32, :], v_o[gc:gc + 32, h * 128:(h + 1) * 128], [v_oR[gi]], [VNR])
            slots = {}

            def A(g):
                if g < 4:
                    half = g // 2
                    step = s4 * 2 + half
                    KCh, KChR = KCb[step % 2]
                    lg, lgr = ps_rot()
                    for kbi in range(8):
                        kbl = (g % 2) * 8 + kbi
                        for t in range(2):
                            mm(lg[:, (kbi * 2 + t) * 32:(kbi * 2 + t) * 32 + 32], KCh[:, kbl * 128:(kbl + 1) * 128],
                               qT[:, t * 512 + qc:t * 512 + qc + 32], True, True, [KChR, qTR], [lgr])
                    Et, EtR = E[st8["ec"] % len(E)]
                    st8["ec"] += 1
                    slots[g] = (Et, EtR)
                    actf(Et[:, 0:512], lg[:, 0:512], AF.Exp, [lgr], [EtR], scale=SC)
                else:
                    lg, lgr = ps_rot()
                    for t in range(2):
                        mm(lg[0:32, t * 32:t * 32 + 32], KN[:, 0:32], qT[:, t * 512 + qc:t * 512 + qc + 32],
                           True, True, [KNR, qTR], [lgr])
                    Et, EtR = E[st8["ec"] % len(E)]
                    st8["ec"] += 1
                    slots[g] = (Et, EtR)
                    actf(Et[0:32, 0:64], lg[0:32, 0:64], AF.Exp, [lgr], [EtR], scale=SC)

            def B(g):
                Et, EtR = slots.pop(g)
                if g < 4:
                    step = s4 * 2 + g // 2
                    VCh, VChR = VCb[step % 2]
                    for kbi in range(8):
                        kbl = (g % 2) * 8 + kbi
                        mm(acc_o[:, 0:64], VCh[:, kbl * 128:(kbl + 1) * 128], Et[:, kbi * 64:kbi * 64 + 64],
                           g == 0 and kbi == 0, False, [VChR, EtR], [aoR])
                    for kbi in range(8):
                        mm(acc_s[:, 0:64], ones[:], Et[:, kbi * 64:kbi * 64 + 64], g == 0 and kbi == 0, False,
                           [constR, EtR], [asR], force=(kbi == 7))
                    if g % 2 == 1:
                        sample_loads(h, step + 2)
                else:
                    mm(acc_o[:, 0:64], VN[0:32, :], Et[0:32, 0:64], False, True, [VNR, EtR], [aoR])
                    mm(acc_s[:, 0:64], ones[0:32, :], Et[0:32, 0:64], False, True, [constR, EtR], [asR])

            gs = list(range(5))
            pipeline([(lambda g=g: A(g)) for g in gs], [(lambda g=g: B(g)) for g in gs])
            fin1(acc_o, aoR, acc_s, asR, nq)
            pending.append(lambda: fin2(h, nq, a + qc, a + qc + 32, ti))

        units = [(h, ti) for h in range(H) for ti in range(len(tiles))]
        hslab = {}

        def prep(ui):
            h, ti = units[ui]
            a, b, segs, smp = tiles[ti]
            n = b - a
            if ti == 0:
                hslab[h] = next_slab()
                slab, slr = hslab[h]
                load_wslab(w_q_d[:, h * 128:(h + 1) * 128], 8, 128, 0, slab, slr)
                kdeps = [kT_oR[h][g] for g in range(gi + 1)]
                vdeps = [v_oR[g] for g in range(gi + 1)]
                KT_, KTR_ = KTb[h % nb]
                Vs_, VsR_ = Vsb[h % nb]
                wdma(KT_[:, 0:nkeys], kT_o[h * 128:(h + 1) * 128, 0:nkeys], kdeps, [KTR_])
                for b0 in range(0, nkb_all, 8):
                    b1 = min(nkb_all, b0 + 8)
                    wdma(Vs_[:, b0 * 128:b1 * 128].rearrange("p (b v) -> p b v", b=b1 - b0),
                         v_o[b0 * 128:b1 * 128, h * 128:(h + 1) * 128].rearrange("(b p) v -> p b v", p=128),
                         vdeps, [VsR_])
                if has_sample and h == 0:
                    sample_loads(0, 0)
                    sample_loads(0, 1)
            slab, slr = hslab[h]
            q_, qR_ = qTb[ui % nbq]
            pq, pqr = ps_rot()
            for k in range(8):
                mm(pq[:, 0:n], slab[:, k * 128:(k + 1) * 128], col(hb, k, a, b), k == 0, k == 7,
                   [slr, hR[k][ti]], [pqr])
            rotary(pq[:, 0:n], pqr, n, Ct, St, tabR, (a, b), 32, 1.0, None, qR_, rtmps,
                   split=(q_[0:64, 0:n], q_[64:128, 512:512 + n]))

        def attend(ui):
            nonlocal KT, KTR, Vs, VsR, qT, qTR
            h, ti = units[ui]
            a, b, segs, smp = tiles[ti]
            n = b - a
            KT, KTR = KTb[h % nb]
            Vs, VsR = Vsb[h % nb]
            qT, qTR = qTb[ui % nbq]
            if not smp:
                for qa in range(0, n, 256):
                    prompt_subtile(h, ti, a, qa, 256)
            else:
                for s4 in range(4):
                    sample_seq(h, ti, a, s4)

        prep(0)
        for ui in range(len(units)):
            if ui + 1 < len(units):
                prep(ui + 1)
            attend(ui)
        run_pending()
        rot_n[0] = 4
        out_proj(w_outb_d, 8, ub, uR, tiles, lambda c, q: abcv(l, s, 2, c, q))

    ident = sb("ident", 128, BF16)
    ident_d = din("ident", [128, 128])
    wdma(ident[:], ident_d, [], [constR])

    if os.environ.get("KNOSAMPLE"):
        GROUPS[2] = (1536, 512, [(0, 512, [(0, 512, 0)], False)])
    STAGE = int(os.environ.get("KSTAGE", "99"))
    NG = int(os.environ.get("KGROUPS", "3"))
    for gi, (g0, T, tiles) in enumerate(GROUPS[:NG]):
        for c in range(DC):
            for ti, (a, b, segs, _) in enumerate(tiles):
                ldma(col(xT, c, a, b), xT_d[c * 128:(c + 1) * 128, g0 + a:g0 + b], [], [xR[c][ti]])
        if STAGE >= 2:
            ffn(0, 0, tiles)
        if STAGE >= 3:
            retention(gi, g0, T, tiles)
        if STAGE >= 4:
            ffn(0, 1, tiles)
        if STAGE >= 5:
            shared_kv(gi, g0, T, tiles)
        if STAGE >= 6:
            ffn(1, 0, tiles)
        if STAGE >= 7:
            diff_attn(gi, g0, T, tiles)
        if STAGE >= 8:
            ffn(1, 1, tiles)
        for c in range(DC):
            for ti, (a, b, segs, _) in enumerate(tiles):
                ldma(yT_d[c * 128:(c + 1) * 128, g0 + a:g0 + b], col(xT, c, a, b), [xR[c][ti]], [])

    fin = {}
    for e in (pool, sp):
        for s_, c_ in zip(e.dsems, e.dcnt):
            if c_ > 0:
                fin[id(s_)] = (s_, c_)
    sp._wait(fin)

    with nc.Block() as block:
        @block.tensor
        def _(e):
            pe.run(e)

        @block.scalar
        def _(e):
            act.run(e)

        @block.vector
        def _(e):
            dve.run(e)

        @block.gpsimd
        def _(e):
            pool.run(e)

        @block.sync
        def _(e):
            sp.run(e)
    es.close()
    return nc


def _const_tables():
    lg = np.log1p(-np.exp2(-5.0 - np.arange(H, dtype=np.float64)))
    pos = np.concatenate([np.arange(2048, dtype=np.float64)] + [PASTN + np.arange(32, dtype=np.float64)] * 4)

    def rot(dh):
        half = dh // 2
        inv = np.power(10000.0, -np.arange(0, dh, 2, dtype=np.float32) / np.float32(dh)).astype(np.float32)
        ang = (pos.astype(np.float32)[None, :] * inv[:, None]).astype(np.float32)
        cos, sin = np.cos(ang).astype(np.float32), np.sin(ang).astype(np.float32)
        C = np.zeros((128, NTOK), np.float32)
        S = np.zeros((128, NTOK), np.float32)
        for p in range(128):
            dd = p % dh
            f = dd % half
            C[p] = cos[f]
            S[p] = -sin[f] if dd < half else sin[f]
        return C, S

    RC, RS = rot(128)
    DCc, DSs = rot(64)
    jl = np.arange(128)[:, None].astype(np.float64)
    dl = np.arange(768)[None, :].astype(np.float64)
    strip = np.zeros((H, 128, 768), np.float32)
    qd = np.zeros((H, 128, 768), np.float32)
    DS = np.zeros((H, 128, 128), np.float32)
    qdS = np.zeros((H, 128, 128), np.float32)
    kd = np.zeros((128, H * 2 * 6), np.float32)
    kdS = np.zeros((128, H * 4), np.float32)
    rc = np.zeros((128, H * 3), np.float32)
    for h in range(H):
        g = lg[h]
        st = np.exp(g * (dl - jl))
        il = np.arange(128)[None, :].astype(np.float64)
        diag = np.exp(g * np.abs(il - jl))
        diag[(jl // 64) > (il // 64) * np.ones_like(jl)] = 0.0
        st[:, 0:128] = diag
        strip[h] = st.astype(np.float32)
        qd[h] = np.broadcast_to(np.exp(g * dl), (128, 768)).astype(np.float32)
        m = np.exp(g * np.abs(il - jl))
        m[(jl // 32) != (il // 32) * np.ones_like(jl)] = 0.0
        DS[h] = m.astype(np.float32)
        qdS[h] = np.broadcast_to(np.exp(g * (np.arange(128) % 32))[None, :], (128, 128)).astype(np.float32)
        for var, Tp in enumerate((768, 512)):
            for blk in range(6):
                j = blk * 128 + np.arange(128)
                kd[:, (h * 2 + var) * 6 + blk] = np.exp(g * (Tp - j)).astype(np.float32)
        for s4 in range(4):
            p = np.arange(128)
            v = np.exp(g * (32 - (p % 32)))
            v[(p // 32) != s4] = 0.0
            kdS[:, h * 4 + s4] = v.astype(np.float32)
        rc[:, h * 3 + 0] = np.float32(np.exp(g * 768))
        rc[:, h * 3 + 1] = np.float32(np.exp(g * 512))
        rc[:, h * 3 + 2] = np.float32(np.exp(g * 32))
    return dict(rotR_C=RC, rotR_S=RS, rotD_C=DCc, rotD_S=DSs, ret_strip=strip, ret_qd=qd, ret_kd=kd, ret_DS=DS,
                ret_qdS=qdS, ret_kdS=kdS, ret_c=rc, ident=np.eye(128, dtype=np.float32))


def _fm(v, nch):
    v = np.asarray(v, np.float32)
    lead = v.shape[:-1]
    r = v.reshape(lead + (nch, 128))
    r = np.moveaxis(r, -1, 0)
    return np.ascontiguousarray(r.reshape(128, -1))


_NC_CACHE = {}


def kernel(x_prompt, x_sample, state_ret, cache_k, cache_v, c_prompt, c_sample,
           w_ada, b_ada, g_norm, w_ffn_in, w_ffn_out, w_in_a, g_gn_a, w_out_a,
           w_ada_kv, b_ada_kv, g_kv, w_kv, w_q_b, lam_b, g_subln_b, w_out_b):
    in_maps = make_in_maps(x_prompt, x_sample, state_ret, cache_k, cache_v, c_prompt, c_sample,
                           w_ada, b_ada, g_norm, w_ffn_in, w_ffn_out, w_in_a, g_gn_a, w_out_a,
                           w_ada_kv, b_ada_kv, g_kv, w_kv, w_q_b, lam_b, g_subln_b, w_out_b)
    n = 8
    if "nc" not in _NC_CACHE:
        _NC_CACHE["nc"] = build_program()
    nc = _NC_CACHE["nc"]
    res = run_bass_kernel_spmd(nc, in_maps, core_ids=list(range(n)))
    return assemble(res.results)


def make_in_maps(x_prompt, x_sample, state_ret, cache_k, cache_v, c_prompt, c_sample,
                 w_ada, b_ada, g_norm, w_ffn_in, w_ffn_out, w_in_a, g_gn_a, w_out_a,
                 w_ada_kv, b_ada_kv, g_kv, w_kv, w_q_b, lam_b, g_subln_b, w_out_b, cores=range(8)):
    f = lambda a: np.ascontiguousarray(np.asarray(a, np.float32))
    shared = dict(
        w_ada=f(w_ada), b_adaT=_fm(b_ada, 72), g_normT=_fm(g_norm, 8), w_ffn_in=f(w_ffn_in), w_ffn_out=f(w_ffn_out),
        w_in_a=f(w_in_a[0]), g_gnT=_fm(g_gn_a[0], 16), w_out_a=f(w_out_a[0]), w_ada_kv=f(w_ada_kv),
        b_kvT=_fm(b_ada_kv, 16), g_kvT=_fm(g_kv, 8), w_kv=f(w_kv), w_q_b=f(w_q_b[0]),
        lam_bb=np.ascontiguousarray(np.broadcast_to(np.asarray(lam_b[0], np.float32).reshape(1, 256), (128, 256))),
        g_subT=f(np.asarray(g_subln_b[0]).reshape(128, 1)), w_out_b=f(w_out_b[0]),
    )
    shared.update(_const_tables())
    x_prompt = np.asarray(x_prompt, np.float32)
    x_sample = np.asarray(x_sample, np.float32)
    cache_k = np.asarray(cache_k, np.float32)
    cache_v = np.asarray(cache_v, np.float32)
    state_ret = np.asarray(state_ret, np.float32)
    in_maps = []
    for i in cores:
        xs = x_sample[4 * i:4 * i + 4].reshape(128, D)
        xT = np.ascontiguousarray(np.concatenate([x_prompt[i], xs], axis=0).T)
        c5 = np.concatenate([np.asarray(c_prompt, np.float32)[i:i + 1], np.asarray(c_sample, np.float32)[4 * i:4 * i + 4]], 0)
        cT = np.ascontiguousarray(c5.reshape(5, 8, 128).transpose(2, 1, 0).reshape(128, 40))
        kcT = np.ascontiguousarray(cache_k[4 * i:4 * i + 4].reshape(4, PASTN, H, 128).transpose(0, 2, 3, 1))
        m = dict(shared)
        m.update(xT=xT, cT=cT, st_in=np.ascontiguousarray(state_ret[0, 4 * i:4 * i + 4]), kcT=kcT,
                 vc=np.ascontiguousarray(cache_v[4 * i:4 * i + 4].reshape(4, 32, 128, H, 128)
                                         .transpose(0, 3, 2, 1, 4).reshape(4, H, 128, 32 * 128)))
        in_maps.append(m)
    return in_maps


def assemble(R):
    n = 8
    y_p = np.stack([R[i]["yT"][:, :2048].T for i in range(n)])
    y_s = np.concatenate([R[i]["yT"][:, 2048:].T.reshape(4, 32, D) for i in range(n)])
    st_p = np.stack([R[i]["st_p"] for i in range(n)])[None]
    k_p = np.stack([R[i]["kT_out"][:, :2048].T.reshape(2048, 16, 64) for i in range(n)])
    v_p = np.stack([R[i]["v_out"][:2048].reshape(2048, 8, 128) for i in range(n)])
    st_s = np.concatenate([R[i]["st_s"] for i in range(n)])[None]
    k_s = np.concatenate([R[i]["kT_out"][:, 2048:].T.reshape(4, 32, 16, 64) for i in range(n)])
    v_s = np.concatenate([R[i]["v_out"][2048:].reshape(4, 32, 8, 128) for i in range(n)])
    out = (y_p, y_s, st_p, k_p, v_p, st_s, k_s, v_s)
    return tuple(np.ascontiguousarray(o, dtype=np.float32) for o in out)
```
